# Optimizing a Trainium2 kernel written in Bass

```python
import jax, jax.numpy as jnp
from jax import lax
import numpy as np

D_MODEL = 1024
BATCH = 8
SEQ = 2048
DEPTH = 4
DEC_BATCH = 128
DEC_SEQ = 8
PAST_LEN = 2048
PAGE_SIZE = 128

N_META = 16
EPS = 1e-6
MASK_NEG = -1e30
A_WIDTH = D_MODEL // 2
A_HEAD = 128
A_HEADS = A_WIDTH // A_HEAD
A_CHUNK = 64
B_WIDTH = D_MODEL // 2
CONV_W = 31
C_HEADS = 8
C_HEAD_DIM = 64
C_WIDTH = C_HEADS * C_HEAD_DIM
C_KV_HEADS = 2
C_KV_WIDTH = C_KV_HEADS * C_HEAD_DIM
IDX_HEADS = 4
IDX_DIM = 64
TOPK_MAX = 256
Q_BLOCK = 128
ROPE_THETA = 500000.0
ROT_DIM = C_HEAD_DIM // 4
IDX_SCALE = (IDX_HEADS * IDX_DIM) ** -0.5
IN_SIZES = (A_WIDTH, A_WIDTH, A_WIDTH, A_WIDTH, 2 * B_WIDTH, B_WIDTH, C_WIDTH, C_KV_WIDTH, C_KV_WIDTH, IDX_HEADS * IDX_DIM, IDX_DIM, IDX_HEADS, C_WIDTH, 3 * D_MODEL)
N_IN = sum(IN_SIZES)
F32 = jnp.float32

kernel_name = 'hybrid_hgrn2_conformer_dsa_step'


def rmsnorm(x, g):
    xf = x.astype(F32)
    y = xf * lax.rsqrt(jnp.mean(xf * xf, axis=-1, keepdims=True) + EPS)
    return (y * g.astype(F32)).astype(x.dtype)


def layernorm(x, g, b):
    xf = x.astype(F32)
    mu = jnp.mean(xf, axis=-1, keepdims=True)
    d = xf - mu
    var = jnp.mean(d * d, axis=-1, keepdims=True)
    return (d * lax.rsqrt(var + EPS) * g.astype(F32) + b.astype(F32)).astype(x.dtype)


def rope(x, pos):
    half = ROT_DIM // 2
    inv = ROPE_THETA ** (-jnp.arange(half, dtype=F32) * 2.0 / ROT_DIM)
    ang = pos.astype(F32)[:, None] * inv
    cos = jnp.cos(ang)[:, None, :]
    sin = jnp.sin(ang)[:, None, :]
    xf = x.astype(F32)
    x1 = xf[..., :half]
    x2 = xf[..., half:ROT_DIM]
    out = jnp.concatenate([x1 * cos - x2 * sin, x2 * cos + x1 * sin, xf[..., ROT_DIM:]], axis=-1)
    return out.astype(x.dtype)


def project(h, norm_g, w_in, b_in):
    u = rmsnorm(h, norm_g) @ w_in + b_in
    parts, start = [], 0
    for n in IN_SIZES:
        parts.append(u[..., start:start + n])
        start += n
    return parts


def hgrn_inputs(aq, af, ai, lb):
    shp = aq.shape[:-1] + (A_HEADS, A_HEAD)
    f = af.astype(F32)
    lb = lb.astype(F32)
    logf = jnp.log(lb + (1.0 - lb) * jax.nn.sigmoid(f))
    k = (1.0 - lb) * jax.nn.sigmoid(-f)
    q = jax.nn.silu(aq.astype(F32))
    return q.reshape(shp), k.reshape(shp), ai.astype(F32).reshape(shp), logf.reshape(shp)


def gla_chunk(S, q, k, v, logf):
    c = q.shape[1]
    b = jnp.cumsum(logf, axis=1)
    o_inter = jnp.einsum('bthk,bhkv->bthv', q * jnp.exp(b), S)
    causal = jnp.tril(jnp.ones((c, c), dtype=bool))[None, :, :, None, None]
    diff = b[:, :, None] - b[:, None, :]
    dec = jnp.exp(jnp.where(causal, diff, MASK_NEG))
    att = jnp.einsum('bthk,btshk,bshk->bths', q, dec, k)
    o_intra = jnp.einsum('bths,bshv->bthv', att, v)
    blast = b[:, -1]
    k_dec = k * jnp.exp(blast[:, None] - b)
    S_new = jnp.exp(blast)[..., None] * S + jnp.einsum('bchk,bchv->bhkv', k_dec, v)
    return S_new, o_inter + o_intra


def hgrn_prompt(q, k, v, logf):
    nb = q.shape[0]
    S0 = jnp.zeros((nb, A_HEADS, A_HEAD, A_HEAD), F32)
    S1, o_meta = gla_chunk(S0, q[:, :N_META], k[:, :N_META], v[:, :N_META], logf[:, :N_META])

    def to_chunks(a):
        a = a[:, N_META:]
        n = a.shape[1] // A_CHUNK
        return a.reshape((nb, n, A_CHUNK) + a.shape[2:]).swapaxes(0, 1)

    def step(S, xs):
        return gla_chunk(S, *xs)

    S_fin, o = lax.scan(step, S1, (to_chunks(q), to_chunks(k), to_chunks(v), to_chunks(logf)))
    o = o.swapaxes(0, 1).reshape(nb, -1, A_HEADS, A_HEAD)
    return jnp.concatenate([o_meta, o], axis=1), S_fin


def conformer_conv(glu_in, z, buf, conv_w, conv_b, ln_g, ln_b, conv_pw, w_pb):
    a, g = jnp.split(glu_in, 2, axis=-1)
    u = a * jax.nn.sigmoid(g)
    xp = jnp.concatenate([buf.astype(u.dtype), u], axis=1)
    y = lax.conv_general_dilated(xp, conv_w[:, None, :].astype(xp.dtype), window_strides=(1,), padding='VALID', dimension_numbers=('NWC', 'WIO', 'NWC'), feature_group_count=B_WIDTH) + conv_b
    y = jax.nn.silu(layernorm(y, ln_g, ln_b))
    out = ((y @ conv_pw) * jax.nn.silu(z)) @ w_pb
    return out, xp[:, xp.shape[1] - (CONV_W - 1):]


def index_scores(qi, wi, ki):
    s = jnp.einsum('bqhd,bsd->bqhs', qi.astype(F32), ki.astype(F32))
    return jnp.einsum('bqhs,bqh->bqs', jax.nn.relu(s), wi.astype(F32) * IDX_SCALE)


def select_keys(scores, allowed, n_sel):
    scores = jnp.where(allowed, scores, MASK_NEG)
    val, idx = lax.top_k(scores, n_sel)
    return idx, val > 0.5 * MASK_NEG


def gather_rows(x, idx):
    return jax.vmap(lambda xb, ib: xb[ib])(x, idx)


def sparse_attend(q, kg, vg, valid):
    nb, nq = q.shape[:2]
    qg = q.reshape(nb, nq, C_KV_HEADS, C_HEADS // C_KV_HEADS, C_HEAD_DIM).astype(F32)
    logits = jnp.einsum('bqhgd,bqnhd->bqhgn', qg, kg.astype(F32)) * (C_HEAD_DIM ** -0.5)
    logits = jnp.where(valid[:, :, None, None, :], logits, MASK_NEG)
    p = jax.nn.softmax(logits, axis=-1)
    o = jnp.einsum('bqhgn,bqnhd->bqhgd', p, vg.astype(F32))
    return o.reshape(nb, nq, C_WIDTH).astype(q.dtype)


def dsa_prompt(q, k, v, qi, ki, wi):
    nb, nt = q.shape[:2]
    n_sel = min(TOPK_MAX, nt // 4)
    n_blk = -(-nt // Q_BLOCK)
    tp = n_blk * Q_BLOCK

    def pad(a):
        return jnp.pad(a, [(0, 0), (0, tp - nt)] + [(0, 0)] * (a.ndim - 2))

    qP, qiP, wiP = pad(q), pad(qi), pad(wi)
    key_pos = jnp.arange(nt)

    def block(i):
        start = i * Q_BLOCK
        qb = lax.dynamic_slice_in_dim(qP, start, Q_BLOCK, axis=1)
        qib = lax.dynamic_slice_in_dim(qiP, start, Q_BLOCK, axis=1)
        wib = lax.dynamic_slice_in_dim(wiP, start, Q_BLOCK, axis=1)
        qpos = start + jnp.arange(Q_BLOCK)
        sc = index_scores(qib, wib, ki)
        idx, valid = select_keys(sc, key_pos[None, :] <= qpos[:, None], n_sel)
        return sparse_attend(qb, gather_rows(k, idx), gather_rows(v, idx), valid)

    out = lax.map(block, jnp.arange(n_blk))
    return out.transpose(1, 0, 2, 3).reshape(nb, tp, C_WIDTH)[:, :nt]


def dsa_sample(q, k_new, v_new, qi, ki_new, wi, cache_k, cache_v, cache_kidx, page_table):
    nb, ns = q.shape[:2]
    past = page_table.shape[1] * PAGE_SIZE
    n_keys = past + ns
    n_sel = min(TOPK_MAX, n_keys // 4)
    ki_past = cache_kidx[page_table].reshape(nb, past, IDX_DIM)
    ki_all = jnp.concatenate([ki_past.astype(ki_new.dtype), ki_new], axis=1)
    sc = index_scores(qi, wi, ki_all)
    allowed = jnp.arange(n_keys)[None, :] <= (past + jnp.arange(ns))[:, None]
    idx, valid = select_keys(sc, allowed, n_sel)
    is_past = idx < past
    pidx = jnp.minimum(idx, past - 1)
    phys = jax.vmap(lambda pt, ii: pt[ii])(page_table, pidx // PAGE_SIZE) * PAGE_SIZE + pidx % PAGE_SIZE
    nidx = jnp.clip(idx - past, 0, ns - 1)

    def fetch(pool, new):
        flat = pool.reshape((-1,) + pool.shape[2:])
        return jnp.where(is_past[..., None, None], flat[phys].astype(new.dtype), gather_rows(new, nidx))

    return sparse_attend(q, fetch(cache_k, k_new), fetch(cache_v, v_new), valid)


def layer_forward(h, pos, lw, past):
    (norm_g, w_in, b_in, lb, gn_g, conv_w, conv_b, ln_g, ln_b, conv_pw, w_pa, w_pb, w_pc, w_out) = lw
    (aq, af, ai, az, b_glu, bz, cq, ck, cv, cqi, cki, cw, cz, gate) = project(h, norm_g, w_in, b_in)
    nb, nt = h.shape[:2]
    qa, ka, va, logf = hgrn_inputs(aq, af, ai, lb)
    if past is None:
        oa, s_new = hgrn_prompt(qa, ka, va, logf)
        buf = jnp.zeros((nb, CONV_W - 1, B_WIDTH), h.dtype)
    else:
        cache_k, cache_v, cache_kidx, s_old, buf, page_table = past
        s_new, oa = gla_chunk(s_old.astype(F32), qa, ka, va, logf)
        s_new = s_new.astype(s_old.dtype)
    ya = (rmsnorm(oa, gn_g).reshape(nb, nt, A_WIDTH).astype(h.dtype) * jax.nn.silu(az)) @ w_pa
    yb, buf_new = conformer_conv(b_glu, bz, buf, conv_w, conv_b, ln_g, ln_b, conv_pw, w_pb)
    qc = rope(cq.reshape(nb, nt, C_HEADS, C_HEAD_DIM), pos)
    kc = rope(ck.reshape(nb, nt, C_KV_HEADS, C_HEAD_DIM), pos)
    vc = cv.reshape(nb, nt, C_KV_HEADS, C_HEAD_DIM)
    qi = rope(cqi.reshape(nb, nt, IDX_HEADS, IDX_DIM), pos)
    ki = rope(cki[:, :, None, :], pos)[:, :, 0]
    if past is None:
        oc = dsa_prompt(qc, kc, vc, qi, ki, cw)
    else:
        oc = dsa_sample(qc, kc, vc, qi, ki, cw, cache_k, cache_v, cache_kidx, page_table)
    yc = (oc * jax.nn.silu(cz)) @ w_pc
    ga, gb, gc = jnp.split(jax.nn.sigmoid(gate), 3, axis=-1)
    h = h + (ga * ya + gb * yb + gc * yc) @ w_out
    return h, (kc, vc, ki, s_new, buf_new)


def setup_inputs(seed: int = 0) -> dict:
    key = jax.random.key(seed)
    ks = jax.random.split(key, 32)
    n_pages = PAST_LEN // PAGE_SIZE
    used = DEC_BATCH * n_pages
    n_phys = used + max(1, used // 4)
    nrm = jax.random.normal
    page_table = jax.random.permutation(ks[0], n_phys)[:used].reshape(DEC_BATCH, n_pages).astype(jnp.int32)
    return {
        'x_prompt': nrm(ks[1], (BATCH, SEQ, D_MODEL), F32),
        'x_sample': nrm(ks[2], (DEC_BATCH, DEC_SEQ, D_MODEL), F32),
        'cache_k': nrm(ks[3], (DEPTH, n_phys, PAGE_SIZE, C_KV_HEADS, C_HEAD_DIM), F32),
        'cache_v': nrm(ks[4], (DEPTH, n_phys, PAGE_SIZE, C_KV_HEADS, C_HEAD_DIM), F32),
        'cache_kidx': nrm(ks[5], (DEPTH, n_phys, PAGE_SIZE, IDX_DIM), F32),
        'state_hgrn': 0.5 * nrm(ks[6], (DEPTH, DEC_BATCH, A_HEADS, A_HEAD, A_HEAD), F32),
        'state_conv': 0.5 * nrm(ks[7], (DEPTH, DEC_BATCH, CONV_W - 1, B_WIDTH), F32),
        'page_table': page_table,
        'meta_tokens': nrm(ks[8], (N_META, D_MODEL), F32),
        'norm_g': 1.0 + 0.05 * nrm(ks[9], (DEPTH, D_MODEL), F32),
        'w_in': nrm(ks[10], (DEPTH, D_MODEL, N_IN), F32) * D_MODEL ** -0.5,
        'b_in': 0.01 * nrm(ks[11], (DEPTH, N_IN), F32),
        'lb_logits': 0.5 * nrm(ks[12], (DEPTH, A_WIDTH), F32),
        'hgrn_norm_g': 1.0 + 0.05 * nrm(ks[13], (DEPTH, A_HEAD), F32),
        'conv_w': nrm(ks[14], (DEPTH, CONV_W, B_WIDTH), F32) * CONV_W ** -0.5,
        'conv_b': 0.01 * nrm(ks[15], (DEPTH, B_WIDTH), F32),
        'conv_ln_g': 1.0 + 0.05 * nrm(ks[16], (DEPTH, B_WIDTH), F32),
        'conv_ln_b': 0.01 * nrm(ks[17], (DEPTH, B_WIDTH), F32),
        'conv_pw': nrm(ks[18], (DEPTH, B_WIDTH, B_WIDTH), F32) * B_WIDTH ** -0.5,
        'w_pa': nrm(ks[19], (DEPTH, A_WIDTH, D_MODEL), F32) * A_WIDTH ** -0.5,
        'w_pb': nrm(ks[20], (DEPTH, B_WIDTH, D_MODEL), F32) * B_WIDTH ** -0.5,
        'w_pc': nrm(ks[21], (DEPTH, C_WIDTH, D_MODEL), F32) * C_WIDTH ** -0.5,
        'w_out': nrm(ks[22], (DEPTH, D_MODEL, D_MODEL), F32) * D_MODEL ** -0.5,
        'final_norm_g': 1.0 + 0.05 * nrm(ks[23], (D_MODEL,), F32),
    }


def reference(x_prompt, x_sample, cache_k, cache_v, cache_kidx, state_hgrn, state_conv, page_table, meta_tokens, norm_g, w_in, b_in, lb_logits, hgrn_norm_g, conv_w, conv_b, conv_ln_g, conv_ln_b, conv_pw, w_pa, w_pb, w_pc, w_out, final_norm_g):
    sm = jax.nn.softmax(lb_logits.astype(F32), axis=0)
    lb_all = jnp.cumsum(sm, axis=0) - sm[0]
    nbp = x_prompt.shape[0]
    meta = jnp.broadcast_to(meta_tokens[None].astype(x_prompt.dtype), (nbp, N_META, x_prompt.shape[2]))
    hp = jnp.concatenate([meta, x_prompt], axis=1)
    pos_p = jnp.arange(hp.shape[1])
    past = page_table.shape[1] * PAGE_SIZE
    pos_s = past + jnp.arange(x_sample.shape[1])
    hs = x_sample
    pk, pv, pki, ps, pc = [], [], [], [], []
    sk, sv, ski, ss, sc = [], [], [], [], []
    for l in range(DEPTH):
        lw = (norm_g[l], w_in[l], b_in[l], lb_all[l], hgrn_norm_g[l], conv_w[l], conv_b[l], conv_ln_g[l], conv_ln_b[l], conv_pw[l], w_pa[l], w_pb[l], w_pc[l], w_out[l])
        hp, (k1, v1, ki1, s1, c1) = layer_forward(hp, pos_p, lw, None)
        hs, (k2, v2, ki2, s2, c2) = layer_forward(hs, pos_s, lw, (cache_k[l], cache_v[l], cache_kidx[l], state_hgrn[l], state_conv[l], page_table))
        pk.append(k1); pv.append(v1); pki.append(ki1); ps.append(s1); pc.append(c1)
        sk.append(k2); sv.append(v2); ski.append(ki2); ss.append(s2); sc.append(c2)
    y_prompt = rmsnorm(hp, final_norm_g)[:, N_META:]
    y_sample = rmsnorm(hs, final_norm_g)
    return (y_prompt, y_sample, jnp.stack(pk), jnp.stack(pv), jnp.stack(pki), jnp.stack(ps), jnp.stack(pc), jnp.stack(sk), jnp.stack(sv), jnp.stack(ski), jnp.stack(ss), jnp.stack(sc))
```

```python
import numpy as np
from contextlib import ExitStack, contextmanager
import concourse.bass as bass
import concourse.mybir as mybir
from concourse.bass_utils import run_bass_kernel_spmd

F32 = mybir.dt.float32
BF16 = mybir.dt.bfloat16
I32 = mybir.dt.int32
ALU = mybir.AluOpType
AF = mybir.ActivationFunctionType
AX = mybir.AxisListType

D = 1024
DEPTH = 4
TP = 2064
NSM = 128
NT = TP + NSM
TB = [(0, 448), (448, 448), (896, 448), (1344, 448), (1792, 400)]
N_IN = 8260
EPS = 1e-6
N_PHYS = 2560
ROPE_THETA = 500000.0
SEM_LIMIT = 30000

O_AQ, O_AF, O_AI, O_AZ = 0, 512, 1024, 1536
O_BA, O_BG, O_BZ = 2048, 2560, 3072
O_CQ, O_CK, O_CV, O_CQI, O_CKI, O_CW, O_CZ, O_GATE = 3584, 4096, 4224, 4352, 4608, 4672, 4676, 5188

FM_UNITS = {}
for h in range(4):
    FM_UNITS[f"aq{h}"] = [(O_AQ + 128 * h, 128, 0)]
    FM_UNITS[f"af{h}"] = [(O_AF + 128 * h, 128, 0)]
    FM_UNITS[f"az{h}"] = [(O_AZ + 128 * h, 128, 0)]
    FM_UNITS[f"ba{h}"] = [(O_BA + 128 * h, 128, 0)]
    FM_UNITS[f"bg{h}"] = [(O_BG + 128 * h, 128, 0)]
    FM_UNITS[f"bz{h}"] = [(O_BZ + 128 * h, 128, 0)]
    FM_UNITS[f"cq{h}"] = [(O_CQ + 64 * h, 64, 0), (O_CQ + 64 * (4 + h), 64, 64)]
    FM_UNITS[f"cz{h}"] = [(O_CZ + 128 * h, 128, 0)]
FM_UNITS["ck"] = [(O_CK, 128, 0)]
FM_UNITS["cqi0"] = [(O_CQI, 128, 0)]
FM_UNITS["cqi1"] = [(O_CQI + 128, 128, 0)]
FM_UNITS["cki"] = [(O_CKI, 64, 0), (O_CKI, 64, 64)]
for h in range(4):
    FM_UNITS[f"cqis{h}"] = [(O_CQI + 64 * h, 64, 0)]
for b in range(3):
    for j in range(8):
        FM_UNITS[f"g{b}_{j}"] = [(O_GATE + 1024 * b + 128 * j, 128, 0)]
FM_NAMES = list(FM_UNITS.keys())
FM_IDX = {n: i for i, n in enumerate(FM_NAMES)}


class Buf:
    __slots__ = ("name", "wr", "rd", "dsem", "dcnt")

    def __init__(self, name):
        self.name = name
        self.wr = []
        self.rd = {}
        self.dsem = None
        self.dcnt = 0


class KB:
    def __init__(self, nc, es):
        self.nc = nc
        self.es = es
        self.engs = {"pe": nc.tensor, "act": nc.scalar, "dve": nc.vector, "pool": nc.gpsimd, "sp": nc.sync}
        self.sem = {}
        self.cnt = {}
        self.seen = {k: {} for k in self.engs}
        self.nsem = 0
        self.dma_bufs = []
        self.n_ins = 0
        self.old_ev = {}
        self.free_sems = []
        for k in ("pe", "act", "dve", "pool"):
            self._rot(k)

    def new_sem(self, name):
        self.nsem += 1
        return self.es.enter_context(self.nc.semaphore(f"{name}_{self.nsem}"))

    def _rot(self, k):
        if k in self.sem:
            self.old_ev[k] = (self.sem[k], self.cnt[k])
        self.sem[k] = self.new_sem("e" + k)
        self.cnt[k] = 0

    def _wait(self, ek, ev):
        sem, val = ev
        key = id(sem)
        if ek == "pe" and sem is self.sem["pe"]:
            return
        if self.seen[ek].get(key, 0) >= val:
            return
        self.engs[ek].wait_ge(sem, val)
        self.seen[ek][key] = val
        self.n_ins += 1

    def _deps(self, ek, reads, writes):
        for b in reads:
            for ev in b.wr:
                self._wait(ek, ev)
        for b in writes:
            for ev in b.wr:
                self._wait(ek, ev)
            for ev in b.rd.values():
                self._wait(ek, ev)

    def _record(self, ev, reads, writes, accum=False):
        for b in writes:
            if accum:
                b.wr = [e for e in b.wr if e[0] is not ev[0]] + [ev]
            else:
                b.wr = [ev]
                b.rd = {}
        for b in reads:
            if b in writes:
                continue
            key = id(ev[0])
            old = b.rd.get(key)
            if old is None or old[1] < ev[1]:
                b.rd[key] = ev

    def op(self, ek, fn, reads=(), writes=(), accum=False):
        self._deps(ek, reads, writes)
        ins = fn(self.engs[ek])
        self.cnt[ek] += 1
        ins.then_inc(self.sem[ek], 1)
        ev = (self.sem[ek], self.cnt[ek])
        self._record(ev, reads, writes, accum=accum)
        self.n_ins += 1
        if self.cnt[ek] >= SEM_LIMIT:
            self._rot(ek)

    def dma(self, qk, out_ap, in_ap, reads, writes, sb, group=False, indirect=None, accum=False):
        if sb.dsem is None:
            self.pin(sb)
        self._deps(qk, reads, writes)
        if not group and sb.dcnt > 0:
            self._wait(qk, (sb.dsem, sb.dcnt))
        if indirect is not None:
            ins = self.engs[qk].indirect_dma_start(out=out_ap, out_offset=None, in_=in_ap, in_offset=indirect)
        else:
            ins = self.engs[qk].dma_start(out=out_ap, in_=in_ap)
        ins.then_inc(sb.dsem, 16)
        sb.dcnt += 16
        ev = (sb.dsem, sb.dcnt)
        self._record(ev, reads, writes, accum=accum)
        self.n_ins += 1

    def pin(self, b):
        if b.dsem is None:
            if self.free_sems:
                b.dsem, b.dcnt = self.free_sems.pop()
            else:
                b.dsem, b.dcnt = self.new_sem("d"), 0
            self.dma_bufs.append(b)

    @contextmanager
    def phase(self):
        start = len(self.dma_bufs)
        with ExitStack() as ph:
            yield ph
            self.barrier()
            for b in self.dma_bufs[start:]:
                self.free_sems.append((b.dsem, b.dcnt))
                b.dsem, b.dcnt = None, 0
            del self.dma_bufs[start:]

    def barrier(self):
        evs = [(self.sem[k], self.cnt[k]) if self.cnt[k] > 0 else self.old_ev.get(k) for k in ("pe", "act", "dve", "pool")]
        evs = [e for e in evs if e is not None]
        evs += [(b.dsem, b.dcnt) for b in self.dma_bufs if b.dcnt > 0]
        for ek in self.engs:
            for ev in evs:
                if ek in self.sem and ev[0] is self.sem[ek]:
                    continue
                self._wait(ek, ev)

    def finish(self):
        for b in self.dma_bufs:
            if b.dcnt > 0:
                self._wait("sp", (b.dsem, b.dcnt))


NIT = 18
NEG = -1.0e30
IDX_SCALE = (4 * 64) ** -0.5


def build_program(n_layers=DEPTH, debug=None):
    nc = bass.Bass("TRN2", target_bir_lowering=False)

    def din(name, shape, dt=F32):
        return nc.dram_tensor(name, list(shape), dt, kind="ExternalInput").ap()

    def dout(name, shape, dt=F32):
        return nc.dram_tensor(name, list(shape), dt, kind="ExternalOutput").ap()

    xp = din("xp", [2048, D]); xs = din("xs", [NSM, D]); meta = din("meta", [16, D])
    w_in = din("w_in", [DEPTH, D, N_IN])
    conv_pw = din("conv_pw", [DEPTH, 512, 512])
    w_pa = din("w_pa", [DEPTH, 512, D]); w_pb = din("w_pb", [DEPTH, 512, D]); w_pc = din("w_pc", [DEPTH, 512, D])
    w_out = din("w_out", [DEPTH, D, D])
    cache_k = din("cache_k", [DEPTH, N_PHYS * 128, 128]); cache_v = din("cache_v", [DEPTH, N_PHYS * 128, 128])
    cache_ki = din("cache_ki", [DEPTH, N_PHYS * 128, 64])
    st_h = din("st_h", [DEPTH, 16, 4, 128, 128]); st_c = din("st_c", [DEPTH, 16, 30, 512])
    ptab = din("ptab", [1, 256], I32)
    gng_d = din("gng", [DEPTH, 128])
    c_ident = din("c_ident", [128, 128]); c_rmat = din("c_rmat", [128, 128])
    c_cos = din("c_cos", [128, NT]); c_sin = din("c_sin", [128, NT])
    c_normg = din("c_normg", [128, DEPTH * 8]); c_fng = din("c_fng", [128, 8])
    c_bias = din("c_bias", [128, DEPTH * len(FM_NAMES)])
    c_btok = din("c_btok", [1, DEPTH * N_IN])
    c_masks = din("c_masks", [128, 4 * 128])
    c_seqsel = din("c_seqsel", [128, 16 * 128]); c_rowsel = din("c_rowsel", [128, 16])
    c_convw = din("c_convw", [128, DEPTH * 4 * 31]); c_cvec = din("c_cvec", [128, DEPTH * 12])
    c_lbl = din("c_lbl", [128, 16])

    y_p = dout("y_p", [2048, D]); y_s = dout("y_s", [NSM, D])
    p_k = dout("p_k", [DEPTH, TP, 128]); p_v = dout("p_v", [DEPTH, TP, 128]); p_ki = dout("p_ki", [DEPTH, TP, 64])
    p_h = dout("p_h", [DEPTH, 4, 128, 128]); p_c = dout("p_c", [DEPTH, 30, 512])
    s_k = dout("s_k", [DEPTH, NSM, 128]); s_v = dout("s_v", [DEPTH, NSM, 128]); s_ki = dout("s_ki", [DEPTH, NSM, 64])
    s_h = dout("s_h", [DEPTH, 16, 4, 128, 128]); s_c = dout("s_c", [DEPTH, 16, 30, 512])
    dbg = dout("dbg", [128, 8 * NT]) if debug else None

    hT_d = nc.dram_tensor("hT_d", [8, 128, NT], F32, kind="Internal").ap()
    HTD = [[Buf(f"htd{c}_{b}") for b in range(len(TB))] for c in range(8)]

    def blocks_of(c0, n):
        return [bi for bi, (t0, nn) in enumerate(TB) if t0 < c0 + n and c0 < t0 + nn]

    with ExitStack() as es:
        kb = KB(nc, es)

        sb_n = [0]

        def sb(name, shape, dt=F32, stack=None):
            sb_n[0] += 1
            return (stack or es).enter_context(nc.sbuf_tensor(f"{name}_{sb_n[0]}", list(shape), dt))

        def mm(out, lhsT, rhs, start, stop, reads, writes, **kw):
            kb.op("pe", lambda e: e.matmul(out, lhsT=lhsT, rhs=rhs, start=start, stop=stop, **kw), reads, writes)

        def tr(out, in_, idn, reads, writes):
            kb.op("pe", lambda e: e.transpose(out, in_, idn), reads, writes)

        def act(out, in_, func, reads, writes, **kw):
            kb.op("act", lambda e: e.activation(out=out, in_=in_, func=func, **kw), reads, writes)

        def tt(out, a, b, op, reads, writes, eng="dve"):
            kb.op(eng, lambda e: e.tensor_tensor(out=out, in0=a, in1=b, op=op), reads, writes)

        def ts(out, a, s1, s2, op0, op1, reads, writes, eng="dve", **kw):
            if s2 is None:
                kb.op(eng, lambda e: e.tensor_scalar(out=out, in0=a, scalar1=s1, scalar2=None, op0=op0, **kw), reads, writes)
            else:
                kb.op(eng, lambda e: e.tensor_scalar(out=out, in0=a, scalar1=s1, scalar2=s2, op0=op0, op1=op1, **kw), reads, writes)

        def stt(out, a, scalar, b, op0, op1, reads, writes):
            kb.op("dve", lambda e: e.scalar_tensor_tensor(out=out, in0=a, scalar=scalar, in1=b, op0=op0, op1=op1), reads, writes)

        def cp(out, in_, reads, writes, eng="dve"):
            kb.op(eng, lambda e: e.tensor_copy(out=out, in_=in_), reads, writes)

        def ms(ap, val, writes, eng="dve"):
            kb.op(eng, lambda e: e.memset(ap, val), [], writes)

        def const(name, shape, src, dt=F32):
            t = sb(name, shape, dt); B_ = Buf(name)
            kb.dma("sp", t[:], src, [], [B_], B_)
            return t, B_

        ident, IDENT = const("ident", [128, 128], c_ident)
        rmat, RMAT = const("rmat", [128, 128], c_rmat)
        normg, NORMG = const("normg", [128, DEPTH * 8], c_normg)
        fng, FNG = const("fng", [128, 8], c_fng)
        biasfm, BIASFM = const("biasfm", [128, DEPTH * len(FM_NAMES)], c_bias)
        masks, MASKS = const("masks", [128, 512], c_masks)
        rowsel, ROWSEL = const("rowsel", [128, 16], c_rowsel)
        convw, CONVW = const("convw", [128, DEPTH * 124], c_convw)
        cvec, CVEC = const("cvec", [128, DEPTH * 12], c_cvec)
        lbl, LBL = const("lbl", [128, 16], c_lbl)
        tri_st = masks[:, 0:128]; blk_st = masks[:, 128:256]; tribias = masks[:, 256:384]; blkbias = masks[:, 384:512]
        ones_f = sb("ones_f", [128, 128]); ONESF = Buf("ones_f")
        ones_b = sb("ones_b", [1, 128], BF16); ONESB = Buf("ones_b")
        identb = sb("identb", [128, 128], BF16); IDENTB = Buf("identb")
        half = sb("half", [128, 1]); HALF = Buf("half")
        neg29 = sb("neg29", [128, 1]); NEG29 = Buf("neg29")
        seqsel = sb("seqsel", [128, 16, 128], BF16); SEQSEL = Buf("seqsel")
        ms(ones_f[:], 1.0, [ONESF]); ms(ones_b[:], 1.0, [ONESB]); ms(half[:], 0.5, [HALF]); ms(neg29[:], -1.0e29, [NEG29])
        cp(identb[:], ident[:], [IDENT], [IDENTB])
        pow2 = sb("pow2", [128, NIT]); POW2 = Buf("pow2")
        for k_ in range(NIT):
            ms(pow2[:, k_:k_ + 1], 2.0 ** -(k_ + 1), [POW2])

        lbe = sb("lbe", [128, 4, 4]); LBE = Buf("lbe")
        lbm = sb("lbm", [128, 4]); LBM = Buf("lbm")
        lbv = sb("lbv", [128, 4, 4]); LBV = Buf("lbv")
        omlv = sb("omlv", [128, 4, 4]); OMLV = Buf("omlv")
        nomlv = sb("nomlv", [128, 4, 4]); NOMLV = Buf("nomlv")
        lb3 = lbl[:, :].rearrange("p (h l) -> p h l", l=4)
        kb.op("dve", lambda e: e.tensor_reduce(out=lbm[:, :], in_=lb3, axis=AX.X, op=ALU.max), [LBL], [LBM])
        tt(lbe[:], lb3, lbm[:, :].unsqueeze(2).to_broadcast([128, 4, 4]), ALU.subtract, [LBL, LBM], [LBE])
        act(lbe[:], lbe[:], AF.Exp, [LBE], [LBE])
        kb.op("dve", lambda e: e.tensor_reduce(out=lbm[:, :], in_=lbe[:], axis=AX.X, op=ALU.add), [LBE], [LBM])
        kb.op("dve", lambda e: e.reciprocal(out=lbm[:, :], in_=lbm[:, :]), [LBM], [LBM])
        tt(lbe[:], lbe[:], lbm[:, :].unsqueeze(2).to_broadcast([128, 4, 4]), ALU.mult, [LBE, LBM], [LBE])
        ms(lbv[:, :, 0:1], 0.0, [LBV])
        cp(lbv[:, :, 1:2], lbe[:, :, 1:2], [LBE], [LBV])
        tt(lbv[:, :, 2:3], lbv[:, :, 1:2], lbe[:, :, 2:3], ALU.add, [LBV, LBE], [LBV])
        tt(lbv[:, :, 3:4], lbv[:, :, 2:3], lbe[:, :, 3:4], ALU.add, [LBV, LBE], [LBV])
        ts(omlv[:], lbv[:], -1.0, 1.0, ALU.mult, ALU.add, [LBV], [OMLV])
        ts(nomlv[:], omlv[:], -1.0, None, ALU.mult, None, [OMLV], [NOMLV])
        with ExitStack() as ph0:
            ssf = sb("ssf", [128, 16, 128], stack=ph0); SSF = Buf("ssf")
            kb.dma("sp", ssf[:], c_seqsel.rearrange("p (s t) -> p s t", t=128), [], [SSF], SSF)
            cp(seqsel[:], ssf[:], [SSF], [SEQSEL])
            kb.barrier()

        ptb = sb("ptb", [128, 256], I32); PTB = Buf("ptb")
        iop = sb("iop", [128, 1], I32); IOP = Buf("iop")
        pidx = sb("pidx", [128, 256], I32); PIDX = Buf("pidx")
        kb.dma("sp", ptb[:], ptab.partition_broadcast(128), [], [PTB], PTB)
        kb.op("pool", lambda e: e.iota(iop[:], pattern=[[0, 1]], base=0, channel_multiplier=1), [], [IOP])
        ts(pidx[:], ptb[:], 128, iop[:, 0:1], ALU.mult, ALU.add, [PTB, IOP], [PIDX])

        hnT = sb("hnT", [128, 8, NT], BF16)
        HN = [Buf(f"hn{b}") for b in range(len(TB))]
        merged = sb("merged", [128, 8, NT], BF16)
        MG = [Buf(f"mg{b}") for b in range(len(TB))]

        psum = [es.enter_context(nc.psum_tensor(f"ps{i}", [128, 512], F32)) for i in range(8)]
        PS = [Buf(f"ps{i}") for i in range(8)]
        ps_rr = [0]

        def next_ps():
            i = ps_rr[0] % 6
            ps_rr[0] += 1
            return psum[i], PS[i]

        wst = [sb(f"wst{i}", [128, 8, 128]) for i in range(3)]
        WST = [Buf(f"wst{i}") for i in range(3)]
        for b_ in WST:
            kb.pin(b_)
        wbf = [sb(f"wbf{i}", [128, 8, 128], BF16) for i in range(3)]
        WBF = [Buf(f"wbf{i}") for i in range(3)]
        w_rr = [0]

        def load_w(w2d, kc, pieces):
            i = w_rr[0] % 3
            w_rr[0] += 1
            wv = w2d.rearrange("(c p) n -> p c n", p=128)
            first = True
            for (col, width, dst) in pieces:
                kb.dma("sp", wst[i][:, 0:kc, dst:dst + width], wv[:, :, col:col + width], [], [WST[i]], WST[i],
                       group=not first)
                first = False
            tot = max(d + w for (_, w, d) in pieces)
            cp(wbf[i][:, 0:kc, 0:tot], wst[i][:, 0:kc, 0:tot], [WST[i]], [WBF[i]], eng="pool")
            return wbf[i], WBF[i]

        def fm_unit(l, name, evac, blocks=None, pieces=None, rows=128):
            wt, WT = load_w(w_in[l], 8, pieces or FM_UNITS[name])
            for bi, (t0, n) in enumerate(TB):
                if blocks is not None and bi not in blocks:
                    continue
                pt, PT = next_ps()
                for dc in range(8):
                    mm(pt[0:rows, 0:n], wt[:, dc, 0:rows], hnT[:, dc, t0:t0 + n], dc == 0, dc == 7, [WT, HN[bi]], [PT])
                evac(pt, PT, bi, t0, n)

        def bias_col(l, name):
            j = l * len(FM_NAMES) + FM_IDX[name]
            return biasfm[:, j:j + 1]

        btok = sb("btok", [1, 128]); BTOK = Buf("btok")
        kb.pin(BTOK)
        btok_b = sb("btok_b", [1, 128], BF16); BTOKB = Buf("btok_b")

        def tok_unit(l, col, width, tiles, evac):
            wt, WT = load_w(w_in[l], 8, [(col, width, 0)])
            kb.dma("sp", btok[:, 0:width], c_btok[:, l * N_IN + col:l * N_IN + col + width], [], [BTOK], BTOK)
            cp(btok_b[:, 0:width], btok[:, 0:width], [BTOK], [BTOKB])
            for ti, (c0, n) in enumerate(tiles):
                pt, PT = next_ps()
                hb_ = [HN[b] for b in blocks_of(c0, n)]
                for dc in range(8):
                    mm(pt[0:n, 0:width], hnT[:, dc, c0:c0 + n], wt[:, dc, 0:width], dc == 0, False, [WT] + hb_, [PT])
                mm(pt[0:n, 0:width], ones_b[0:1, 0:n], btok_b[0:1, 0:width], False, True, [ONESB, BTOKB], [PT])
                evac(pt, PT, ti, c0, n)

        gsb = [sb(f"gsb{i}", [128, 448]) for i in range(2)]
        GSB = [Buf(f"gsb{i}") for i in range(2)]
        gtmp = sb("gtmp", [128, 448]); GTMP = Buf("gtmp")
        g_rr = [0]

        def branch_out(l, br, XT, XB, w2d):
            for dc in range(8):
                gname = f"g{br}_{dc}"
                wtg, WTG = load_w(w_in[l], 8, FM_UNITS[gname])
                wty, WTY = load_w(w2d, 4, [(dc * 128, 128, 0)])
                for bi, (t0, n) in enumerate(TB):
                    i = g_rr[0] % 2
                    g_rr[0] += 1
                    pg, PG = next_ps()
                    for c in range(8):
                        mm(pg[:, 0:n], wtg[:, c, :], hnT[:, c, t0:t0 + n], c == 0, c == 7, [WTG, HN[bi]], [PG])
                    if debug:
                        ms(gsb[i][:, 0:n], 1.0, [GSB[i]])
                    else:
                        act(gsb[i][:, 0:n], pg[:, 0:n], AF.Sigmoid, [PG, BIASFM], [GSB[i]], bias=bias_col(l, gname), scale=1.0)
                    py, PY = next_ps()
                    for c in range(4):
                        mm(py[:, 0:n], wty[:, c, :], XT[:, c, t0:t0 + n], c == 0, c == 3, [WTY, XB], [PY])
                    if br == 0 or debug:
                        tt(merged[:, dc, t0:t0 + n], py[:, 0:n], gsb[i][:, 0:n], ALU.mult, [PY, GSB[i]], [MG[bi]])
                    else:
                        tt(gtmp[:, 0:n], py[:, 0:n], gsb[i][:, 0:n], ALU.mult, [PY, GSB[i]], [GTMP])
                        tt(merged[:, dc, t0:t0 + n], merged[:, dc, t0:t0 + n], gtmp[:, 0:n], ALU.add, [MG[bi], GTMP], [MG[bi]])

        with kb.phase() as ph:
            xt = [sb(f"xt{i}", [128, D], stack=ph) for i in range(2)]
            XT_ = [Buf(f"xt{i}") for i in range(2)]
            xo = [sb(f"xo{i}", [128, 8, 128], stack=ph) for i in range(2)]
            XO = [Buf(f"xo{i}") for i in range(2)]
            tiles = [(meta, 0, 16, 0)] + [(xp, 128 * i, 128, 16 + 128 * i) for i in range(16)] + [(xs, 0, 128, TP)]
            for ti, (src, r0, nr, c0) in enumerate(tiles):
                i = ti % 2
                kb.dma("sp", xt[i][0:nr, :], src[r0:r0 + nr, :], [], [XT_[i]], XT_[i])
                for hf in range(2):
                    pt, PT = next_ps()
                    for q in range(4):
                        dc = hf * 4 + q
                        tr(pt[:, q * 128:q * 128 + nr], xt[i][0:nr, dc * 128:(dc + 1) * 128], ident[0:nr, 0:nr], [XT_[i], IDENT], [PT])
                    act(xo[i][:, hf * 4:hf * 4 + 4, 0:nr], pt[:, :].rearrange("p (q t) -> p q t", q=4)[:, :, 0:nr], AF.Copy, [PT], [XO[i]])
                wr = [HTD[c][b] for c in range(8) for b in blocks_of(c0, nr)]
                kb.dma("sp", hT_d[:, :, c0:c0 + nr].rearrange("c p t -> p c t"), xo[i][:, :, 0:nr], [XO[i]], wr, XO[i], accum=True)
            kb.barrier()

        tok_tiles = [(0, 16, 0)] + [(16 + 128 * i, 128, 16 + 128 * i) for i in range(16)]
        all_tiles = tok_tiles + [(TP, 128, None)]

        def rmsnorm_block(ph, l, gains, tag):
            hb = [sb(f"hb{i}{tag}", [128, 8, 448], stack=ph) for i in range(2)]
            HB = [Buf(f"hb{i}") for i in range(2)]
            sq = sb("sq" + tag, [128, 8, 448], stack=ph); SQ = Buf("sq")
            rs = sb("rs" + tag, [128, 448], stack=ph); RS = Buf("rs")
            rs2 = sb("rs2" + tag, [128, 448], stack=ph); RS2 = Buf("rs2")
            for bi, (t0, n) in enumerate(TB):
                i = bi % 2
                kb.dma("sp", hb[i][:, :, 0:n], hT_d[:, :, t0:t0 + n].rearrange("c p t -> p c t"),
                       [HTD[c][bi] for c in range(8)], [HB[i]], HB[i])
                act(sq[:, :, 0:n], hb[i][:, :, 0:n], AF.Square, [HB[i]], [SQ])
                pt, PT = next_ps()
                for dc in range(8):
                    mm(pt[:, 0:n], ones_f[:, :], sq[:, dc, 0:n], dc == 0, dc == 7, [ONESF, SQ], [PT])
                act(rs[:, 0:n], pt[:, 0:n], AF.Sqrt, [PT], [RS], bias=EPS, scale=1.0 / D)
                kb.op("dve", lambda e: e.reciprocal(out=rs2[:, 0:n], in_=rs[:, 0:n]), [RS], [RS2])
                for dc in range(8):
                    stt(hnT[:, dc, t0:t0 + n], hb[i][:, dc, 0:n], gains[:, l * 8 + dc:l * 8 + dc + 1], rs2[:, 0:n],
                        ALU.mult, ALU.mult, [HB[i], NORMG, RS2], [HN[bi]])

        ot_rr = [0]

        def rows_out(ot, OT, srcf, SRC, width, dst_of, tiles_, idn=None):
            for (c0, n, r0) in tiles_:
                pt, PT = next_ps()
                tr(pt[0:n, 0:128], srcf[:, c0:c0 + n], ident[:, :], [SRC, IDENT], [PT])
                j = ot_rr[0] % 3
                ot_rr[0] += 1
                act(ot[j][0:n, 0:width], pt[0:n, 0:width], AF.Copy, [PT], [OT[j]])
                kb.dma("sp", dst_of(r0, n), ot[j][0:n, 0:width], [OT[j]], [], OT[j])

        def phase_B(l):
            cv0 = l * 12
            rrb = [0]
            with kb.phase() as ph:
                cy = sb("cy", [128, 4, NT], stack=ph); CY = Buf("cy")
                sg = [sb(f"bsg{i}", [128, 448], stack=ph) for i in range(2)]
                SG = [Buf(f"bsg{i}") for i in range(2)]
                with kb.phase() as ph1:
                    up = sb("up", [128, 4, 30 + TP], stack=ph1); UP = Buf("up")
                    xps = sb("xps", [128, 4, 16, 38], stack=ph1); XPS = Buf("xps")
                    stc = [sb(f"stc{i}", [120, 512], stack=ph1) for i in range(2)]
                    STC = [Buf(f"stc{i}") for i in range(2)]
                    usm = sb("usm", [128, 4, 128], stack=ph1); USM = Buf("usm")
                    bot = [sb(f"bot{i}", [128, 512], stack=ph1) for i in range(2)]
                    BOT = [Buf(f"bot{i}") for i in range(2)]
                    DD = Buf("dd")
                    ms(up[:, :, 0:30], 0.0, [UP])
                    for g4 in range(4):
                        i = g4 % 2
                        kb.dma("sp", stc[i][:, :], st_c[l, 4 * g4:4 * g4 + 4, :, :].rearrange("s r c -> (s r) c"), [], [STC[i]], STC[i])
                        pt, PT = next_ps()
                        for j in range(4):
                            tr(pt[:, j * 128:j * 128 + 120], stc[i][:, j * 128:(j + 1) * 128], ident[0:120, 0:120], [STC[i], IDENT], [PT])
                        for j in range(4):
                            kb.op("act", lambda e: e.activation(out=xps[:, j, 4 * g4:4 * g4 + 4, 0:30],
                                                                in_=pt[:, j * 128:j * 128 + 120].rearrange("p (s r) -> p s r", r=30),
                                                                func=AF.Copy), [PT], [XPS], accum=True)
                    for j in range(4):
                        wta, WTA = load_w(w_in[l], 8, FM_UNITS[f"ba{j}"])
                        wtg, WTG = load_w(w_in[l], 8, FM_UNITS[f"bg{j}"])
                        for bi, (t0, n) in enumerate(TB):
                            i = rrb[0] % 2
                            rrb[0] += 1
                            pg, PG = next_ps()
                            for c in range(8):
                                mm(pg[:, 0:n], wtg[:, c, :], hnT[:, c, t0:t0 + n], c == 0, c == 7, [WTG, HN[bi]], [PG])
                            act(sg[i][:, 0:n], pg[:, 0:n], AF.Sigmoid, [PG, BIASFM], [SG[i]], bias=bias_col(l, f"bg{j}"), scale=1.0)
                            pa, PA = next_ps()
                            for c in range(8):
                                mm(pa[:, 0:n], wta[:, c, :], hnT[:, c, t0:t0 + n], c == 0, c == 7, [WTA, HN[bi]], [PA])
                            np_ = max(0, min(t0 + n, TP) - t0)
                            if np_ > 0:
                                stt(up[:, j, 30 + t0:30 + t0 + np_], pa[:, 0:np_], bias_col(l, f"ba{j}"), sg[i][:, 0:np_],
                                    ALU.add, ALU.mult, [PA, BIASFM, SG[i]], [UP])
                            if t0 + n > TP:
                                so = TP - t0
                                kb.op("dve", lambda e: e.scalar_tensor_tensor(
                                    out=xps[:, j, :, 30:38], in0=pa[:, so:so + 128].rearrange("p (s t) -> p s t", t=8),
                                    scalar=bias_col(l, f"ba{j}"), in1=sg[i][:, so:so + 128].rearrange("p (s t) -> p s t", t=8),
                                    op0=ALU.add, op1=ALU.mult), [PA, BIASFM, SG[i]], [XPS], accum=True)
                    for j in range(4):
                        wcol = lambda k: convw[:, l * 124 + j * 31 + k:l * 124 + j * 31 + k + 1]
                        bcol = cvec[:, cv0 + j:cv0 + j + 1]
                        ysv = cy[:, j, TP:NT].rearrange("p (s t) -> p s t", t=8)
                        ts(cy[:, j, 0:TP], up[:, j, 0:TP], wcol(0), bcol, ALU.mult, ALU.add, [UP, CONVW, CVEC], [CY])
                        ts(ysv, xps[:, j, :, 0:8], wcol(0), bcol, ALU.mult, ALU.add, [XPS, CONVW, CVEC], [CY])
                        for k in range(1, 31):
                            stt(cy[:, j, 0:TP], up[:, j, k:k + TP], wcol(k), cy[:, j, 0:TP], ALU.mult, ALU.add, [UP, CONVW, CY], [CY])
                            stt(ysv, xps[:, j, :, k:k + 8], wcol(k), ysv, ALU.mult, ALU.add, [XPS, CONVW, CY], [CY])
                    pt, PT = next_ps()
                    for j in range(4):
                        tr(pt[0:30, j * 128:(j + 1) * 128], up[:, j, TP:TP + 30], ident[:, :], [UP, IDENT], [PT])
                    act(bot[0][0:30, :], pt[0:30, :], AF.Copy, [PT], [BOT[0]])
                    kb.dma("sp", p_c[l, :, :], bot[0][0:30, :], [BOT[0]], [], BOT[0])
                    for j in range(4):
                        cp(usm[:, j, :].rearrange("p (s t) -> p s t", t=8), xps[:, j, :, 30:38], [XPS], [USM])
                    pt, PT = next_ps()
                    for j in range(4):
                        tr(pt[:, j * 128:(j + 1) * 128], usm[:, j, :], ident[:, :], [USM, IDENT], [PT])
                    act(bot[1][:, :], pt[:, :], AF.Copy, [PT], [BOT[1]])
                    for s in range(16):
                        kb.dma("sp", s_c[l, s, 22:30, :], bot[1][8 * s:8 * s + 8, :], [BOT[1]], [], BOT[1], group=(s > 0))
                    kb.dma("sp", s_c[l, :, 0:22, :], st_c[l, :, 8:30, :], [], [], DD)
                    kb.barrier()
                with kb.phase() as ph2:
                    yn = sb("yn", [128, 4, NT], BF16, stack=ph2); YN = Buf("yn")
                    zb = sb("zb", [128, 4, NT], BF16, stack=ph2); ZB = Buf("zb")
                    ysq = sb("ysq", [128, 4, 448], stack=ph2); YSQ = Buf("ysq")
                    mean = sb("mean", [128, 448], stack=ph2); MEAN = Buf("mean")
                    msq = sb("msq", [128, 448], stack=ph2); MSQ = Buf("msq")
                    rstd = sb("rstd", [128, 448], stack=ph2); RSTD = Buf("rstd")
                    dtm = [sb(f"dtm{i}", [128, 448], stack=ph2) for i in range(2)]
                    DTM = [Buf(f"dtm{i}") for i in range(2)]
                    for bi, (t0, n) in enumerate(TB):
                        act(ysq[:, :, 0:n], cy[:, :, t0:t0 + n], AF.Square, [CY], [YSQ])
                        p1, P1 = next_ps()
                        for j in range(4):
                            mm(p1[:, 0:n], ones_f[:, :], cy[:, j, t0:t0 + n], j == 0, j == 3, [ONESF, CY], [P1])
                        p2, P2 = next_ps()
                        for j in range(4):
                            mm(p2[:, 0:n], ones_f[:, :], ysq[:, j, 0:n], j == 0, j == 3, [ONESF, YSQ], [P2])
                        act(mean[:, 0:n], p1[:, 0:n], AF.Identity, [P1], [MEAN], scale=1.0 / 512)
                        tt(msq[:, 0:n], mean[:, 0:n], mean[:, 0:n], ALU.mult, [MEAN], [MSQ])
                        stt(msq[:, 0:n], p2[:, 0:n], 1.0 / 512, msq[:, 0:n], ALU.mult, ALU.subtract, [P2, MSQ], [MSQ])
                        act(rstd[:, 0:n], msq[:, 0:n], AF.Sqrt, [MSQ], [RSTD], bias=EPS, scale=1.0)
                        kb.op("dve", lambda e: e.reciprocal(out=rstd[:, 0:n], in_=rstd[:, 0:n]), [RSTD], [RSTD])
                        for j in range(4):
                            i = j % 2
                            tt(dtm[i][:, 0:n], cy[:, j, t0:t0 + n], mean[:, 0:n], ALU.subtract, [CY, MEAN], [DTM[i]])
                            tt(dtm[i][:, 0:n], dtm[i][:, 0:n], rstd[:, 0:n], ALU.mult, [DTM[i], RSTD], [DTM[i]])
                            act(yn[:, j, t0:t0 + n], dtm[i][:, 0:n], AF.Silu, [DTM[i], CVEC], [YN],
                                scale=cvec[:, cv0 + 4 + j:cv0 + 5 + j], bias=cvec[:, cv0 + 8 + j:cv0 + 9 + j])
                    for j in range(4):
                        wtz, WTZ = load_w(w_in[l], 8, FM_UNITS[f"bz{j}"])
                        wtp, WTP = load_w(conv_pw[l], 4, [(j * 128, 128, 0)])
                        for bi, (t0, n) in enumerate(TB):
                            i = rrb[0] % 2
                            rrb[0] += 1
                            pz, PZ = next_ps()
                            for c in range(8):
                                mm(pz[:, 0:n], wtz[:, c, :], hnT[:, c, t0:t0 + n], c == 0, c == 7, [WTZ, HN[bi]], [PZ])
                            act(sg[i][:, 0:n], pz[:, 0:n], AF.Silu, [PZ, BIASFM], [SG[i]], bias=bias_col(l, f"bz{j}"), scale=1.0)
                            pp, PP = next_ps()
                            for c in range(4):
                                mm(pp[:, 0:n], wtp[:, c, :], yn[:, c, t0:t0 + n], c == 0, c == 3, [WTP, YN], [PP])
                            tt(zb[:, j, t0:t0 + n], pp[:, 0:n], sg[i][:, 0:n], ALU.mult, [PP, SG[i]], [ZB])
                    branch_out(l, 1, zb, ZB, w_pb[l])
                    kb.barrier()

        def phase_A(l):
            pb = lambda p: p[:, :].bitcast(BF16)
            chunks = [(0, 16)] + [(16 + 64 * c, 64) for c in range(32)]
            with kb.phase() as ph:
                xa = sb("xa", [128, 4, NT], BF16, stack=ph); XA = Buf("xa")
                gng = sb("gngt", [128, 128], stack=ph); GNG = Buf("gng")
                kb.dma("sp", gng[:], gng_d[l:l + 1, :].partition_broadcast(128), [], [GNG], GNG)
                T = [sb(f"aT{i}", [128, NT], stack=ph) for i in range(4)]
                TT = [Buf(f"aT{i}") for i in range(4)]
                qt_ = sb("aqt", [128, NT], BF16, stack=ph); QT = Buf("aqt")
                kt_ = sb("akt", [128, NT], BF16, stack=ph); KT = Buf("akt")
                kd_ = sb("akd", [128, NT], BF16, stack=ph); KD = Buf("akd")
                az_ = sb("aaz", [128, NT], BF16, stack=ph); AZ = Buf("aaz")
                qs = [sb(f"aqs{i}", [128, 448], stack=ph) for i in range(2)]
                QS = [Buf(f"aqs{i}") for i in range(2)]
                r1 = sb("aR1", [128, 4352], BF16, stack=ph); R1 = Buf("aR1")
                r2 = sb("aR2", [128, 4224], BF16, stack=ph); R2 = Buf("aR2")
                r3 = sb("aR3", [128, 4224], BF16, stack=ph); R3 = Buf("aR3")
                V = r1[0:64, :].rearrange("p (c v) -> p c v", v=128)
                SO = r1[:, 0:4096].bitcast(F32).rearrange("p (s v) -> p s v", v=128)
                KDT = r2[0:64, :].rearrange("p (c v) -> p c v", v=128)
                SOb = r2[:, 0:2048].rearrange("p (s v) -> p s v", v=128)
                QZ = r2[:, 2048:4096].rearrange("p (s v) -> p s v", v=128)
                OR = r3[0:64, :].rearrange("p (c v) -> p c v", v=128)
                VZ = r3[:, 0:2048].rearrange("p (s v) -> p s v", v=128)
                vs = sb("avs", [128, 128], BF16, stack=ph); VS = Buf("avs")
                kdts = sb("akdts", [128, 128], BF16, stack=ph); KDTS = Buf("akdts")
                ors = sb("aors", [128, 128], BF16, stack=ph); ORS = Buf("aors")
                attm = [sb(f"attm{i}", [128, 128], BF16, stack=ph) for i in range(2)]
                ATT = [Buf(f"attm{i}") for i in range(2)]
                junk = sb("ajunk", [128, 128], BF16, stack=ph); JUNK = Buf("ajunk")
                ss = sb("ass", [128, 34], stack=ph); SS = Buf("ass")
                rsa = sb("arsa", [128, 34], stack=ph); RSA = Buf("arsa")
                S = sb("aS", [128, 128], stack=ph); S_ = Buf("aS")
                Sb = sb("aSb", [128, 128], BF16, stack=ph); SB_ = Buf("aSb")
                ebl = sb("aebl", [128, 49], stack=ph); EBL = Buf("aebl")
                bs = sb("abs", [128, 49], stack=ph); BS = Buf("abs")
                bl = sb("abl", [128, 49], stack=ph); BL = Buf("abl")
                rr = [0]

                def views(t):
                    return (t[:, 16:TP].rearrange("p (c t) -> p c t", t=64), t[:, TP:NT].rearrange("p (s t) -> p s t", t=8))

                for hd in range(4):
                    lbc = lbv[:, hd, l:l + 1]; omlc = omlv[:, hd, l:l + 1]; nomlc = nomlv[:, hd, l:l + 1]
                    B = T[0]

                    def ev_f(pt, PT, bi, t0, n):
                        act(T[0][:, t0:t0 + n], pt[:, 0:n], AF.Sigmoid, [PT, BIASFM], [TT[0]], bias=bias_col(l, f"af{hd}"), scale=1.0)
                    fm_unit(l, f"af{hd}", ev_f)
                    act(T[1][:, :], T[0][:, :], AF.Ln, [TT[0], OMLV, LBV], [TT[1]], scale=omlc, bias=lbc)
                    ts(T[2][:, :], T[0][:, :], nomlc, omlc, ALU.mult, ALU.add, [TT[0], NOMLV, OMLV], [TT[2]])
                    onesbc = ones_f[:, 0:1]
                    kb.op("dve", lambda e: e.tensor_tensor_scan(out=T[0][:, 0:TP], data0=onesbc.to_broadcast([128, TP]),
                                                                 data1=T[1][:, 0:TP], initial=0.0, op0=ALU.mult, op1=ALU.add),
                          [TT[1], ONESF, TT[0]], [TT[0]])
                    kb.op("dve", lambda e: e.tensor_tensor_scan(out=T[0][:, TP:NT], data0=onesbc.to_broadcast([128, NSM]),
                                                                 data1=T[1][:, TP:NT], initial=0.0, op0=ALU.mult, op1=ALU.add),
                          [TT[1], ONESF, TT[0]], [TT[0]])
                    ms(bs[:, 0:1], 0.0, [BS]); ms(bs[:, 33:34], 0.0, [BS])
                    cp(bs[:, 1:33], B[:, 15:15 + 64 * 32:64], [TT[0]], [BS])
                    cp(bs[:, 34:49], B[:, TP + 7:TP + 7 + 8 * 15:8], [TT[0]], [BS])
                    cp(bl[:, 0:1], B[:, 15:16], [TT[0]], [BL])
                    cp(bl[:, 1:33], B[:, 79:79 + 64 * 32:64], [TT[0]], [BL])
                    cp(bl[:, 33:49], B[:, TP + 7:NT:8], [TT[0]], [BL])
                    bx, bsm = views(B)
                    t1x, t1s = views(T[1])
                    cp(T[1][:, 0:16], B[:, 0:16], [TT[0]], [TT[1]])
                    tt(t1x, bx, bs[:, 1:33].unsqueeze(2).to_broadcast([128, 32, 64]), ALU.subtract, [TT[0], BS], [TT[1]])
                    tt(t1s, bsm, bs[:, 33:49].unsqueeze(2).to_broadcast([128, 16, 8]), ALU.subtract, [TT[0], BS], [TT[1]])
                    act(T[3][:, :], T[1][:, :], AF.Exp, [TT[1]], [TT[3]])
                    act(T[1][:, :], T[1][:, :], AF.Exp, [TT[1]], [TT[1]], scale=-1.0)
                    tt(kt_[:, :], T[2][:, :], T[1][:, :], ALU.mult, [TT[2], TT[1]], [KT])
                    tt(T[1][:, 0:16], B[:, 0:16], bl[:, 0:1].to_broadcast([128, 16]), ALU.subtract, [TT[0], BL, KT], [TT[1]])
                    tt(t1x, bx, bl[:, 1:33].unsqueeze(2).to_broadcast([128, 32, 64]), ALU.subtract, [TT[0], BL], [TT[1]])
                    tt(t1s, bsm, bl[:, 33:49].unsqueeze(2).to_broadcast([128, 16, 8]), ALU.subtract, [TT[0], BL], [TT[1]])
                    act(T[1][:, :], T[1][:, :], AF.Exp, [TT[1]], [TT[1]], scale=-1.0)
                    tt(kd_[:, :], T[2][:, :], T[1][:, :], ALU.mult, [TT[2], TT[1]], [KD])
                    cp(ebl[:, 0:1], T[3][:, 15:16], [TT[3]], [EBL])
                    cp(ebl[:, 1:33], T[3][:, 79:79 + 64 * 32:64], [TT[3]], [EBL])
                    cp(ebl[:, 33:49], T[3][:, TP + 7:NT:8], [TT[3]], [EBL])

                    def ev_q(pt, PT, bi, t0, n):
                        i = rr[0] % 2
                        rr[0] += 1
                        act(qs[i][:, 0:n], pt[:, 0:n], AF.Silu, [PT, BIASFM], [QS[i]], bias=bias_col(l, f"aq{hd}"), scale=1.0)
                        tt(qt_[:, t0:t0 + n], qs[i][:, 0:n], T[3][:, t0:t0 + n], ALU.mult, [QS[i], TT[3]], [QT])
                    fm_unit(l, f"aq{hd}", ev_q)

                    def ev_z(pt, PT, bi, t0, n):
                        act(az_[:, t0:t0 + n], pt[:, 0:n], AF.Silu, [PT, BIASFM], [AZ], bias=bias_col(l, f"az{hd}"), scale=1.0)
                    fm_unit(l, f"az{hd}", ev_z)

                    def ev_v(pt, PT, ti, c0, n):
                        if ti < 33:
                            act(V[0:n, ti, :], pt[0:n, 0:128], AF.Copy, [PT], [R1])
                        else:
                            act(vs[:, :], pt[:, 0:128], AF.Copy, [PT], [VS])
                    tok_unit(l, O_AI + 128 * hd, 128, chunks + [(TP, 128)], ev_v)

                    pt, PT = next_ps()
                    tr(pb(pt)[0:16, 0:128], kd_[:, 0:16], identb[:, :], [KD, IDENTB], [PT])
                    act(KDT[0:16, 0, :], pb(pt)[0:16, 0:128], AF.Copy, [PT], [R2])
                    for g in range(8):
                        pt, PT = next_ps()
                        for q in range(4):
                            c0 = 16 + 64 * (4 * g + q)
                            tr(pb(pt)[0:64, q * 128:(q + 1) * 128], kd_[:, c0:c0 + 64], identb[:, :], [KD, IDENTB], [PT])
                        act(KDT[0:64, 1 + 4 * g:5 + 4 * g, :], pb(pt)[0:64, 0:512].rearrange("p (q v) -> p q v", v=128), AF.Copy, [PT], [R2])
                    pt, PT = next_ps()
                    tr(pb(pt)[:, 0:128], kd_[:, TP:NT], identb[:, :], [KD, IDENTB], [PT])
                    act(kdts[:, :], pb(pt)[:, 0:128], AF.Copy, [PT], [KDTS])

                    ms(S[:, :], 0.0, [S_]); ms(Sb[:, :], 0.0, [SB_]); ms(ss[:, :], 1.0, [SS])
                    for ci, (c0, n) in enumerate(chunks):
                        i = ci % 2
                        pa, PA = next_ps()
                        mm(pa[0:n, 0:n], kt_[:, c0:c0 + n], qt_[:, c0:c0 + n], True, True, [KT, QT], [PA])
                        tt(attm[i][0:n, 0:n], pa[0:n, 0:n], tri_st[0:n, 0:n], ALU.mult, [PA, MASKS], [ATT[i]])
                        po, PO = next_ps()
                        mm(po[0:n, 0:128], qt_[:, c0:c0 + n], Sb[:, :], True, False, [QT, SB_], [PO])
                        mm(po[0:n, 0:128], attm[i][0:n, 0:n], V[0:n, ci, :], False, True, [ATT[i], R1], [PO])
                        act(OR[0:n, ci, :], po[0:n, 0:128], AF.Copy, [PO], [R3])
                        act(junk[0:n, :], po[0:n, 0:128], AF.Square, [PO], [JUNK, SS], accum_out=ss[0:n, ci:ci + 1])
                        pS, PSB = next_ps()
                        mm(pS[:, 0:128], KDT[0:n, ci, :], V[0:n, ci, :], True, True, [R2, R1], [PSB])
                        stt(Sb[:, :], S[:, :], ebl[:, ci:ci + 1], pS[:, 0:128], ALU.mult, ALU.add, [S_, EBL, PSB], [SB_])
                        stt(S[:, :], S[:, :], ebl[:, ci:ci + 1], pS[:, 0:128], ALU.mult, ALU.add, [S_, EBL, PSB], [S_])
                    kb.dma("sp", p_h[l, hd, :, :], S[:, :], [S_], [], S_)
                    act(rsa[0:64, 0:33], ss[0:64, 0:33], AF.Sqrt, [SS], [RSA], bias=EPS, scale=1.0 / 128)
                    kb.op("dve", lambda e: e.reciprocal(out=rsa[0:64, 0:33], in_=rsa[0:64, 0:33]), [RSA], [RSA])
                    tt(OR[:, :, :], OR[:, :, :], rsa[0:64, 0:33].unsqueeze(2).to_broadcast([64, 33, 128]), ALU.mult, [R3, RSA], [R3])
                    tt(OR[:, :, :], OR[:, :, :], gng[0:64, :].unsqueeze(1).to_broadcast([64, 33, 128]), ALU.mult, [R3, GNG], [R3])
                    pt, PT = next_ps()
                    tr(pb(pt)[:, 0:16], OR[0:16, 0, :], identb[0:16, 0:16], [R3, IDENTB], [PT])
                    tt(xa[:, hd, 0:16], pb(pt)[:, 0:16], az_[:, 0:16], ALU.mult, [PT, AZ], [XA])
                    for g in range(4):
                        pt, PT = next_ps()
                        for q in range(8):
                            tr(pb(pt)[:, q * 64:(q + 1) * 64], OR[0:64, 1 + 8 * g + q, :], identb[0:64, 0:64], [R3, IDENTB], [PT])
                        c0 = 16 + 512 * g
                        tt(xa[:, hd, c0:c0 + 512], pb(pt)[:, 0:512], az_[:, c0:c0 + 512], ALU.mult, [PT, AZ], [XA])

                    kb.dma("sp", SO, st_h[l, :, hd, :, :].rearrange("s k v -> k s v"), [], [R1], R1)
                    cp(SOb, SO, [R1], [R2], eng="pool")
                    kb.op("dve", lambda e: e.tensor_tensor(out=QZ, in0=qt_[:, TP:NT].unsqueeze(1).to_broadcast([128, 16, 128]),
                                                           in1=seqsel[:, :, :], op=ALU.mult), [QT, SEQSEL], [R2], accum=True)
                    tt(VZ, vs[:, :].unsqueeze(1).to_broadcast([128, 16, 128]), rowsel[:, :].unsqueeze(2).to_broadcast([128, 16, 128]),
                       ALU.mult, [VS, ROWSEL], [R3])
                    pa, PA = next_ps()
                    mm(pa[:, 0:128], kt_[:, TP:NT], qt_[:, TP:NT], True, True, [KT, QT], [PA])
                    tt(attm[0][:, :], pa[:, 0:128], blk_st, ALU.mult, [PA, MASKS], [ATT[0]])
                    po, PO = next_ps()
                    for s in range(16):
                        mm(po[:, 0:128], QZ[:, s, :], SOb[:, s, :], s == 0, False, [R2], [PO])
                    mm(po[:, 0:128], attm[0][:, :], vs[:, :], False, True, [ATT[0], VS], [PO])
                    act(ors[:, :], po[:, 0:128], AF.Copy, [PO], [ORS])
                    act(junk[:, :], po[:, 0:128], AF.Square, [PO], [JUNK, SS], accum_out=ss[:, 33:34])
                    for s in range(16):
                        pS, PSB = next_ps()
                        mm(pS[:, 0:128], kdts[:, :], VZ[:, s, :], True, True, [KDTS, R3], [PSB])
                        stt(SO[:, s, :], SO[:, s, :], ebl[:, 33 + s:34 + s], pS[:, 0:128], ALU.mult, ALU.add, [R1, EBL, PSB], [R1])
                    kb.dma("sp", s_h[l, :, hd, :, :].rearrange("s k v -> k s v"), SO, [R1], [], R1)
                    act(rsa[:, 33:34], ss[:, 33:34], AF.Sqrt, [SS], [RSA], bias=EPS, scale=1.0 / 128)
                    kb.op("dve", lambda e: e.reciprocal(out=rsa[:, 33:34], in_=rsa[:, 33:34]), [RSA], [RSA])
                    ts(ors[:, :], ors[:, :], rsa[:, 33:34], None, ALU.mult, None, [ORS, RSA], [ORS])
                    tt(ors[:, :], ors[:, :], gng[:, :], ALU.mult, [ORS, GNG], [ORS])
                    pt, PT = next_ps()
                    tr(pb(pt)[:, 0:128], ors[:, :], identb[:, :], [ORS, IDENTB], [PT])
                    tt(xa[:, hd, TP:NT], pb(pt)[:, 0:128], az_[:, TP:NT], ALU.mult, [PT, AZ], [XA])
                branch_out(l, 0, xa, XA, w_pa[l])
                kb.barrier()

        def topk_group(items):
            act_items = []
            for (nq, S, sc, SC, m01, M01, st) in items:
                (amax, mid, htab, cnt, gh, lo, TKB) = st
                if S <= 256:
                    continue
                ts(gh[0:nq, :], amax[0:nq, :], 2.0, 1.0, ALU.mult, ALU.add, [TKB], [TKB])
                ts(htab[0:nq, :], pow2[0:nq, :], gh[0:nq, 0:1], None, ALU.mult, None, [POW2, TKB], [TKB])
                ms(mid[0:nq, :], -0.5, [TKB])
                act_items.append((nq, S, sc, SC, m01, M01, st))
            for it in range(NIT):
                for (nq, S, sc, SC, m01, M01, st) in act_items:
                    (amax, mid, htab, cnt, gh, lo, TKB) = st
                    kb.op("dve", lambda e: e.tensor_scalar(out=m01[0:nq, 0:S], in0=sc[0:nq, 0:S], scalar1=mid[0:nq, 0:1], scalar2=None,
                                                           op0=ALU.is_gt, op1=ALU.add, accum_out=cnt[0:nq, 0:1]), [SC, TKB], [TKB, M01])
                for (nq, S, sc, SC, m01, M01, st) in act_items:
                    (amax, mid, htab, cnt, gh, lo, TKB) = st
                    ts(gh[0:nq, :], cnt[0:nq, :], 255.5, htab[0:nq, it:it + 1], ALU.is_gt, ALU.mult, [TKB], [TKB])
                for (nq, S, sc, SC, m01, M01, st) in act_items:
                    (amax, mid, htab, cnt, gh, lo, TKB) = st
                    if it < NIT - 1:
                        stt(mid[0:nq, :], gh[0:nq, :], htab[0:nq, it + 1:it + 2], mid[0:nq, :], ALU.subtract, ALU.add, [TKB], [TKB])
                    else:
                        stt(lo[0:nq, :], gh[0:nq, :], htab[0:nq, it:it + 1], mid[0:nq, :], ALU.subtract, ALU.add, [TKB], [TKB])
            for (nq, S, sc, SC, m01, M01, st) in items:
                (amax, mid, htab, cnt, gh, lo, TKB) = st
                thr = neg29[0:nq, 0:1] if S <= 256 else lo[0:nq, 0:1]
                ts(m01[0:nq, 0:S], sc[0:nq, 0:S], thr, None, ALU.is_gt, None, [SC, TKB, NEG29], [M01])

        def tk_state(stack, tag):
            return (sb(f"tka{tag}", [128, 1], stack=stack), sb(f"tkm{tag}", [128, 1], stack=stack), sb(f"tkh{tag}", [128, NIT], stack=stack),
                    sb(f"tkc{tag}", [128, 1], stack=stack), sb(f"tkg{tag}", [128, 1], stack=stack), sb(f"tkl{tag}", [128, 1], stack=stack),
                    Buf(f"tk{tag}"))

        def phase_C(l):
            pb = lambda p: p[:, :].bitcast(BF16)
            with kb.phase() as ph:
                qtc = sb("cqt", [128, 4, NT], BF16, stack=ph); QTC = Buf("cqt")
                qi = sb("cqi", [128, 2, NT], BF16, stack=ph); QI = Buf("cqi")
                qis = sb("cqis", [64, 4, 128], BF16, stack=ph); QIS = Buf("cqis")
                ktb = sb("cktb", [128, NT], BF16, stack=ph); KTB = Buf("cktb")
                kib = sb("ckib", [128, NT], BF16, stack=ph); KIB = Buf("ckib")
                vaug = sb("cvaug", [128, 18, 2, 65], BF16, stack=ph); VAUG = Buf("cvaug")
                cw = sb("ccw", [128, 18, 4], stack=ph); CW = Buf("ccw")
                oct_ = sb("coct", [128, 4, NT], BF16, stack=ph); OCT = Buf("coct")
                with kb.phase() as p1:
                    cosT = sb("cosT", [128, NT], stack=p1); COS = Buf("cos")
                    sinT = sb("sinT", [128, NT], stack=p1); SIN = Buf("sin")
                    kb.dma("sp", cosT[:], c_cos, [], [COS], COS)
                    kb.dma("sp", sinT[:], c_sin, [], [SIN], SIN)
                    xf = [sb(f"xf{i}", [128, 448], stack=p1) for i in range(2)]
                    XF = [Buf(f"xf{i}") for i in range(2)]
                    t1 = sb("t1", [128, 448], stack=p1); T1 = Buf("t1")
                    t2 = sb("t2", [128, 448], stack=p1); T2 = Buf("t2")
                    kTf = sb("kTf", [128, NT], stack=p1); KTF = Buf("kTf")
                    kiTf = sb("kiTf", [128, NT], stack=p1); KITF = Buf("kiTf")
                    ot = [sb(f"ot{i}", [128, 128], stack=p1) for i in range(3)]
                    OT = [Buf(f"ot{i}") for i in range(3)]
                    vtok = [sb(f"vtok{i}", [128, 128], stack=p1) for i in range(2)]
                    VTOK = [Buf(f"vtok{i}") for i in range(2)]

                    def rope_evac(dst_of, DST, name, rows=128):
                        def evac(pt, PT, bi, t0, n):
                            i = bi % 2
                            act(xf[i][0:rows, 0:n], pt[0:rows, 0:n], AF.Identity, [PT, BIASFM], [XF[i]], bias=bias_col(l, name)[0:rows, :], scale=1.0)
                            p2, P2 = next_ps()
                            mm(p2[0:rows, 0:n], rmat[0:rows, 0:rows], xf[i][0:rows, 0:n], True, True, [RMAT, XF[i]], [P2])
                            tt(t1[0:rows, 0:n], xf[i][0:rows, 0:n], cosT[0:rows, t0:t0 + n], ALU.mult, [XF[i], COS], [T1])
                            tt(t2[0:rows, 0:n], p2[0:rows, 0:n], sinT[0:rows, t0:t0 + n], ALU.mult, [P2, SIN], [T2])
                            tt(dst_of(t0, n), t1[0:rows, 0:n], t2[0:rows, 0:n], ALU.add, [T1, T2], [DST])
                        return evac

                    fm_unit(l, "ck", rope_evac(lambda t0, n: kTf[:, t0:t0 + n], KTF, "ck"))
                    fm_unit(l, "cki", rope_evac(lambda t0, n: kiTf[:, t0:t0 + n], KITF, "cki"))
                    cp(ktb[:, :], kTf[:, :], [KTF], [KTB], eng="pool")
                    cp(kib[:, :], kiTf[:, :], [KITF], [KIB], eng="pool")
                    rows_out(ot, OT, kTf, KTF, 128, lambda r0, n: (p_k[l, r0:r0 + n, :] if r0 is not None else s_k[l, :, :]), all_tiles)
                    rows_out(ot, OT, kiTf, KITF, 64, lambda r0, n: (p_ki[l, r0:r0 + n, :] if r0 is not None else s_ki[l, :, :]), all_tiles)
                    for j in range(4):
                        fm_unit(l, f"cq{j}", rope_evac(lambda t0, n, j=j: qtc[:, j, t0:t0 + n], QTC, f"cq{j}"))
                    for j in range(2):
                        fm_unit(l, f"cqi{j}", rope_evac(lambda t0, n, j=j: qi[:, j, t0:t0 + n], QI, f"cqi{j}"))
                    for h in range(4):
                        def evq(pt, PT, bi, t0, n, h=h):
                            so = TP - t0
                            i = bi % 2
                            act(xf[i][0:64, 0:128], pt[0:64, so:so + 128], AF.Identity, [PT, BIASFM], [XF[i]],
                                bias=bias_col(l, f"cqis{h}")[0:64, :], scale=1.0)
                            p2, P2 = next_ps()
                            mm(p2[0:64, 0:128], rmat[0:64, 0:64], xf[i][0:64, 0:128], True, True, [RMAT, XF[i]], [P2])
                            tt(t1[0:64, 0:128], xf[i][0:64, 0:128], cosT[0:64, TP:NT], ALU.mult, [XF[i], COS], [T1])
                            tt(t2[0:64, 0:128], p2[0:64, 0:128], sinT[0:64, TP:NT], ALU.mult, [P2, SIN], [T2])
                            tt(qis[:, h, :], t1[0:64, 0:128], t2[0:64, 0:128], ALU.add, [T1, T2], [QIS])
                        fm_unit(l, f"cqis{h}", evq, blocks={4}, rows=64)

                    ms(vaug[:, :, :, 64:65], 1.0, [VAUG])

                    def ev_v(pt, PT, ti, c0, n):
                        i = ti % 2
                        act(vtok[i][0:n, :], pt[0:n, 0:128], AF.Copy, [PT], [VTOK[i]])
                        cp(vaug[0:n, ti, :, 0:64], vtok[i][0:n, :].rearrange("p (g d) -> p g d", d=64), [VTOK[i]], [VAUG])
                        r0 = all_tiles[ti][2]
                        dst = p_v[l, r0:r0 + n, :] if r0 is not None else s_v[l, :, :]
                        kb.dma("sp", dst, vtok[i][0:n, :], [VTOK[i]], [], VTOK[i])
                    tok_unit(l, O_CV, 128, [(c0, n) for (c0, n, _) in all_tiles], ev_v)

                    def ev_w(pt, PT, ti, c0, n):
                        act(cw[0:n, ti, :], pt[0:n, 0:4], AF.Identity, [PT], [CW], scale=IDX_SCALE)
                    tok_unit(l, O_CW, 4, [(c0, n) for (c0, n, _) in all_tiles], ev_w)
                    kb.barrier()

                with kb.phase() as p2s:
                    NSL = 2
                    scs = [sb(f"csc{i}", [128, TP], stack=p2s) for i in range(NSL)]; SCS = [Buf(f"csc{i}") for i in range(NSL)]
                    m01s = [sb(f"cm01{i}", [128, TP], BF16, stack=p2s) for i in range(NSL)]; M01S = [Buf(f"cm01{i}") for i in range(NSL)]
                    mTs = [sb(f"cmT{i}", [128, 17, 128], BF16, stack=p2s) for i in range(NSL)]; MTS = [Buf(f"cmT{i}") for i in range(NSL)]
                    sts = [tk_state(p2s, f"c{i}") for i in range(NSL)]
                    rl = [sb(f"crl{i}", [128, 512], stack=p2s) for i in range(2)]
                    RL = [Buf(f"crl{i}") for i in range(2)]
                    pP = [sb(f"cP{i}", [128, 4, 128], BF16, stack=p2s) for i in range(3)]
                    PP = [Buf(f"cP{i}") for i in range(3)]
                    osb = sb("cosb", [128, 8, 65], stack=p2s); OSB = Buf("cosb")
                    orc = sb("corc", [128, 8], stack=p2s); ORC = Buf("corc")
                    onb = sb("conb", [128, 512], BF16, stack=p2s); ONB = Buf("conb")
                    rr = [0]
                    pr = [0]
                    for grp0 in range(0, len(tok_tiles), NSL):
                        grp_tiles = list(enumerate(tok_tiles))[grp0:grp0 + NSL]
                        items = []
                        for slot, (qt_i, (c0, nq, _)) in enumerate(grp_tiles):
                            sc, SC, m01, M01, st = scs[slot], SCS[slot], m01s[slot], M01S[slot], sts[slot]
                            S = c0 + nq
                            for h in range(4):
                                b0 = (h % 2) * 64
                                for k0 in range(0, S, 512):
                                    kn = min(512, S - k0)
                                    i = rr[0] % 2
                                    rr[0] += 1
                                    pt, PT = next_ps()
                                    mm(pt[0:nq, 0:kn], qi[b0:b0 + 64, h // 2, c0:c0 + nq], kib[b0:b0 + 64, k0:k0 + kn], True, True, [QI, KIB], [PT])
                                    act(rl[i][0:nq, 0:kn], pt[0:nq, 0:kn], AF.Relu, [PT], [RL[i]])
                                    if h == 0:
                                        ts(sc[0:nq, k0:k0 + kn], rl[i][0:nq, 0:kn], cw[0:nq, qt_i, 0:1], None, ALU.mult, None, [RL[i], CW], [SC])
                                    else:
                                        stt(sc[0:nq, k0:k0 + kn], rl[i][0:nq, 0:kn], cw[0:nq, qt_i, h:h + 1], sc[0:nq, k0:k0 + kn],
                                            ALU.mult, ALU.add, [RL[i], CW, SC], [SC])
                            if S > 256:
                                kb.op("dve", lambda e: e.reduce_max(out=st[0][0:nq, :], in_=sc[0:nq, 0:S], axis=AX.X, apply_absolute_value=True),
                                      [SC], [st[-1]])
                            tt(sc[0:nq, c0:c0 + nq], sc[0:nq, c0:c0 + nq], tribias[0:nq, 0:nq], ALU.add, [SC, MASKS], [SC])
                            items.append((nq, S, sc, SC, m01, M01, st))
                        topk_group(items)
                        for slot, (qt_i, (c0, nq, _)) in enumerate(grp_tiles):
                            m01, M01, maskT, MT = m01s[slot], M01S[slot], mTs[slot], MTS[slot]
                            ktiles = [(0, 16)] + [(16 + 128 * i, 128) for i in range(qt_i)]
                            for g0 in range(0, len(ktiles), 4):
                                pt, PT = next_ps()
                                grp = ktiles[g0:g0 + 4]
                                for q, (kc0, nk) in enumerate(grp):
                                    tr(pb(pt)[0:nk, q * 128:q * 128 + nq], m01[0:nq, kc0:kc0 + nk], identb[0:nq, 0:nq], [M01, IDENTB], [PT])
                                if g0 == 0:
                                    cp(maskT[0:16, 0, 0:nq], pb(pt)[0:16, 0:nq], [PT], [MT])
                                    if len(grp) > 1:
                                        cp(maskT[:, 1:len(grp), 0:nq], pb(pt)[:, 128:128 * len(grp)].rearrange("p (q t) -> p q t", t=128)[:, :, 0:nq], [PT], [MT])
                                else:
                                    cp(maskT[:, g0:g0 + len(grp), 0:nq], pb(pt)[:, 0:128 * len(grp)].rearrange("p (q t) -> p q t", t=128)[:, :, 0:nq], [PT], [MT])
                            steps = [(kt, kc0, nk, g) for kt, (kc0, nk) in enumerate(ktiles) for g in range(2)]

                            def emit_qk(step):
                                kt, kc0, nk, g = step
                                pt, PT = next_ps()
                                mm(pt[0:nk, 0:4 * nq], ktb[g * 64:g * 64 + 64, kc0:kc0 + nk], qtc[g * 64:g * 64 + 64, :, c0:c0 + nq], True, True, [KTB, QTC], [PT])
                                return pt, PT
                            pend = emit_qk(steps[0])
                            for si, (kt, kc0, nk, g) in enumerate(steps):
                                pt, PT = pend
                                if si + 1 < len(steps):
                                    pend = emit_qk(steps[si + 1])
                                i = pr[0] % 3
                                pr[0] += 1
                                act(pP[i][0:nk, :, 0:nq], pt[0:nk, 0:4 * nq].rearrange("p (j t) -> p j t", j=4), AF.Exp, [PT], [PP[i]], scale=0.125)
                                tt(pP[i][0:nk, :, 0:nq], pP[i][0:nk, :, 0:nq], maskT[0:nk, kt, 0:nq].unsqueeze(1).to_broadcast([nk, 4, nq]), ALU.mult, [PP[i], MT], [PP[i]])
                                for jj in range(4):
                                    mm(psum[6 + g][0:nq, jj * 65:(jj + 1) * 65], pP[i][0:nk, jj, 0:nq], vaug[0:nk, kt, g, :],
                                       (kt == 0 and jj == 0), (kt == len(ktiles) - 1), [PP[i], VAUG], [PS[6 + g]], skip_group_check=True)
                            for g in range(2):
                                act(osb[0:nq, 4 * g:4 * g + 4, :], psum[6 + g][0:nq, 0:260].rearrange("p (j d) -> p j d", d=65), AF.Copy, [PS[6 + g]], [OSB])
                            kb.op("dve", lambda e: e.reciprocal(out=orc[0:nq, :], in_=osb[0:nq, :, 64]), [OSB], [ORC])
                            tt(onb[0:nq, :].rearrange("p (h d) -> p h d", d=64), osb[0:nq, :, 0:64], orc[0:nq, :].unsqueeze(2).to_broadcast([nq, 8, 64]),
                               ALU.mult, [OSB, ORC], [ONB])
                            pt, PT = next_ps()
                            for j in range(4):
                                tr(pb(pt)[:, j * 128:j * 128 + nq], onb[0:nq, j * 128:(j + 1) * 128], identb[0:nq, 0:nq], [ONB, IDENTB], [PT])
                            cp(oct_[:, :, c0:c0 + nq], pb(pt)[:, 0:512].rearrange("p (j t) -> p j t", t=128)[:, :, 0:nq], [PT], [OCT])
                    kb.barrier()

                with kb.phase() as p3:
                    maskT = sb("smT", [128, 17, 128], BF16, stack=p3); MT = Buf("smT")
                    rr = [0]
                    pidxl = sb("pidxl", [128, 256], I32, stack=p3); PIDXL = Buf("pidxl")
                    ts(pidxl[:, :], pidx[:, :], l * N_PHYS * 128, None, ALU.add, None, [PIDX], [PIDXL])
                    cache_ki_f = cache_ki.rearrange("l r c -> (l r) c")
                    cache_k_f = cache_k.rearrange("l r c -> (l r) c")
                    cache_v_f = cache_v.rearrange("l r c -> (l r) c")
                    with kb.phase() as p3a:
                        sc = sb("ssc", [128, 2176], stack=p3a); SC = Buf("ssc")
                        m01 = sb("sm01", [128, 2176], BF16, stack=p3a); M01 = Buf("sm01")
                        tkb = tk_state(p3a, "s")
                        TKB = tkb[-1]
                        rl = [sb(f"srl{i}", [128, 512], stack=p3a) for i in range(2)]
                        RL = [Buf(f"srl{i}") for i in range(2)]
                        kikb = sb("skikb", [64, 16, 256], BF16, stack=p3a); KIKB = Buf("skikb")
                        qiz = [sb(f"sqiz{i}", [64, 16, 128], BF16, stack=p3a) for i in range(1)]
                        QIZ = [Buf(f"sqiz{i}") for i in range(1)]
                        kst = [sb(f"skst{i}", [128, 2, 64], stack=p3a) for i in range(4)]
                        KST = [Buf(f"skst{i}") for i in range(4)]
                        sr = 0
                        for kbk in range(8):
                            for s2 in range(0, 16, 2):
                                pt, PT = next_ps()
                                for sq_ in range(2):
                                    s = s2 + sq_
                                    i = sr % 4
                                    sr += 1
                                    for pg in range(2):
                                        col = s * 16 + kbk * 2 + pg
                                        kb.dma("pool", kst[i][:, pg, :], cache_ki_f, [PIDXL], [KST[i]], KST[i], group=(pg > 0),
                                               indirect=bass.IndirectOffsetOnAxis(ap=pidxl[:, col:col + 1], axis=0))
                                    for pg in range(2):
                                        tr(pt[0:64, (sq_ * 2 + pg) * 128:(sq_ * 2 + pg + 1) * 128], kst[i][:, pg, :], ident[:, :], [KST[i], IDENT], [PT])
                                act(kikb[:, s2:s2 + 2, :], pt[0:64, 0:512].rearrange("p (s k) -> p s k", k=256), AF.Copy, [PT], [KIKB])
                            for h in range(4):
                                i = 0
                                tt(qiz[i][:, :, :], qis[:, h, :].unsqueeze(1).to_broadcast([64, 16, 128]), seqsel[0:64, :, :], ALU.mult, [QIS, SEQSEL], [QIZ[i]])
                                pt, PT = next_ps()
                                for s in range(16):
                                    mm(pt[:, 0:256], qiz[i][:, s, :], kikb[:, s, :], s == 0, s == 15, [QIZ[i], KIKB], [PT])
                                j = rr[0] % 2
                                rr[0] += 1
                                act(rl[j][:, 0:256], pt[:, 0:256], AF.Relu, [PT], [RL[j]])
                                if h == 0:
                                    ts(sc[:, kbk * 256:(kbk + 1) * 256], rl[j][:, 0:256], cw[:, 17, 0:1], None, ALU.mult, None, [RL[j], CW], [SC])
                                else:
                                    stt(sc[:, kbk * 256:(kbk + 1) * 256], rl[j][:, 0:256], cw[:, 17, h:h + 1], sc[:, kbk * 256:(kbk + 1) * 256],
                                        ALU.mult, ALU.add, [RL[j], CW, SC], [SC])
                        for h in range(4):
                            pt, PT = next_ps()
                            mm(pt[:, 0:128], qis[:, h, :], kib[0:64, TP:NT], True, True, [QIS, KIB], [PT])
                            j = rr[0] % 2
                            rr[0] += 1
                            act(rl[j][:, 0:128], pt[:, 0:128], AF.Relu, [PT], [RL[j]])
                            if h == 0:
                                ts(sc[:, 2048:2176], rl[j][:, 0:128], cw[:, 17, 0:1], None, ALU.mult, None, [RL[j], CW], [SC])
                            else:
                                stt(sc[:, 2048:2176], rl[j][:, 0:128], cw[:, 17, h:h + 1], sc[:, 2048:2176], ALU.mult, ALU.add, [RL[j], CW, SC], [SC])
                        kb.op("dve", lambda e: e.reduce_max(out=tkb[0][:, :], in_=sc[:, :], axis=AX.X, apply_absolute_value=True), [SC], [TKB])
                        tt(sc[:, 2048:2176], sc[:, 2048:2176], blkbias, ALU.add, [SC, MASKS], [SC])
                        topk_group([(128, 2176, sc, SC, m01, M01, tkb)])
                        for g0 in range(0, 17, 4):
                            pt, PT = next_ps()
                            ng = min(4, 17 - g0)
                            for q in range(ng):
                                tr(pb(pt)[:, q * 128:(q + 1) * 128], m01[:, (g0 + q) * 128:(g0 + q + 1) * 128], identb[:, :], [M01, IDENTB], [PT])
                            cp(maskT[:, g0:g0 + ng, :], pb(pt)[:, 0:128 * ng].rearrange("p (q t) -> p q t", t=128), [PT], [MT])
                        kb.barrier()
                    with kb.phase() as p3b:
                        kst = [sb(f"sk4{i}", [128, 4, 128], stack=p3b) for i in range(3)]
                        KST = [Buf(f"sk4{i}") for i in range(3)]
                        vst = [sb(f"sv4{i}", [128, 4, 128], stack=p3b) for i in range(3)]
                        VST = [Buf(f"sv4{i}") for i in range(3)]
                        kts = [sb(f"skts{i}", [128, 2048], BF16, stack=p3b) for i in range(2)]
                        KTS = [Buf(f"skts{i}") for i in range(2)]
                        vas = [sb(f"svas{i}", [128, 16, 2, 65], BF16, stack=p3b) for i in range(2)]
                        VAS = [Buf(f"svas{i}") for i in range(2)]
                        pP = [sb(f"sP{i}", [128, 4, 8], BF16, stack=p3b) for i in range(3)]
                        PP = [Buf(f"sP{i}") for i in range(3)]
                        osb = sb("sosb", [8, 8, 65], stack=p3b); OSB = Buf("sosb")
                        orc = sb("sorc", [8, 8], stack=p3b); ORC = Buf("sorc")
                        onb = sb("sonb", [8, 512], BF16, stack=p3b); ONB = Buf("sonb")
                        for i in range(2):
                            ms(vas[i][:, :, :, 64:65], 1.0, [VAS[i]])
                        sr = 0
                        pr = 0
                        for s in range(16):
                            b = s % 2
                            for g4 in range(4):
                                i = sr % 3
                                sr += 1
                                for pg in range(4):
                                    col = s * 16 + g4 * 4 + pg
                                    kb.dma("pool", kst[i][:, pg, :], cache_k_f, [PIDXL], [KST[i]], KST[i], group=(pg > 0),
                                           indirect=bass.IndirectOffsetOnAxis(ap=pidxl[:, col:col + 1], axis=0))
                                    kb.dma("pool", vst[i][:, pg, :], cache_v_f, [PIDXL], [VST[i]], VST[i], group=(pg > 0),
                                           indirect=bass.IndirectOffsetOnAxis(ap=pidxl[:, col:col + 1], axis=0))
                                pt, PT = next_ps()
                                for pg in range(4):
                                    tr(pt[:, pg * 128:(pg + 1) * 128], kst[i][:, pg, :], ident[:, :], [KST[i], IDENT], [PT])
                                act(kts[b][:, g4 * 512:(g4 + 1) * 512], pt[:, :], AF.Copy, [PT], [KTS[b]])
                                cp(vas[b][:, g4 * 4:g4 * 4 + 4, :, 0:64], vst[i][:, :, :].rearrange("p q (g d) -> p q g d", d=64), [VST[i]], [VAS[b]], eng="pool")
                            q0 = TP + 8 * s
                            steps = [(kt, g) for kt in range(17) for g in range(2)]

                            def emit_qk(step):
                                kt, g = step
                                pt, PT = next_ps()
                                if kt < 16:
                                    mm(pt[:, 0:32], kts[b][g * 64:g * 64 + 64, kt * 128:(kt + 1) * 128], qtc[g * 64:g * 64 + 64, :, q0:q0 + 8], True, True, [KTS[b], QTC], [PT])
                                else:
                                    mm(pt[:, 0:32], ktb[g * 64:g * 64 + 64, TP:NT], qtc[g * 64:g * 64 + 64, :, q0:q0 + 8], True, True, [KTB, QTC], [PT])
                                return pt, PT
                            pend = emit_qk(steps[0])
                            for si, (kt, g) in enumerate(steps):
                                pt, PT = pend
                                if si + 1 < len(steps):
                                    pend = emit_qk(steps[si + 1])
                                i = pr % 3
                                pr += 1
                                act(pP[i][:, :, :], pt[:, 0:32].rearrange("p (j t) -> p j t", j=4), AF.Exp, [PT], [PP[i]], scale=0.125)
                                tt(pP[i][:, :, :], pP[i][:, :, :], maskT[:, kt, 8 * s:8 * s + 8].unsqueeze(1).to_broadcast([128, 4, 8]), ALU.mult, [PP[i], MT], [PP[i]])
                                for jj in range(4):
                                    rhs = vas[b][:, kt, g, :] if kt < 16 else vaug[:, 17, g, :]
                                    mm(psum[6 + g][0:8, jj * 65:(jj + 1) * 65], pP[i][:, jj, :], rhs, (kt == 0 and jj == 0), (kt == 16),
                                       [PP[i], VAS[b], VAUG], [PS[6 + g]], skip_group_check=True)
                            for g in range(2):
                                act(osb[:, 4 * g:4 * g + 4, :], psum[6 + g][0:8, 0:260].rearrange("p (j d) -> p j d", d=65), AF.Copy, [PS[6 + g]], [OSB])
                            kb.op("dve", lambda e: e.reciprocal(out=orc[:, :], in_=osb[:, :, 64]), [OSB], [ORC])
                            tt(onb[:, :].rearrange("p (h d) -> p h d", d=64), osb[:, :, 0:64], orc[:, :].unsqueeze(2).to_broadcast([8, 8, 64]),
                               ALU.mult, [OSB, ORC], [ONB])
                            pt, PT = next_ps()
                            for j in range(4):
                                tr(pb(pt)[:, j * 128:j * 128 + 8], onb[:, j * 128:(j + 1) * 128], identb[0:8, 0:8], [ONB, IDENTB], [PT])
                            cp(oct_[:, :, q0:q0 + 8], pb(pt)[:, 0:512].rearrange("p (j t) -> p j t", t=128)[:, :, 0:8], [PT], [OCT])
                        kb.barrier()

                with kb.phase() as p4:
                    zs = [sb(f"czs{i}", [128, 448], stack=p4) for i in range(2)]
                    ZS = [Buf(f"czs{i}") for i in range(2)]
                    for j in range(4):
                        def ev_cz(pt, PT, bi, t0, n, j=j):
                            i = bi % 2
                            act(zs[i][:, 0:n], pt[:, 0:n], AF.Silu, [PT, BIASFM], [ZS[i]], bias=bias_col(l, f"cz{j}"), scale=1.0)
                            tt(oct_[:, j, t0:t0 + n], oct_[:, j, t0:t0 + n], zs[i][:, 0:n], ALU.mult, [OCT, ZS[i]], [OCT])
                        fm_unit(l, f"cz{j}", ev_cz)
                    branch_out(l, 2, oct_, OCT, w_pc[l])
                    kb.barrier()

        for l in range(n_layers):
            with kb.phase() as ph:
                rmsnorm_block(ph, l, normg, "n")
                kb.barrier()

            if debug in (None, "a"):
                phase_A(l)
            if debug in (None, "b"):
                phase_B(l)
            if debug in (None, "c"):
                phase_C(l)

            if debug:
                with kb.phase() as ph:
                    dsb = sb("dsb", [128, NT], stack=ph); DSB = Buf("dsb")
                    for dc in range(8):
                        cp(dsb[:, :], merged[:, dc, :], MG, [DSB])
                        kb.dma("sp", dbg[:, dc * NT:(dc + 1) * NT], dsb[:, :], [DSB], [], DSB)
                    kb.barrier()
                continue

            with kb.phase() as ph:
                hr = [sb(f"hr{i}", [128, 448], stack=ph) for i in range(3)]
                HR = [Buf(f"hr{i}") for i in range(3)]
                rr = 0
                for dc2 in range(8):
                    wt, WT = load_w(w_out[l], 8, [(dc2 * 128, 128, 0)])
                    for bi, (t0, n) in enumerate(TB):
                        i = rr % 3
                        rr += 1
                        kb.dma("sp", hr[i][:, 0:n], hT_d[dc2, :, t0:t0 + n], [HTD[dc2][bi]], [HR[i]], HR[i])
                        pt, PT = next_ps()
                        for c in range(8):
                            mm(pt[:, 0:n], wt[:, c, :], merged[:, c, t0:t0 + n], c == 0, c == 7, [WT, MG[bi]], [PT])
                        tt(hr[i][:, 0:n], hr[i][:, 0:n], pt[:, 0:n], ALU.add, [HR[i], PT], [HR[i]])
                        kb.dma("sp", hT_d[dc2, :, t0:t0 + n], hr[i][:, 0:n], [HR[i]], [HTD[dc2][bi]], HR[i])
                kb.barrier()

        if not debug:
            with kb.phase() as ph:
                hb = [sb(f"fhb{i}", [128, 8, 128], stack=ph) for i in range(2)]
                HB = [Buf(f"fhb{i}") for i in range(2)]
                sq = sb("fsq", [128, 8, 128], stack=ph); SQ = Buf("fsq")
                rs = sb("frs", [128, 128], stack=ph); RS = Buf("frs")
                rs2 = sb("frs2", [128, 128], stack=ph); RS2 = Buf("frs2")
                yo = [sb(f"yo{i}", [128, D], stack=ph) for i in range(2)]
                YO = [Buf(f"yo{i}") for i in range(2)]
                for ti, (c0, n, r0) in enumerate(all_tiles[1:]):
                    i = ti % 2
                    rd = [HTD[c][b] for c in range(8) for b in blocks_of(c0, n)]
                    kb.dma("sp", hb[i][:, :, :], hT_d[:, :, c0:c0 + n].rearrange("c p t -> p c t"), rd, [HB[i]], HB[i])
                    act(sq[:, :, :], hb[i][:, :, :], AF.Square, [HB[i]], [SQ])
                    pt, PT = next_ps()
                    for dc in range(8):
                        mm(pt[:, 0:n], ones_f[:, :], sq[:, dc, :], dc == 0, dc == 7, [ONESF, SQ], [PT])
                    act(rs[:, :], pt[:, 0:n], AF.Sqrt, [PT], [RS], bias=EPS, scale=1.0 / D)
                    kb.op("dve", lambda e: e.reciprocal(out=rs2[:, :], in_=rs[:, :]), [RS], [RS2])
                    for dc in range(8):
                        stt(hb[i][:, dc, :], hb[i][:, dc, :], fng[:, dc:dc + 1], rs2[:, :], ALU.mult, ALU.mult,
                            [HB[i], FNG, RS2], [HB[i]])
                    for hf in range(2):
                        pt, PT = next_ps()
                        for q in range(4):
                            tr(pt[:, q * 128:(q + 1) * 128], hb[i][:, hf * 4 + q, :], ident[:, :], [HB[i], IDENT], [PT])
                        act(yo[i][:, hf * 512:(hf + 1) * 512], pt[:, :], AF.Copy, [PT], [YO[i]])
                    dst = y_p[r0 - 16:r0 - 16 + n, :] if r0 is not None else y_s[:, :]
                    kb.dma("sp", dst, yo[i][:, :], [YO[i]], [], YO[i])
                kb.barrier()

        kb.finish()
        print(f"[kernel] emitted ~{kb.n_ins} instructions, {kb.nsem} semaphores")
    return nc


def _constants():
    ident = np.eye(128, dtype=np.float32)
    rmat = np.zeros((128, 128), np.float32)
    for base in (0, 64):
        for d in range(8):
            rmat[base + d + 8, base + d] = -1.0
            rmat[base + d, base + d + 8] = 1.0
    pos = np.concatenate([np.arange(TP), 2048 + (np.arange(NSM) % 8)]).astype(np.float32)
    inv = (np.float32(ROPE_THETA) ** (-np.arange(8, dtype=np.float32) * np.float32(2.0) / np.float32(16))).astype(np.float32)
    ang = pos[None, :] * inv[:, None]
    cos = np.ones((128, NT), np.float32)
    sin = np.zeros((128, NT), np.float32)
    for base in (0, 64):
        cos[base:base + 8] = np.cos(ang); cos[base + 8:base + 16] = np.cos(ang)
        sin[base:base + 8] = np.sin(ang); sin[base + 8:base + 16] = np.sin(ang)
    a = np.arange(128)
    tri_st = (a[:, None] <= a[None, :]).astype(np.float32)
    same = (a[:, None] // 8 == a[None, :] // 8)
    blk_st = (same & (a[:, None] <= a[None, :])).astype(np.float32)
    tribias = np.where(a[None, :] <= a[:, None], 0.0, NEG).astype(np.float32)
    blkbias = np.where(same & (a[None, :] <= a[:, None]), 0.0, NEG).astype(np.float32)
    masks = np.concatenate([tri_st, blk_st, tribias, blkbias], axis=1)
    seqsel = np.zeros((128, 16, 128), np.float32)
    for s in range(16):
        seqsel[:, s, 8 * s:8 * s + 8] = 1.0
    rowsel = (a[:, None] // 8 == np.arange(16)[None, :]).astype(np.float32)
    return ident, rmat, cos, sin, masks, seqsel.reshape(128, -1), rowsel


_NC_CACHE = {}
DEBUG = None


def kernel(x_prompt, x_sample, cache_k, cache_v, cache_kidx, state_hgrn, state_conv, page_table, meta_tokens, norm_g,
           w_in, b_in, lb_logits, hgrn_norm_g, conv_w, conv_b, conv_ln_g, conv_ln_b, conv_pw, w_pa, w_pb, w_pc, w_out,
           final_norm_g):
    f = lambda a: np.ascontiguousarray(np.asarray(a))
    (x_prompt, x_sample, cache_k, cache_v, cache_kidx, state_hgrn, state_conv, page_table, meta_tokens, norm_g, w_in, b_in,
     lb_logits, hgrn_norm_g, conv_w, conv_b, conv_ln_g, conv_ln_b, conv_pw, w_pa, w_pb, w_pc, w_out, final_norm_g) = map(f, (
        x_prompt, x_sample, cache_k, cache_v, cache_kidx, state_hgrn, state_conv, page_table, meta_tokens, norm_g, w_in, b_in,
        lb_logits, hgrn_norm_g, conv_w, conv_b, conv_ln_g, conv_ln_b, conv_pw, w_pa, w_pb, w_pc, w_out, final_norm_g))
    key = ("nc", DEBUG)
    if key not in _NC_CACHE:
        _NC_CACHE[key] = build_program(n_layers=1 if DEBUG else DEPTH, debug=DEBUG)
    nc = _NC_CACHE[key]
    ident, rmat, cos, sin, masks, seqsel, rowsel = _constants()
    normg_fm = np.ascontiguousarray(norm_g.reshape(DEPTH, 8, 128).transpose(2, 0, 1).reshape(128, DEPTH * 8))
    fng_fm = np.ascontiguousarray(final_norm_g.reshape(8, 128).T)
    bias_fm = np.zeros((128, DEPTH, len(FM_NAMES)), np.float32)
    for l in range(DEPTH):
        for j, n in enumerate(FM_NAMES):
            for (col, width, base) in FM_UNITS[n]:
                bias_fm[base:base + width, l, j] = b_in[l, col:col + width]
    bias_fm = bias_fm.reshape(128, -1)
    btok = np.ascontiguousarray(b_in.reshape(1, -1))
    convw_fm = np.ascontiguousarray(conv_w.reshape(DEPTH, 31, 4, 128).transpose(3, 0, 2, 1).reshape(128, DEPTH * 4 * 31))
    cvec = np.stack([conv_b, conv_ln_g, conv_ln_b], axis=1)
    cvec_fm = np.ascontiguousarray(cvec.reshape(DEPTH, 3, 4, 128).transpose(3, 0, 1, 2).reshape(128, DEPTH * 12))
    lbl_fm = np.ascontiguousarray(lb_logits.reshape(DEPTH, 4, 128).transpose(2, 1, 0).reshape(128, 16))
    ck = cache_k.reshape(DEPTH, N_PHYS * 128, 128)
    cv = cache_v.reshape(DEPTH, N_PHYS * 128, 128)
    cki = cache_kidx.reshape(DEPTH, N_PHYS * 128, 64)
    in_maps = []
    for c in range(8):
        in_maps.append({
            "xp": x_prompt[c], "xs": x_sample[16 * c:16 * c + 16].reshape(NSM, D), "meta": meta_tokens,
            "w_in": w_in, "conv_pw": conv_pw, "w_pa": w_pa, "w_pb": w_pb, "w_pc": w_pc, "w_out": w_out,
            "cache_k": ck, "cache_v": cv, "cache_ki": cki,
            "st_h": np.ascontiguousarray(state_hgrn[:, 16 * c:16 * c + 16]),
            "st_c": np.ascontiguousarray(state_conv[:, 16 * c:16 * c + 16]),
            "ptab": np.ascontiguousarray(page_table[16 * c:16 * c + 16].reshape(1, 256)).astype(np.int32),
            "gng": hgrn_norm_g,
            "c_ident": ident, "c_rmat": rmat, "c_cos": cos, "c_sin": sin, "c_normg": normg_fm, "c_fng": fng_fm,
            "c_bias": bias_fm, "c_btok": btok, "c_masks": masks, "c_seqsel": seqsel, "c_rowsel": rowsel,
            "c_convw": convw_fm, "c_cvec": cvec_fm, "c_lbl": lbl_fm,
        })
    if DEBUG:
        in_maps = in_maps[:1]
    res = run_bass_kernel_spmd(nc, in_maps, core_ids=list(range(len(in_maps))))
    R = res.results
    if DEBUG:
        return [np.asarray(r["dbg"]) for r in R]
    g = lambda k: np.stack([np.asarray(r[k]) for r in R])
    y_prompt = g("y_p")
    y_sample = g("y_s").reshape(128, 8, D)
    pk = g("p_k").transpose(1, 0, 2, 3).reshape(DEPTH, 8, TP, 2, 64)
    pv = g("p_v").transpose(1, 0, 2, 3).reshape(DEPTH, 8, TP, 2, 64)
    pki = g("p_ki").transpose(1, 0, 2, 3).reshape(DEPTH, 8, TP, 64)
    ph_ = g("p_h").transpose(1, 0, 2, 3, 4)
    pc_ = g("p_c").transpose(1, 0, 2, 3)
    sk = g("s_k").transpose(1, 0, 2, 3).reshape(DEPTH, 128, 8, 2, 64)
    sv = g("s_v").transpose(1, 0, 2, 3).reshape(DEPTH, 128, 8, 2, 64)
    ski = g("s_ki").transpose(1, 0, 2, 3).reshape(DEPTH, 128, 8, 64)
    sh = g("s_h").transpose(1, 0, 2, 3, 4, 5).reshape(DEPTH, 128, 4, 128, 128)
    sc_ = g("s_c").transpose(1, 0, 2, 3, 4).reshape(DEPTH, 128, 30, 512)
    c = np.ascontiguousarray
    return (c(y_prompt), c(y_sample), c(pk), c(pv), c(pki), c(ph_), c(pc_), c(sk), c(sv), c(ski), c(sh), c(sc_))
```

```python
import numpy as np
from contextlib import ExitStack, contextmanager
import concourse.bass as bass
import concourse.mybir as mybir
from concourse.bass_utils import run_bass_kernel_spmd

F32 = mybir.dt.float32
BF16 = mybir.dt.bfloat16
I32 = mybir.dt.int32
ALU = mybir.AluOpType
AF = mybir.ActivationFunctionType
AX = mybir.AxisListType

D = 1024
DEPTH = 4
TP = 2064
NSM = 128
NT = TP + NSM
TB = [(0, 448), (448, 448), (896, 448), (1344, 448), (1792, 400)]
N_IN = 8260
EPS = 1e-6
N_PHYS = 2560
ROPE_THETA = 500000.0
SEM_LIMIT = 30000

O_AQ, O_AF, O_AI, O_AZ = 0, 512, 1024, 1536
O_BA, O_BG, O_BZ = 2048, 2560, 3072
O_CQ, O_CK, O_CV, O_CQI, O_CKI, O_CW, O_CZ, O_GATE = 3584, 4096, 4224, 4352, 4608, 4672, 4676, 5188

FM_UNITS = {}
for h in range(4):
    FM_UNITS[f"aq{h}"] = [(O_AQ + 128 * h, 128, 0)]
    FM_UNITS[f"af{h}"] = [(O_AF + 128 * h, 128, 0)]
    FM_UNITS[f"az{h}"] = [(O_AZ + 128 * h, 128, 0)]
    FM_UNITS[f"ba{h}"] = [(O_BA + 128 * h, 128, 0)]
    FM_UNITS[f"bg{h}"] = [(O_BG + 128 * h, 128, 0)]
    FM_UNITS[f"bz{h}"] = [(O_BZ + 128 * h, 128, 0)]
    FM_UNITS[f"cq{h}"] = [(O_CQ + 64 * h, 64, 0), (O_CQ + 64 * (4 + h), 64, 64)]
    FM_UNITS[f"cz{h}"] = [(O_CZ + 128 * h, 128, 0)]
FM_UNITS["ck"] = [(O_CK, 128, 0)]
FM_UNITS["cqi0"] = [(O_CQI, 128, 0)]
FM_UNITS["cqi1"] = [(O_CQI + 128, 128, 0)]
FM_UNITS["cki"] = [(O_CKI, 64, 0), (O_CKI, 64, 64)]
for h in range(4):
    FM_UNITS[f"cqis{h}"] = [(O_CQI + 64 * h, 64, 0)]
for b in range(3):
    for j in range(8):
        FM_UNITS[f"g{b}_{j}"] = [(O_GATE + 1024 * b + 128 * j, 128, 0)]
FM_NAMES = list(FM_UNITS.keys())
FM_IDX = {n: i for i, n in enumerate(FM_NAMES)}


class Buf:
    __slots__ = ("name", "wr", "rd", "dsem", "dcnt")

    def __init__(self, name):
        self.name = name
        self.wr = []
        self.rd = {}
        self.dsem = None
        self.dcnt = 0


class KB:
    def __init__(self, nc, es):
        self.nc = nc
        self.es = es
        self.engs = {"pe": nc.tensor, "act": nc.scalar, "dve": nc.vector, "pool": nc.gpsimd, "sp": nc.sync}
        self.sem = {}
        self.cnt = {}
        self.seen = {k: {} for k in self.engs}
        self.nsem = 0
        self.dma_bufs = []
        self.n_ins = 0
        self.old_ev = {}
        self.free_sems = []
        for k in ("pe", "act", "dve", "pool"):
            self._rot(k)

    def new_sem(self, name):
        self.nsem += 1
        return self.es.enter_context(self.nc.semaphore(f"{name}_{self.nsem}"))

    def _rot(self, k):
        if k in self.sem:
            self.old_ev[k] = (self.sem[k], self.cnt[k])
        self.sem[k] = self.new_sem("e" + k)
        self.cnt[k] = 0

    def _wait(self, ek, ev):
        sem, val = ev
        key = id(sem)
        if ek == "pe" and sem is self.sem["pe"]:
            return
        if self.seen[ek].get(key, 0) >= val:
            return
        self.engs[ek].wait_ge(sem, val)
        self.seen[ek][key] = val
        self.n_ins += 1

    def _deps(self, ek, reads, writes):
        for b in reads:
            for ev in b.wr:
                self._wait(ek, ev)
        for b in writes:
            for ev in b.wr:
                self._wait(ek, ev)
            for ev in b.rd.values():
                self._wait(ek, ev)

    def _record(self, ev, reads, writes, accum=False):
        for b in writes:
            if accum:
                b.wr = [e for e in b.wr if e[0] is not ev[0]] + [ev]
            else:
                b.wr = [ev]
                b.rd = {}
        for b in reads:
            if b in writes:
                continue
            key = id(ev[0])
            old = b.rd.get(key)
            if old is None or old[1] < ev[1]:
                b.rd[key] = ev

    def op(self, ek, fn, reads=(), writes=(), accum=False):
        self._deps(ek, reads, writes)
        ins = fn(self.engs[ek])
        self.cnt[ek] += 1
        ins.then_inc(self.sem[ek], 1)
        ev = (self.sem[ek], self.cnt[ek])
        self._record(ev, reads, writes, accum=accum)
        self.n_ins += 1
        if self.cnt[ek] >= SEM_LIMIT:
            self._rot(ek)

    def dma(self, qk, out_ap, in_ap, reads, writes, sb, group=False, indirect=None, accum=False):
        if sb.dsem is None:
            self.pin(sb)
        self._deps(qk, reads, writes)
        if not group and sb.dcnt > 0:
            self._wait(qk, (sb.dsem, sb.dcnt))
        if indirect is not None:
            ins = self.engs[qk].indirect_dma_start(out=out_ap, out_offset=None, in_=in_ap, in_offset=indirect)
        else:
            ins = self.engs[qk].dma_start(out=out_ap, in_=in_ap)
        ins.then_inc(sb.dsem, 16)
        sb.dcnt += 16
        ev = (sb.dsem, sb.dcnt)
        self._record(ev, reads, writes, accum=accum)
        self.n_ins += 1

    def pin(self, b):
        if b.dsem is None:
            if self.free_sems:
                b.dsem, b.dcnt = self.free_sems.pop()
            else:
                b.dsem, b.dcnt = self.new_sem("d"), 0
            self.dma_bufs.append(b)

    @contextmanager
    def phase(self):
        start = len(self.dma_bufs)
        with ExitStack() as ph:
            yield ph
            self.barrier()
            for b in self.dma_bufs[start:]:
                self.free_sems.append((b.dsem, b.dcnt))
                b.dsem, b.dcnt = None, 0
            del self.dma_bufs[start:]

    def barrier(self):
        evs = [(self.sem[k], self.cnt[k]) if self.cnt[k] > 0 else self.old_ev.get(k) for k in ("pe", "act", "dve", "pool")]
        evs = [e for e in evs if e is not None]
        evs += [(b.dsem, b.dcnt) for b in self.dma_bufs if b.dcnt > 0]
        for ek in self.engs:
            for ev in evs:
                if ek in self.sem and ev[0] is self.sem[ek]:
                    continue
                self._wait(ek, ev)

    def finish(self):
        for b in self.dma_bufs:
            if b.dcnt > 0:
                self._wait("sp", (b.dsem, b.dcnt))


NIT = 18
NEG = -1.0e30
IDX_SCALE = (4 * 64) ** -0.5


def build_program(n_layers=DEPTH, debug=None):
    nc = bass.Bass("TRN2", target_bir_lowering=False)

    def din(name, shape, dt=F32):
        return nc.dram_tensor(name, list(shape), dt, kind="ExternalInput").ap()

    def dout(name, shape, dt=F32):
        return nc.dram_tensor(name, list(shape), dt, kind="ExternalOutput").ap()

    xp = din("xp", [2048, D]); xs = din("xs", [NSM, D]); meta = din("meta", [16, D])
    w_in = din("w_in", [DEPTH, D, N_IN])
    conv_pw = din("conv_pw", [DEPTH, 512, 512])
    w_pa = din("w_pa", [DEPTH, 512, D]); w_pb = din("w_pb", [DEPTH, 512, D]); w_pc = din("w_pc", [DEPTH, 512, D])
    w_out = din("w_out", [DEPTH, D, D])
    cache_k = din("cache_k", [DEPTH, N_PHYS * 128, 128]); cache_v = din("cache_v", [DEPTH, N_PHYS * 128, 128])
    cache_ki = din("cache_ki", [DEPTH, N_PHYS * 128, 64])
    st_h = din("st_h", [DEPTH, 16, 4, 128, 128]); st_c = din("st_c", [DEPTH, 16, 30, 512])
    ptab = din("ptab", [1, 256], I32)
    gng_d = din("gng", [DEPTH, 128])
    c_ident = din("c_ident", [128, 128]); c_rmat = din("c_rmat", [128, 128])
    c_cos = din("c_cos", [128, NT]); c_sin = din("c_sin", [128, NT])
    c_normg = din("c_normg", [128, DEPTH * 8]); c_fng = din("c_fng", [128, 8])
    c_bias = din("c_bias", [128, DEPTH * len(FM_NAMES)])
    c_btok = din("c_btok", [1, DEPTH * N_IN])
    c_masks = din("c_masks", [128, 4 * 128])
    c_seqsel = din("c_seqsel", [128, 16 * 128]); c_rowsel = din("c_rowsel", [128, 16])
    c_convw = din("c_convw", [128, DEPTH * 4 * 31]); c_cvec = din("c_cvec", [128, DEPTH * 12])
    c_lbl = din("c_lbl", [128, 16])

    y_p = dout("y_p", [2048, D]); y_s = dout("y_s", [NSM, D])
    p_k = dout("p_k", [DEPTH, TP, 128]); p_v = dout("p_v", [DEPTH, TP, 128]); p_ki = dout("p_ki", [DEPTH, TP, 64])
    p_h = dout("p_h", [DEPTH, 4, 128, 128]); p_c = dout("p_c", [DEPTH, 30, 512])
    s_k = dout("s_k", [DEPTH, NSM, 128]); s_v = dout("s_v", [DEPTH, NSM, 128]); s_ki = dout("s_ki", [DEPTH, NSM, 64])
    s_h = dout("s_h", [DEPTH, 16, 4, 128, 128]); s_c = dout("s_c", [DEPTH, 16, 30, 512])
    dbg = dout("dbg", [128, 8 * NT]) if debug else None

    hT_d = nc.dram_tensor("hT_d", [8, 128, NT], F32, kind="Internal").ap()
    HTD = [[Buf(f"htd{c}_{b}") for b in range(len(TB))] for c in range(8)]

    def blocks_of(c0, n):
        return [bi for bi, (t0, nn) in enumerate(TB) if t0 < c0 + n and c0 < t0 + nn]

    with ExitStack() as es:
        kb = KB(nc, es)

        sb_n = [0]

        def sb(name, shape, dt=F32, stack=None):
            sb_n[0] += 1
            return (stack or es).enter_context(nc.sbuf_tensor(f"{name}_{sb_n[0]}", list(shape), dt))

        def mm(out, lhsT, rhs, start, stop, reads, writes, **kw):
            kb.op("pe", lambda e: e.matmul(out, lhsT=lhsT, rhs=rhs, start=start, stop=stop, **kw), reads, writes)

        def tr(out, in_, idn, reads, writes):
            kb.op("pe", lambda e: e.transpose(out, in_, idn), reads, writes)

        def act(out, in_, func, reads, writes, **kw):
            kb.op("act", lambda e: e.activation(out=out, in_=in_, func=func, **kw), reads, writes)

        def tt(out, a, b, op, reads, writes, eng="dve"):
            kb.op(eng, lambda e: e.tensor_tensor(out=out, in0=a, in1=b, op=op), reads, writes)

        def ts(out, a, s1, s2, op0, op1, reads, writes, eng="dve", **kw):
            if s2 is None:
                kb.op(eng, lambda e: e.tensor_scalar(out=out, in0=a, scalar1=s1, scalar2=None, op0=op0, **kw), reads, writes)
            else:
                kb.op(eng, lambda e: e.tensor_scalar(out=out, in0=a, scalar1=s1, scalar2=s2, op0=op0, op1=op1, **kw), reads, writes)

        def stt(out, a, scalar, b, op0, op1, reads, writes):
            kb.op("dve", lambda e: e.scalar_tensor_tensor(out=out, in0=a, scalar=scalar, in1=b, op0=op0, op1=op1), reads, writes)

        def cp(out, in_, reads, writes, eng="dve"):
            kb.op(eng, lambda e: e.tensor_copy(out=out, in_=in_), reads, writes)

        def ms(ap, val, writes, eng="dve"):
            kb.op(eng, lambda e: e.memset(ap, val), [], writes)

        def const(name, shape, src, dt=F32):
            t = sb(name, shape, dt); B_ = Buf(name)
            kb.dma("sp", t[:], src, [], [B_], B_)
            return t, B_

        ident, IDENT = const("ident", [128, 128], c_ident)
        rmat, RMAT = const("rmat", [128, 128], c_rmat)
        normg, NORMG = const("normg", [128, DEPTH * 8], c_normg)
        fng, FNG = const("fng", [128, 8], c_fng)
        biasfm, BIASFM = const("biasfm", [128, DEPTH * len(FM_NAMES)], c_bias)
        masks, MASKS = const("masks", [128, 512], c_masks)
        rowsel, ROWSEL = const("rowsel", [128, 16], c_rowsel)
        convw, CONVW = const("convw", [128, DEPTH * 124], c_convw)
        cvec, CVEC = const("cvec", [128, DEPTH * 12], c_cvec)
        lbl, LBL = const("lbl", [128, 16], c_lbl)
        tri_st = masks[:, 0:128]; blk_st = masks[:, 128:256]; tribias = masks[:, 256:384]; blkbias = masks[:, 384:512]
        ones_f = sb("ones_f", [128, 128]); ONESF = Buf("ones_f")
        ones_b = sb("ones_b", [1, 128], BF16); ONESB = Buf("ones_b")
        identb = sb("identb", [128, 128], BF16); IDENTB = Buf("identb")
        half = sb("half", [128, 1]); HALF = Buf("half")
        neg29 = sb("neg29", [128, 1]); NEG29 = Buf("neg29")
        seqsel = sb("seqsel", [128, 16, 128], BF16); SEQSEL = Buf("seqsel")
        ms(ones_f[:], 1.0, [ONESF]); ms(ones_b[:], 1.0, [ONESB]); ms(half[:], 0.5, [HALF]); ms(neg29[:], -1.0e29, [NEG29])
        cp(identb[:], ident[:], [IDENT], [IDENTB])
        pow2 = sb("pow2", [128, NIT]); POW2 = Buf("pow2")
        for k_ in range(NIT):
            ms(pow2[:, k_:k_ + 1], 2.0 ** -(k_ + 1), [POW2])

        lbe = sb("lbe", [128, 4, 4]); LBE = Buf("lbe")
        lbm = sb("lbm", [128, 4]); LBM = Buf("lbm")
        lbv = sb("lbv", [128, 4, 4]); LBV = Buf("lbv")
        omlv = sb("omlv", [128, 4, 4]); OMLV = Buf("omlv")
        nomlv = sb("nomlv", [128, 4, 4]); NOMLV = Buf("nomlv")
        lb3 = lbl[:, :].rearrange("p (h l) -> p h l", l=4)
        kb.op("dve", lambda e: e.tensor_reduce(out=lbm[:, :], in_=lb3, axis=AX.X, op=ALU.max), [LBL], [LBM])
        tt(lbe[:], lb3, lbm[:, :].unsqueeze(2).to_broadcast([128, 4, 4]), ALU.subtract, [LBL, LBM], [LBE])
        act(lbe[:], lbe[:], AF.Exp, [LBE], [LBE])
        kb.op("dve", lambda e: e.tensor_reduce(out=lbm[:, :], in_=lbe[:], axis=AX.X, op=ALU.add), [LBE], [LBM])
        kb.op("dve", lambda e: e.reciprocal(out=lbm[:, :], in_=lbm[:, :]), [LBM], [LBM])
        tt(lbe[:], lbe[:], lbm[:, :].unsqueeze(2).to_broadcast([128, 4, 4]), ALU.mult, [LBE, LBM], [LBE])
        ms(lbv[:, :, 0:1], 0.0, [LBV])
        cp(lbv[:, :, 1:2], lbe[:, :, 1:2], [LBE], [LBV])
        tt(lbv[:, :, 2:3], lbv[:, :, 1:2], lbe[:, :, 2:3], ALU.add, [LBV, LBE], [LBV])
        tt(lbv[:, :, 3:4], lbv[:, :, 2:3], lbe[:, :, 3:4], ALU.add, [LBV, LBE], [LBV])
        ts(omlv[:], lbv[:], -1.0, 1.0, ALU.mult, ALU.add, [LBV], [OMLV])
        ts(nomlv[:], omlv[:], -1.0, None, ALU.mult, None, [OMLV], [NOMLV])
        with ExitStack() as ph0:
            ssf = sb("ssf", [128, 16, 128], stack=ph0); SSF = Buf("ssf")
            kb.dma("sp", ssf[:], c_seqsel.rearrange("p (s t) -> p s t", t=128), [], [SSF], SSF)
            cp(seqsel[:], ssf[:], [SSF], [SEQSEL])
            kb.barrier()

        ptb = sb("ptb", [128, 256], I32); PTB = Buf("ptb")
        iop = sb("iop", [128, 1], I32); IOP = Buf("iop")
        pidx = sb("pidx", [128, 256], I32); PIDX = Buf("pidx")
        kb.dma("sp", ptb[:], ptab.partition_broadcast(128), [], [PTB], PTB)
        kb.op("pool", lambda e: e.iota(iop[:], pattern=[[0, 1]], base=0, channel_multiplier=1), [], [IOP])
        ts(pidx[:], ptb[:], 128, iop[:, 0:1], ALU.mult, ALU.add, [PTB, IOP], [PIDX])

        hnT = sb("hnT", [128, 8, NT], BF16)
        HN = [Buf(f"hn{b}") for b in range(len(TB))]
        merged = sb("merged", [128, 8, NT], BF16)
        MG = [Buf(f"mg{b}") for b in range(len(TB))]

        psum = [es.enter_context(nc.psum_tensor(f"ps{i}", [128, 512], F32)) for i in range(8)]
        PS = [Buf(f"ps{i}") for i in range(8)]
        ps_rr = [0]

        def next_ps():
            i = ps_rr[0] % 6
            ps_rr[0] += 1
            return psum[i], PS[i]

        wst = [sb(f"wst{i}", [128, 8, 128]) for i in range(3)]
        WST = [Buf(f"wst{i}") for i in range(3)]
        for b_ in WST:
            kb.pin(b_)
        wbf = [sb(f"wbf{i}", [128, 8, 128], BF16) for i in range(3)]
        WBF = [Buf(f"wbf{i}") for i in range(3)]
        w_rr = [0]

        def load_w(w2d, kc, pieces):
            i = w_rr[0] % 3
            w_rr[0] += 1
            wv = w2d.rearrange("(c p) n -> p c n", p=128)
            first = True
            for (col, width, dst) in pieces:
                kb.dma("sp", wst[i][:, 0:kc, dst:dst + width], wv[:, :, col:col + width], [], [WST[i]], WST[i],
                       group=not first)
                first = False
            tot = max(d + w for (_, w, d) in pieces)
            cp(wbf[i][:, 0:kc, 0:tot], wst[i][:, 0:kc, 0:tot], [WST[i]], [WBF[i]], eng="pool")
            return wbf[i], WBF[i]

        def fm_unit(l, name, evac, blocks=None, pieces=None, rows=128):
            wt, WT = load_w(w_in[l], 8, pieces or FM_UNITS[name])
            for bi, (t0, n) in enumerate(TB):
                if blocks is not None and bi not in blocks:
                    continue
                pt, PT = next_ps()
                for dc in range(8):
                    mm(pt[0:rows, 0:n], wt[:, dc, 0:rows], hnT[:, dc, t0:t0 + n], dc == 0, dc == 7, [WT, HN[bi]], [PT])
                evac(pt, PT, bi, t0, n)

        def bias_col(l, name):
            j = l * len(FM_NAMES) + FM_IDX[name]
            return biasfm[:, j:j + 1]

        btok = sb("btok", [1, 128]); BTOK = Buf("btok")
        kb.pin(BTOK)
        btok_b = sb("btok_b", [1, 128], BF16); BTOKB = Buf("btok_b")

        def tok_unit(l, col, width, tiles, evac):
            wt, WT = load_w(w_in[l], 8, [(col, width, 0)])
            kb.dma("sp", btok[:, 0:width], c_btok[:, l * N_IN + col:l * N_IN + col + width], [], [BTOK], BTOK)
            cp(btok_b[:, 0:width], btok[:, 0:width], [BTOK], [BTOKB])
            for ti, (c0, n) in enumerate(tiles):
                pt, PT = next_ps()
                hb_ = [HN[b] for b in blocks_of(c0, n)]
                for dc in range(8):
                    mm(pt[0:n, 0:width], hnT[:, dc, c0:c0 + n], wt[:, dc, 0:width], dc == 0, False, [WT] + hb_, [PT])
                mm(pt[0:n, 0:width], ones_b[0:1, 0:n], btok_b[0:1, 0:width], False, True, [ONESB, BTOKB], [PT])
                evac(pt, PT, ti, c0, n)

        gsb = [sb(f"gsb{i}", [128, 448]) for i in range(2)]
        GSB = [Buf(f"gsb{i}") for i in range(2)]
        gtmp = sb("gtmp", [128, 448]); GTMP = Buf("gtmp")
        g_rr = [0]

        def branch_out(l, br, XT, XB, w2d):
            for dc in range(8):
                gname = f"g{br}_{dc}"
                wtg, WTG = load_w(w_in[l], 8, FM_UNITS[gname])
                wty, WTY = load_w(w2d, 4, [(dc * 128, 128, 0)])
                for bi, (t0, n) in enumerate(TB):
                    i = g_rr[0] % 2
                    g_rr[0] += 1
                    pg, PG = next_ps()
                    for c in range(8):
                        mm(pg[:, 0:n], wtg[:, c, :], hnT[:, c, t0:t0 + n], c == 0, c == 7, [WTG, HN[bi]], [PG])
                    if debug:
                        ms(gsb[i][:, 0:n], 1.0, [GSB[i]])
                    else:
                        act(gsb[i][:, 0:n], pg[:, 0:n], AF.Sigmoid, [PG, BIASFM], [GSB[i]], bias=bias_col(l, gname), scale=1.0)
                    py, PY = next_ps()
                    for c in range(4):
                        mm(py[:, 0:n], wty[:, c, :], XT[:, c, t0:t0 + n], c == 0, c == 3, [WTY, XB], [PY])
                    if br == 0 or debug:
                        tt(merged[:, dc, t0:t0 + n], py[:, 0:n], gsb[i][:, 0:n], ALU.mult, [PY, GSB[i]], [MG[bi]])
                    else:
                        tt(gtmp[:, 0:n], py[:, 0:n], gsb[i][:, 0:n], ALU.mult, [PY, GSB[i]], [GTMP])
                        tt(merged[:, dc, t0:t0 + n], merged[:, dc, t0:t0 + n], gtmp[:, 0:n], ALU.add, [MG[bi], GTMP], [MG[bi]])

        with kb.phase() as ph:
            xt = [sb(f"xt{i}", [128, D], stack=ph) for i in range(2)]
            XT_ = [Buf(f"xt{i}") for i in range(2)]
            xo = [sb(f"xo{i}", [128, 8, 128], stack=ph) for i in range(2)]
            XO = [Buf(f"xo{i}") for i in range(2)]
            tiles = [(meta, 0, 16, 0)] + [(xp, 128 * i, 128, 16 + 128 * i) for i in range(16)] + [(xs, 0, 128, TP)]
            for ti, (src, r0, nr, c0) in enumerate(tiles):
                i = ti % 2
                kb.dma("sp", xt[i][0:nr, :], src[r0:r0 + nr, :], [], [XT_[i]], XT_[i])
                for hf in range(2):
                    pt, PT = next_ps()
                    for q in range(4):
                        dc = hf * 4 + q
                        tr(pt[:, q * 128:q * 128 + nr], xt[i][0:nr, dc * 128:(dc + 1) * 128], ident[0:nr, 0:nr], [XT_[i], IDENT], [PT])
                    act(xo[i][:, hf * 4:hf * 4 + 4, 0:nr], pt[:, :].rearrange("p (q t) -> p q t", q=4)[:, :, 0:nr], AF.Copy, [PT], [XO[i]])
                wr = [HTD[c][b] for c in range(8) for b in blocks_of(c0, nr)]
                kb.dma("sp", hT_d[:, :, c0:c0 + nr].rearrange("c p t -> p c t"), xo[i][:, :, 0:nr], [XO[i]], wr, XO[i], accum=True)
            kb.barrier()

        tok_tiles = [(0, 16, 0)] + [(16 + 128 * i, 128, 16 + 128 * i) for i in range(16)]
        all_tiles = tok_tiles + [(TP, 128, None)]

        def rmsnorm_block(ph, l, gains, tag):
            hb = [sb(f"hb{i}{tag}", [128, 8, 448], stack=ph) for i in range(2)]
            HB = [Buf(f"hb{i}") for i in range(2)]
            sq = sb("sq" + tag, [128, 8, 448], stack=ph); SQ = Buf("sq")
            rs = sb("rs" + tag, [128, 448], stack=ph); RS = Buf("rs")
            rs2 = sb("rs2" + tag, [128, 448], stack=ph); RS2 = Buf("rs2")
            for bi, (t0, n) in enumerate(TB):
                i = bi % 2
                kb.dma("sp", hb[i][:, :, 0:n], hT_d[:, :, t0:t0 + n].rearrange("c p t -> p c t"),
                       [HTD[c][bi] for c in range(8)], [HB[i]], HB[i])
                act(sq[:, :, 0:n], hb[i][:, :, 0:n], AF.Square, [HB[i]], [SQ])
                pt, PT = next_ps()
                for dc in range(8):
                    mm(pt[:, 0:n], ones_f[:, :], sq[:, dc, 0:n], dc == 0, dc == 7, [ONESF, SQ], [PT])
                act(rs[:, 0:n], pt[:, 0:n], AF.Sqrt, [PT], [RS], bias=EPS, scale=1.0 / D)
                kb.op("dve", lambda e: e.reciprocal(out=rs2[:, 0:n], in_=rs[:, 0:n]), [RS], [RS2])
                for dc in range(8):
                    stt(hnT[:, dc, t0:t0 + n], hb[i][:, dc, 0:n], gains[:, l * 8 + dc:l * 8 + dc + 1], rs2[:, 0:n],
                        ALU.mult, ALU.mult, [HB[i], NORMG, RS2], [HN[bi]])

        ot_rr = [0]

        def rows_out(ot, OT, srcf, SRC, width, dst_of, tiles_, idn=None):
            for (c0, n, r0) in tiles_:
                pt, PT = next_ps()
                tr(pt[0:n, 0:128], srcf[:, c0:c0 + n], ident[:, :], [SRC, IDENT], [PT])
                j = ot_rr[0] % 3
                ot_rr[0] += 1
                act(ot[j][0:n, 0:width], pt[0:n, 0:width], AF.Copy, [PT], [OT[j]])
                kb.dma("sp", dst_of(r0, n), ot[j][0:n, 0:width], [OT[j]], [], OT[j])

        def phase_B(l):
            cv0 = l * 12
            rrb = [0]
            with kb.phase() as ph:
                cy = sb("cy", [128, 4, NT], stack=ph); CY = Buf("cy")
                sg = [sb(f"bsg{i}", [128, 448], stack=ph) for i in range(2)]
                SG = [Buf(f"bsg{i}") for i in range(2)]
                with kb.phase() as ph1:
                    up = sb("up", [128, 4, 30 + TP], stack=ph1); UP = Buf("up")
                    xps = sb("xps", [128, 4, 16, 38], stack=ph1); XPS = Buf("xps")
                    stc = [sb(f"stc{i}", [120, 512], stack=ph1) for i in range(2)]
                    STC = [Buf(f"stc{i}") for i in range(2)]
                    usm = sb("usm", [128, 4, 128], stack=ph1); USM = Buf("usm")
                    bot = [sb(f"bot{i}", [128, 512], stack=ph1) for i in range(2)]
                    BOT = [Buf(f"bot{i}") for i in range(2)]
                    DD = Buf("dd")
                    ms(up[:, :, 0:30], 0.0, [UP])
                    for g4 in range(4):
                        i = g4 % 2
                        kb.dma("sp", stc[i][:, :], st_c[l, 4 * g4:4 * g4 + 4, :, :].rearrange("s r c -> (s r) c"), [], [STC[i]], STC[i])
                        pt, PT = next_ps()
                        for j in range(4):
                            tr(pt[:, j * 128:j * 128 + 120], stc[i][:, j * 128:(j + 1) * 128], ident[0:120, 0:120], [STC[i], IDENT], [PT])
                        for j in range(4):
                            kb.op("act", lambda e: e.activation(out=xps[:, j, 4 * g4:4 * g4 + 4, 0:30],
                                                                in_=pt[:, j * 128:j * 128 + 120].rearrange("p (s r) -> p s r", r=30),
                                                                func=AF.Copy), [PT], [XPS], accum=True)
                    for j in range(4):
                        wta, WTA = load_w(w_in[l], 8, FM_UNITS[f"ba{j}"])
                        wtg, WTG = load_w(w_in[l], 8, FM_UNITS[f"bg{j}"])
                        for bi, (t0, n) in enumerate(TB):
                            i = rrb[0] % 2
                            rrb[0] += 1
                            pg, PG = next_ps()
                            for c in range(8):
                                mm(pg[:, 0:n], wtg[:, c, :], hnT[:, c, t0:t0 + n], c == 0, c == 7, [WTG, HN[bi]], [PG])
                            act(sg[i][:, 0:n], pg[:, 0:n], AF.Sigmoid, [PG, BIASFM], [SG[i]], bias=bias_col(l, f"bg{j}"), scale=1.0)
                            pa, PA = next_ps()
                            for c in range(8):
                                mm(pa[:, 0:n], wta[:, c, :], hnT[:, c, t0:t0 + n], c == 0, c == 7, [WTA, HN[bi]], [PA])
                            np_ = max(0, min(t0 + n, TP) - t0)
                            if np_ > 0:
                                stt(up[:, j, 30 + t0:30 + t0 + np_], pa[:, 0:np_], bias_col(l, f"ba{j}"), sg[i][:, 0:np_],
                                    ALU.add, ALU.mult, [PA, BIASFM, SG[i]], [UP])
                            if t0 + n > TP:
                                so = TP - t0
                                kb.op("dve", lambda e: e.scalar_tensor_tensor(
                                    out=xps[:, j, :, 30:38], in0=pa[:, so:so + 128].rearrange("p (s t) -> p s t", t=8),
                                    scalar=bias_col(l, f"ba{j}"), in1=sg[i][:, so:so + 128].rearrange("p (s t) -> p s t", t=8),
                                    op0=ALU.add, op1=ALU.mult), [PA, BIASFM, SG[i]], [XPS], accum=True)
                    for j in range(4):
                        wcol = lambda k: convw[:, l * 124 + j * 31 + k:l * 124 + j * 31 + k + 1]
                        bcol = cvec[:, cv0 + j:cv0 + j + 1]
                        ysv = cy[:, j, TP:NT].rearrange("p (s t) -> p s t", t=8)
                        ts(cy[:, j, 0:TP], up[:, j, 0:TP], wcol(0), bcol, ALU.mult, ALU.add, [UP, CONVW, CVEC], [CY])
                        ts(ysv, xps[:, j, :, 0:8], wcol(0), bcol, ALU.mult, ALU.add, [XPS, CONVW, CVEC], [CY])
                        for k in range(1, 31):
                            stt(cy[:, j, 0:TP], up[:, j, k:k + TP], wcol(k), cy[:, j, 0:TP], ALU.mult, ALU.add, [UP, CONVW, CY], [CY])
                            stt(ysv, xps[:, j, :, k:k + 8], wcol(k), ysv, ALU.mult, ALU.add, [XPS, CONVW, CY], [CY])
                    pt, PT = next_ps()
                    for j in range(4):
                        tr(pt[0:30, j * 128:(j + 1) * 128], up[:, j, TP:TP + 30], ident[:, :], [UP, IDENT], [PT])
                    act(bot[0][0:30, :], pt[0:30, :], AF.Copy, [PT], [BOT[0]])
                    kb.dma("sp", p_c[l, :, :], bot[0][0:30, :], [BOT[0]], [], BOT[0])
                    for j in range(4):
                        cp(usm[:, j, :].rearrange("p (s t) -> p s t", t=8), xps[:, j, :, 30:38], [XPS], [USM])
                    pt, PT = next_ps()
                    for j in range(4):
                        tr(pt[:, j * 128:(j + 1) * 128], usm[:, j, :], ident[:, :], [USM, IDENT], [PT])
                    act(bot[1][:, :], pt[:, :], AF.Copy, [PT], [BOT[1]])
                    for s in range(16):
                        kb.dma("sp", s_c[l, s, 22:30, :], bot[1][8 * s:8 * s + 8, :], [BOT[1]], [], BOT[1], group=(s > 0))
                    kb.dma("sp", s_c[l, :, 0:22, :], st_c[l, :, 8:30, :], [], [], DD)
                    kb.barrier()
                with kb.phase() as ph2:
                    yn = sb("yn", [128, 4, NT], BF16, stack=ph2); YN = Buf("yn")
                    zb = sb("zb", [128, 4, NT], BF16, stack=ph2); ZB = Buf("zb")
                    ysq = sb("ysq", [128, 4, 448], stack=ph2); YSQ = Buf("ysq")
                    mean = sb("mean", [128, 448], stack=ph2); MEAN = Buf("mean")
                    msq = sb("msq", [128, 448], stack=ph2); MSQ = Buf("msq")
                    rstd = sb("rstd", [128, 448], stack=ph2); RSTD = Buf("rstd")
                    dtm = [sb(f"dtm{i}", [128, 448], stack=ph2) for i in range(2)]
                    DTM = [Buf(f"dtm{i}") for i in range(2)]
                    for bi, (t0, n) in enumerate(TB):
                        act(ysq[:, :, 0:n], cy[:, :, t0:t0 + n], AF.Square, [CY], [YSQ])
                        p1, P1 = next_ps()
                        for j in range(4):
                            mm(p1[:, 0:n], ones_f[:, :], cy[:, j, t0:t0 + n], j == 0, j == 3, [ONESF, CY], [P1])
                        p2, P2 = next_ps()
                        for j in range(4):
                            mm(p2[:, 0:n], ones_f[:, :], ysq[:, j, 0:n], j == 0, j == 3, [ONESF, YSQ], [P2])
                        act(mean[:, 0:n], p1[:, 0:n], AF.Identity, [P1], [MEAN], scale=1.0 / 512)
                        tt(msq[:, 0:n], mean[:, 0:n], mean[:, 0:n], ALU.mult, [MEAN], [MSQ])
                        stt(msq[:, 0:n], p2[:, 0:n], 1.0 / 512, msq[:, 0:n], ALU.mult, ALU.subtract, [P2, MSQ], [MSQ])
                        act(rstd[:, 0:n], msq[:, 0:n], AF.Sqrt, [MSQ], [RSTD], bias=EPS, scale=1.0)
                        kb.op("dve", lambda e: e.reciprocal(out=rstd[:, 0:n], in_=rstd[:, 0:n]), [RSTD], [RSTD])
                        for j in range(4):
                            i = j % 2
                            tt(dtm[i][:, 0:n], cy[:, j, t0:t0 + n], mean[:, 0:n], ALU.subtract, [CY, MEAN], [DTM[i]])
                            tt(dtm[i][:, 0:n], dtm[i][:, 0:n], rstd[:, 0:n], ALU.mult, [DTM[i], RSTD], [DTM[i]])
                            act(yn[:, j, t0:t0 + n], dtm[i][:, 0:n], AF.Silu, [DTM[i], CVEC], [YN],
                                scale=cvec[:, cv0 + 4 + j:cv0 + 5 + j], bias=cvec[:, cv0 + 8 + j:cv0 + 9 + j])
                    for j in range(4):
                        wtz, WTZ = load_w(w_in[l], 8, FM_UNITS[f"bz{j}"])
                        wtp, WTP = load_w(conv_pw[l], 4, [(j * 128, 128, 0)])
                        for bi, (t0, n) in enumerate(TB):
                            i = rrb[0] % 2
                            rrb[0] += 1
                            pz, PZ = next_ps()
                            for c in range(8):
                                mm(pz[:, 0:n], wtz[:, c, :], hnT[:, c, t0:t0 + n], c == 0, c == 7, [WTZ, HN[bi]], [PZ])
                            act(sg[i][:, 0:n], pz[:, 0:n], AF.Silu, [PZ, BIASFM], [SG[i]], bias=bias_col(l, f"bz{j}"), scale=1.0)
                            pp, PP = next_ps()
                            for c in range(4):
                                mm(pp[:, 0:n], wtp[:, c, :], yn[:, c, t0:t0 + n], c == 0, c == 3, [WTP, YN], [PP])
                            tt(zb[:, j, t0:t0 + n], pp[:, 0:n], sg[i][:, 0:n], ALU.mult, [PP, SG[i]], [ZB])
                    branch_out(l, 1, zb, ZB, w_pb[l])
                    kb.barrier()

        def phase_A(l):
            pb = lambda p: p[:, :].bitcast(BF16)
            chunks = [(0, 16)] + [(16 + 64 * c, 64) for c in range(32)]
            with kb.phase() as ph:
                xa = sb("xa", [128, 4, NT], BF16, stack=ph); XA = Buf("xa")
                gng = sb("gngt", [128, 128], stack=ph); GNG = Buf("gng")
                kb.dma("sp", gng[:], gng_d[l:l + 1, :].partition_broadcast(128), [], [GNG], GNG)
                T = [sb(f"aT{i}", [128, NT], stack=ph) for i in range(4)]
                TT = [Buf(f"aT{i}") for i in range(4)]
                qt_ = sb("aqt", [128, NT], BF16, stack=ph); QT = Buf("aqt")
                kt_ = sb("akt", [128, NT], BF16, stack=ph); KT = Buf("akt")
                kd_ = sb("akd", [128, NT], BF16, stack=ph); KD = Buf("akd")
                az_ = sb("aaz", [128, NT], BF16, stack=ph); AZ = Buf("aaz")
                qs = [sb(f"aqs{i}", [128, 448], stack=ph) for i in range(2)]
                QS = [Buf(f"aqs{i}") for i in range(2)]
                r1 = sb("aR1", [128, 4352], BF16, stack=ph); R1 = Buf("aR1")
                r2 = sb("aR2", [128, 4224], BF16, stack=ph); R2 = Buf("aR2")
                r3 = sb("aR3", [128, 4224], BF16, stack=ph); R3 = Buf("aR3")
                V = r1[0:64, :].rearrange("p (c v) -> p c v", v=128)
                SO = r1[:, 0:4096].bitcast(F32).rearrange("p (s v) -> p s v", v=128)
                KDT = r2[0:64, :].rearrange("p (c v) -> p c v", v=128)
                SOb = r2[:, 0:2048].rearrange("p (s v) -> p s v", v=128)
                QZ = r2[:, 2048:4096].rearrange("p (s v) -> p s v", v=128)
                OR = r3[0:64, :].rearrange("p (c v) -> p c v", v=128)
                VZ = r3[:, 0:2048].rearrange("p (s v) -> p s v", v=128)
                vs = sb("avs", [128, 128], BF16, stack=ph); VS = Buf("avs")
                kdts = sb("akdts", [128, 128], BF16, stack=ph); KDTS = Buf("akdts")
                ors = sb("aors", [128, 128], BF16, stack=ph); ORS = Buf("aors")
                attm = [sb(f"attm{i}", [128, 128], BF16, stack=ph) for i in range(2)]
                ATT = [Buf(f"attm{i}") for i in range(2)]
                junk = sb("ajunk", [128, 128], BF16, stack=ph); JUNK = Buf("ajunk")
                ss = sb("ass", [128, 34], stack=ph); SS = Buf("ass")
                rsa = sb("arsa", [128, 34], stack=ph); RSA = Buf("arsa")
                S = sb("aS", [128, 128], stack=ph); S_ = Buf("aS")
                Sb = sb("aSb", [128, 128], BF16, stack=ph); SB_ = Buf("aSb")
                ebl = sb("aebl", [128, 49], stack=ph); EBL = Buf("aebl")
                bs = sb("abs", [128, 49], stack=ph); BS = Buf("abs")
                bl = sb("abl", [128, 49], stack=ph); BL = Buf("abl")
                rr = [0]

                def views(t):
                    return (t[:, 16:TP].rearrange("p (c t) -> p c t", t=64), t[:, TP:NT].rearrange("p (s t) -> p s t", t=8))

                for hd in range(4):
                    lbc = lbv[:, hd, l:l + 1]; omlc = omlv[:, hd, l:l + 1]; nomlc = nomlv[:, hd, l:l + 1]
                    B = T[0]

                    def ev_f(pt, PT, bi, t0, n):
                        act(T[0][:, t0:t0 + n], pt[:, 0:n], AF.Sigmoid, [PT, BIASFM], [TT[0]], bias=bias_col(l, f"af{hd}"), scale=1.0)
                    fm_unit(l, f"af{hd}", ev_f)
                    act(T[1][:, :], T[0][:, :], AF.Ln, [TT[0], OMLV, LBV], [TT[1]], scale=omlc, bias=lbc)
                    ts(T[2][:, :], T[0][:, :], nomlc, omlc, ALU.mult, ALU.add, [TT[0], NOMLV, OMLV], [TT[2]])
                    onesbc = ones_f[:, 0:1]
                    kb.op("dve", lambda e: e.tensor_tensor_scan(out=T[0][:, 0:TP], data0=onesbc.to_broadcast([128, TP]),
                                                                 data1=T[1][:, 0:TP], initial=0.0, op0=ALU.mult, op1=ALU.add),
                          [TT[1], ONESF, TT[0]], [TT[0]])
                    kb.op("dve", lambda e: e.tensor_tensor_scan(out=T[0][:, TP:NT], data0=onesbc.to_broadcast([128, NSM]),
                                                                 data1=T[1][:, TP:NT], initial=0.0, op0=ALU.mult, op1=ALU.add),
                          [TT[1], ONESF, TT[0]], [TT[0]])
                    ms(bs[:, 0:1], 0.0, [BS]); ms(bs[:, 33:34], 0.0, [BS])
                    cp(bs[:, 1:33], B[:, 15:15 + 64 * 32:64], [TT[0]], [BS])
                    cp(bs[:, 34:49], B[:, TP + 7:TP + 7 + 8 * 15:8], [TT[0]], [BS])
                    cp(bl[:, 0:1], B[:, 15:16], [TT[0]], [BL])
                    cp(bl[:, 1:33], B[:, 79:79 + 64 * 32:64], [TT[0]], [BL])
                    cp(bl[:, 33:49], B[:, TP + 7:NT:8], [TT[0]], [BL])
                    bx, bsm = views(B)
                    t1x, t1s = views(T[1])
                    cp(T[1][:, 0:16], B[:, 0:16], [TT[0]], [TT[1]])
                    tt(t1x, bx, bs[:, 1:33].unsqueeze(2).to_broadcast([128, 32, 64]), ALU.subtract, [TT[0], BS], [TT[1]])
                    tt(t1s, bsm, bs[:, 33:49].unsqueeze(2).to_broadcast([128, 16, 8]), ALU.subtract, [TT[0], BS], [TT[1]])
                    act(T[3][:, :], T[1][:, :], AF.Exp, [TT[1]], [TT[3]])
                    act(T[1][:, :], T[1][:, :], AF.Exp, [TT[1]], [TT[1]], scale=-1.0)
                    tt(kt_[:, :], T[2][:, :], T[1][:, :], ALU.mult, [TT[2], TT[1]], [KT])
                    tt(T[1][:, 0:16], B[:, 0:16], bl[:, 0:1].to_broadcast([128, 16]), ALU.subtract, [TT[0], BL, KT], [TT[1]])
                    tt(t1x, bx, bl[:, 1:33].unsqueeze(2).to_broadcast([128, 32, 64]), ALU.subtract, [TT[0], BL], [TT[1]])
                    tt(t1s, bsm, bl[:, 33:49].unsqueeze(2).to_broadcast([128, 16, 8]), ALU.subtract, [TT[0], BL], [TT[1]])
                    act(T[1][:, :], T[1][:, :], AF.Exp, [TT[1]], [TT[1]], scale=-1.0)
                    tt(kd_[:, :], T[2][:, :], T[1][:, :], ALU.mult, [TT[2], TT[1]], [KD])
                    cp(ebl[:, 0:1], T[3][:, 15:16], [TT[3]], [EBL])
                    cp(ebl[:, 1:33], T[3][:, 79:79 + 64 * 32:64], [TT[3]], [EBL])
                    cp(ebl[:, 33:49], T[3][:, TP + 7:NT:8], [TT[3]], [EBL])

                    def ev_q(pt, PT, bi, t0, n):
                        i = rr[0] % 2
                        rr[0] += 1
                        act(qs[i][:, 0:n], pt[:, 0:n], AF.Silu, [PT, BIASFM], [QS[i]], bias=bias_col(l, f"aq{hd}"), scale=1.0)
                        tt(qt_[:, t0:t0 + n], qs[i][:, 0:n], T[3][:, t0:t0 + n], ALU.mult, [QS[i], TT[3]], [QT])
                    fm_unit(l, f"aq{hd}", ev_q)

                    def ev_z(pt, PT, bi, t0, n):
                        act(az_[:, t0:t0 + n], pt[:, 0:n], AF.Silu, [PT, BIASFM], [AZ], bias=bias_col(l, f"az{hd}"), scale=1.0)
                    fm_unit(l, f"az{hd}", ev_z)

                    def ev_v(pt, PT, ti, c0, n):
                        if ti < 33:
                            act(V[0:n, ti, :], pt[0:n, 0:128], AF.Copy, [PT], [R1])
                        else:
                            act(vs[:, :], pt[:, 0:128], AF.Copy, [PT], [VS])
                    tok_unit(l, O_AI + 128 * hd, 128, chunks + [(TP, 128)], ev_v)

                    pt, PT = next_ps()
                    tr(pb(pt)[0:16, 0:128], kd_[:, 0:16], identb[:, :], [KD, IDENTB], [PT])
                    act(KDT[0:16, 0, :], pb(pt)[0:16, 0:128], AF.Copy, [PT], [R2])
                    for g in range(8):
                        pt, PT = next_ps()
                        for q in range(4):
                            c0 = 16 + 64 * (4 * g + q)
                            tr(pb(pt)[0:64, q * 128:(q + 1) * 128], kd_[:, c0:c0 + 64], identb[:, :], [KD, IDENTB], [PT])
                        act(KDT[0:64, 1 + 4 * g:5 + 4 * g, :], pb(pt)[0:64, 0:512].rearrange("p (q v) -> p q v", v=128), AF.Copy, [PT], [R2])
                    pt, PT = next_ps()
                    tr(pb(pt)[:, 0:128], kd_[:, TP:NT], identb[:, :], [KD, IDENTB], [PT])
                    act(kdts[:, :], pb(pt)[:, 0:128], AF.Copy, [PT], [KDTS])

                    ms(S[:, :], 0.0, [S_]); ms(Sb[:, :], 0.0, [SB_]); ms(ss[:, :], 1.0, [SS])
                    for ci, (c0, n) in enumerate(chunks):
                        i = ci % 2
                        pa, PA = next_ps()
                        mm(pa[0:n, 0:n], kt_[:, c0:c0 + n], qt_[:, c0:c0 + n], True, True, [KT, QT], [PA])
                        tt(attm[i][0:n, 0:n], pa[0:n, 0:n], tri_st[0:n, 0:n], ALU.mult, [PA, MASKS], [ATT[i]])
                        po, PO = next_ps()
                        mm(po[0:n, 0:128], qt_[:, c0:c0 + n], Sb[:, :], True, False, [QT, SB_], [PO])
                        mm(po[0:n, 0:128], attm[i][0:n, 0:n], V[0:n, ci, :], False, True, [ATT[i], R1], [PO])
                        act(OR[0:n, ci, :], po[0:n, 0:128], AF.Copy, [PO], [R3])
                        act(junk[0:n, :], po[0:n, 0:128], AF.Square, [PO], [JUNK, SS], accum_out=ss[0:n, ci:ci + 1])
                        pS, PSB = next_ps()
                        mm(pS[:, 0:128], KDT[0:n, ci, :], V[0:n, ci, :], True, True, [R2, R1], [PSB])
                        stt(Sb[:, :], S[:, :], ebl[:, ci:ci + 1], pS[:, 0:128], ALU.mult, ALU.add, [S_, EBL, PSB], [SB_])
                        stt(S[:, :], S[:, :], ebl[:, ci:ci + 1], pS[:, 0:128], ALU.mult, ALU.add, [S_, EBL, PSB], [S_])
                    kb.dma("sp", p_h[l, hd, :, :], S[:, :], [S_], [], S_)
                    act(rsa[0:64, 0:33], ss[0:64, 0:33], AF.Sqrt, [SS], [RSA], bias=EPS, scale=1.0 / 128)
                    kb.op("dve", lambda e: e.reciprocal(out=rsa[0:64, 0:33], in_=rsa[0:64, 0:33]), [RSA], [RSA])
                    tt(OR[:, :, :], OR[:, :, :], rsa[0:64, 0:33].unsqueeze(2).to_broadcast([64, 33, 128]), ALU.mult, [R3, RSA], [R3])
                    tt(OR[:, :, :], OR[:, :, :], gng[0:64, :].unsqueeze(1).to_broadcast([64, 33, 128]), ALU.mult, [R3, GNG], [R3])
                    pt, PT = next_ps()
                    tr(pb(pt)[:, 0:16], OR[0:16, 0, :], identb[0:16, 0:16], [R3, IDENTB], [PT])
                    tt(xa[:, hd, 0:16], pb(pt)[:, 0:16], az_[:, 0:16], ALU.mult, [PT, AZ], [XA])
                    for g in range(4):
                        pt, PT = next_ps()
                        for q in range(8):
                            tr(pb(pt)[:, q * 64:(q + 1) * 64], OR[0:64, 1 + 8 * g + q, :], identb[0:64, 0:64], [R3, IDENTB], [PT])
                        c0 = 16 + 512 * g
                        tt(xa[:, hd, c0:c0 + 512], pb(pt)[:, 0:512], az_[:, c0:c0 + 512], ALU.mult, [PT, AZ], [XA])

                    kb.dma("sp", SO, st_h[l, :, hd, :, :].rearrange("s k v -> k s v"), [], [R1], R1)
                    cp(SOb, SO, [R1], [R2], eng="pool")
                    kb.op("dve", lambda e: e.tensor_tensor(out=QZ, in0=qt_[:, TP:NT].unsqueeze(1).to_broadcast([128, 16, 128]),
                                                           in1=seqsel[:, :, :], op=ALU.mult), [QT, SEQSEL], [R2], accum=True)
                    tt(VZ, vs[:, :].unsqueeze(1).to_broadcast([128, 16, 128]), rowsel[:, :].unsqueeze(2).to_broadcast([128, 16, 128]),
                       ALU.mult, [VS, ROWSEL], [R3])
                    pa, PA = next_ps()
                    mm(pa[:, 0:128], kt_[:, TP:NT], qt_[:, TP:NT], True, True, [KT, QT], [PA])
                    tt(attm[0][:, :], pa[:, 0:128], blk_st, ALU.mult, [PA, MASKS], [ATT[0]])
                    po, PO = next_ps()
                    for s in range(16):
                        mm(po[:, 0:128], QZ[:, s, :], SOb[:, s, :], s == 0, False, [R2], [PO])
                    mm(po[:, 0:128], attm[0][:, :], vs[:, :], False, True, [ATT[0], VS], [PO])
                    act(ors[:, :], po[:, 0:128], AF.Copy, [PO], [ORS])
                    act(junk[:, :], po[:, 0:128], AF.Square, [PO], [JUNK, SS], accum_out=ss[:, 33:34])
                    for s in range(16):
                        pS, PSB = next_ps()
                        mm(pS[:, 0:128], kdts[:, :], VZ[:, s, :], True, True, [KDTS, R3], [PSB])
                        stt(SO[:, s, :], SO[:, s, :], ebl[:, 33 + s:34 + s], pS[:, 0:128], ALU.mult, ALU.add, [R1, EBL, PSB], [R1])
                    kb.dma("sp", s_h[l, :, hd, :, :].rearrange("s k v -> k s v"), SO, [R1], [], R1)
                    act(rsa[:, 33:34], ss[:, 33:34], AF.Sqrt, [SS], [RSA], bias=EPS, scale=1.0 / 128)
                    kb.op("dve", lambda e: e.reciprocal(out=rsa[:, 33:34], in_=rsa[:, 33:34]), [RSA], [RSA])
                    ts(ors[:, :], ors[:, :], rsa[:, 33:34], None, ALU.mult, None, [ORS, RSA], [ORS])
                    tt(ors[:, :], ors[:, :], gng[:, :], ALU.mult, [ORS, GNG], [ORS])
                    pt, PT = next_ps()
                    tr(pb(pt)[:, 0:128], ors[:, :], identb[:, :], [ORS, IDENTB], [PT])
                    tt(xa[:, hd, TP:NT], pb(pt)[:, 0:128], az_[:, TP:NT], ALU.mult, [PT, AZ], [XA])
                branch_out(l, 0, xa, XA, w_pa[l])
                kb.barrier()

        def topk_group(items):
            act_items = []
            for (nq, S, sc, SC, m01, M01, st) in items:
                (amax, mid, htab, cnt, gh, lo, TKB) = st
                if S <= 256:
                    continue
                ts(gh[0:nq, :], amax[0:nq, :], 2.0, 1.0, ALU.mult, ALU.add, [TKB], [TKB])
                ts(htab[0:nq, :], pow2[0:nq, :], gh[0:nq, 0:1], None, ALU.mult, None, [POW2, TKB], [TKB])
                ms(mid[0:nq, :], -0.5, [TKB])
                act_items.append((nq, S, sc, SC, m01, M01, st))
            for it in range(NIT):
                for (nq, S, sc, SC, m01, M01, st) in act_items:
                    (amax, mid, htab, cnt, gh, lo, TKB) = st
                    kb.op("dve", lambda e: e.tensor_scalar(out=m01[0:nq, 0:S], in0=sc[0:nq, 0:S], scalar1=mid[0:nq, 0:1], scalar2=None,
                                                           op0=ALU.is_gt, op1=ALU.add, accum_out=cnt[0:nq, 0:1]), [SC, TKB], [TKB, M01])
                for (nq, S, sc, SC, m01, M01, st) in act_items:
                    (amax, mid, htab, cnt, gh, lo, TKB) = st
                    ts(gh[0:nq, :], cnt[0:nq, :], 255.5, htab[0:nq, it:it + 1], ALU.is_gt, ALU.mult, [TKB], [TKB])
                for (nq, S, sc, SC, m01, M01, st) in act_items:
                    (amax, mid, htab, cnt, gh, lo, TKB) = st
                    if it < NIT - 1:
                        stt(mid[0:nq, :], gh[0:nq, :], htab[0:nq, it + 1:it + 2], mid[0:nq, :], ALU.subtract, ALU.add, [TKB], [TKB])
                    else:
                        stt(lo[0:nq, :], gh[0:nq, :], htab[0:nq, it:it + 1], mid[0:nq, :], ALU.subtract, ALU.add, [TKB], [TKB])
            for (nq, S, sc, SC, m01, M01, st) in items:
                (amax, mid, htab, cnt, gh, lo, TKB) = st
                thr = neg29[0:nq, 0:1] if S <= 256 else lo[0:nq, 0:1]
                ts(m01[0:nq, 0:S], sc[0:nq, 0:S], thr, None, ALU.is_gt, None, [SC, TKB, NEG29], [M01])

        def tk_state(stack, tag):
            return (sb(f"tka{tag}", [128, 1], stack=stack), sb(f"tkm{tag}", [128, 1], stack=stack), sb(f"tkh{tag}", [128, NIT], stack=stack),
                    sb(f"tkc{tag}", [128, 1], stack=stack), sb(f"tkg{tag}", [128, 1], stack=stack), sb(f"tkl{tag}", [128, 1], stack=stack),
                    Buf(f"tk{tag}"))

        def phase_C(l):
            pb = lambda p: p[:, :].bitcast(BF16)
            with kb.phase() as ph:
                qtc = sb("cqt", [128, 4, NT], BF16, stack=ph); QTC = Buf("cqt")
                qi = sb("cqi", [128, 2, NT], BF16, stack=ph); QI = Buf("cqi")
                qis = sb("cqis", [64, 4, 128], BF16, stack=ph); QIS = Buf("cqis")
                ktb = sb("cktb", [128, NT], BF16, stack=ph); KTB = Buf("cktb")
                kib = sb("ckib", [128, NT], BF16, stack=ph); KIB = Buf("ckib")
                vaug = sb("cvaug", [128, 18, 2, 65], BF16, stack=ph); VAUG = Buf("cvaug")
                cw = sb("ccw", [128, 18, 4], stack=ph); CW = Buf("ccw")
                oct_ = sb("coct", [128, 4, NT], BF16, stack=ph); OCT = Buf("coct")
                with kb.phase() as p1:
                    cosT = sb("cosT", [128, NT], stack=p1); COS = Buf("cos")
                    sinT = sb("sinT", [128, NT], stack=p1); SIN = Buf("sin")
                    kb.dma("sp", cosT[:], c_cos, [], [COS], COS)
                    kb.dma("sp", sinT[:], c_sin, [], [SIN], SIN)
                    xf = [sb(f"xf{i}", [128, 448], stack=p1) for i in range(2)]
                    XF = [Buf(f"xf{i}") for i in range(2)]
                    t1 = sb("t1", [128, 448], stack=p1); T1 = Buf("t1")
                    t2 = sb("t2", [128, 448], stack=p1); T2 = Buf("t2")
                    kTf = sb("kTf", [128, NT], stack=p1); KTF = Buf("kTf")
                    kiTf = sb("kiTf", [128, NT], stack=p1); KITF = Buf("kiTf")
                    ot = [sb(f"ot{i}", [128, 128], stack=p1) for i in range(3)]
                    OT = [Buf(f"ot{i}") for i in range(3)]
                    vtok = [sb(f"vtok{i}", [128, 128], stack=p1) for i in range(2)]
                    VTOK = [Buf(f"vtok{i}") for i in range(2)]

                    def rope_evac(dst_of, DST, name, rows=128):
                        def evac(pt, PT, bi, t0, n):
                            i = bi % 2
                            act(xf[i][0:rows, 0:n], pt[0:rows, 0:n], AF.Identity, [PT, BIASFM], [XF[i]], bias=bias_col(l, name)[0:rows, :], scale=1.0)
                            p2, P2 = next_ps()
                            mm(p2[0:rows, 0:n], rmat[0:rows, 0:rows], xf[i][0:rows, 0:n], True, True, [RMAT, XF[i]], [P2])
                            tt(t1[0:rows, 0:n], xf[i][0:rows, 0:n], cosT[0:rows, t0:t0 + n], ALU.mult, [XF[i], COS], [T1])
                            tt(t2[0:rows, 0:n], p2[0:rows, 0:n], sinT[0:rows, t0:t0 + n], ALU.mult, [P2, SIN], [T2])
                            tt(dst_of(t0, n), t1[0:rows, 0:n], t2[0:rows, 0:n], ALU.add, [T1, T2], [DST])
                        return evac

                    fm_unit(l, "ck", rope_evac(lambda t0, n: kTf[:, t0:t0 + n], KTF, "ck"))
                    fm_unit(l, "cki", rope_evac(lambda t0, n: kiTf[:, t0:t0 + n], KITF, "cki"))
                    cp(ktb[:, :], kTf[:, :], [KTF], [KTB], eng="pool")
                    cp(kib[:, :], kiTf[:, :], [KITF], [KIB], eng="pool")
                    rows_out(ot, OT, kTf, KTF, 128, lambda r0, n: (p_k[l, r0:r0 + n, :] if r0 is not None else s_k[l, :, :]), all_tiles)
                    rows_out(ot, OT, kiTf, KITF, 64, lambda r0, n: (p_ki[l, r0:r0 + n, :] if r0 is not None else s_ki[l, :, :]), all_tiles)
                    for j in range(4):
                        fm_unit(l, f"cq{j}", rope_evac(lambda t0, n, j=j: qtc[:, j, t0:t0 + n], QTC, f"cq{j}"))
                    for j in range(2):
                        fm_unit(l, f"cqi{j}", rope_evac(lambda t0, n, j=j: qi[:, j, t0:t0 + n], QI, f"cqi{j}"))
                    for h in range(4):
                        def evq(pt, PT, bi, t0, n, h=h):
                            so = TP - t0
                            i = bi % 2
                            act(xf[i][0:64, 0:128], pt[0:64, so:so + 128], AF.Identity, [PT, BIASFM], [XF[i]],
                                bias=bias_col(l, f"cqis{h}")[0:64, :], scale=1.0)
                            p2, P2 = next_ps()
                            mm(p2[0:64, 0:128], rmat[0:64, 0:64], xf[i][0:64, 0:128], True, True, [RMAT, XF[i]], [P2])
                            tt(t1[0:64, 0:128], xf[i][0:64, 0:128], cosT[0:64, TP:NT], ALU.mult, [XF[i], COS], [T1])
                            tt(t2[0:64, 0:128], p2[0:64, 0:128], sinT[0:64, TP:NT], ALU.mult, [P2, SIN], [T2])
                            tt(qis[:, h, :], t1[0:64, 0:128], t2[0:64, 0:128], ALU.add, [T1, T2], [QIS])
                        fm_unit(l, f"cqis{h}", evq, blocks={4}, rows=64)

                    ms(vaug[:, :, :, 64:65], 1.0, [VAUG])

                    def ev_v(pt, PT, ti, c0, n):
                        i = ti % 2
                        act(vtok[i][0:n, :], pt[0:n, 0:128], AF.Copy, [PT], [VTOK[i]])
                        cp(vaug[0:n, ti, :, 0:64], vtok[i][0:n, :].rearrange("p (g d) -> p g d", d=64), [VTOK[i]], [VAUG])
                        r0 = all_tiles[ti][2]
                        dst = p_v[l, r0:r0 + n, :] if r0 is not None else s_v[l, :, :]
                        kb.dma("sp", dst, vtok[i][0:n, :], [VTOK[i]], [], VTOK[i])
                    tok_unit(l, O_CV, 128, [(c0, n) for (c0, n, _) in all_tiles], ev_v)

                    def ev_w(pt, PT, ti, c0, n):
                        act(cw[0:n, ti, :], pt[0:n, 0:4], AF.Identity, [PT], [CW], scale=IDX_SCALE)
                    tok_unit(l, O_CW, 4, [(c0, n) for (c0, n, _) in all_tiles], ev_w)
                    kb.barrier()

                with kb.phase() as p2s:
                    NSL = 2
                    scs = [sb(f"csc{i}", [128, TP], stack=p2s) for i in range(NSL)]; SCS = [Buf(f"csc{i}") for i in range(NSL)]
                    m01s = [sb(f"cm01{i}", [128, TP], BF16, stack=p2s) for i in range(NSL)]; M01S = [Buf(f"cm01{i}") for i in range(NSL)]
                    mTs = [sb(f"cmT{i}", [128, 17, 128], BF16, stack=p2s) for i in range(NSL)]; MTS = [Buf(f"cmT{i}") for i in range(NSL)]
                    sts = [tk_state(p2s, f"c{i}") for i in range(NSL)]
                    rl = [sb(f"crl{i}", [128, 512], stack=p2s) for i in range(2)]
                    RL = [Buf(f"crl{i}") for i in range(2)]
                    pP = [sb(f"cP{i}", [128, 4, 128], BF16, stack=p2s) for i in range(3)]
                    PP = [Buf(f"cP{i}") for i in range(3)]
                    osb = sb("cosb", [128, 8, 65], stack=p2s); OSB = Buf("cosb")
                    orc = sb("corc", [128, 8], stack=p2s); ORC = Buf("corc")
                    onb = sb("conb", [128, 512], BF16, stack=p2s); ONB = Buf("conb")
                    rr = [0]
                    pr = [0]
                    for grp0 in range(0, len(tok_tiles), NSL):
                        grp_tiles = list(enumerate(tok_tiles))[grp0:grp0 + NSL]
                        items = []
                        for slot, (qt_i, (c0, nq, _)) in enumerate(grp_tiles):
                            sc, SC, m01, M01, st = scs[slot], SCS[slot], m01s[slot], M01S[slot], sts[slot]
                            S = c0 + nq
                            for h in range(4):
                                b0 = (h % 2) * 64
                                for k0 in range(0, S, 512):
                                    kn = min(512, S - k0)
                                    i = rr[0] % 2
                                    rr[0] += 1
                                    pt, PT = next_ps()
                                    mm(pt[0:nq, 0:kn], qi[b0:b0 + 64, h // 2, c0:c0 + nq], kib[b0:b0 + 64, k0:k0 + kn], True, True, [QI, KIB], [PT])
                                    act(rl[i][0:nq, 0:kn], pt[0:nq, 0:kn], AF.Relu, [PT], [RL[i]])
                                    if h == 0:
                                        ts(sc[0:nq, k0:k0 + kn], rl[i][0:nq, 0:kn], cw[0:nq, qt_i, 0:1], None, ALU.mult, None, [RL[i], CW], [SC])
                                    else:
                                        stt(sc[0:nq, k0:k0 + kn], rl[i][0:nq, 0:kn], cw[0:nq, qt_i, h:h + 1], sc[0:nq, k0:k0 + kn],
                                            ALU.mult, ALU.add, [RL[i], CW, SC], [SC])
                            if S > 256:
                                kb.op("dve", lambda e: e.reduce_max(out=st[0][0:nq, :], in_=sc[0:nq, 0:S], axis=AX.X, apply_absolute_value=True),
                                      [SC], [st[-1]])
                            tt(sc[0:nq, c0:c0 + nq], sc[0:nq, c0:c0 + nq], tribias[0:nq, 0:nq], ALU.add, [SC, MASKS], [SC])
                            items.append((nq, S, sc, SC, m01, M01, st))
                        topk_group(items)
                        for slot, (qt_i, (c0, nq, _)) in enumerate(grp_tiles):
                            m01, M01, maskT, MT = m01s[slot], M01S[slot], mTs[slot], MTS[slot]
                            ktiles = [(0, 16)] + [(16 + 128 * i, 128) for i in range(qt_i)]
                            for g0 in range(0, len(ktiles), 4):
                                pt, PT = next_ps()
                                grp = ktiles[g0:g0 + 4]
                                for q, (kc0, nk) in enumerate(grp):
                                    tr(pb(pt)[0:nk, q * 128:q * 128 + nq], m01[0:nq, kc0:kc0 + nk], identb[0:nq, 0:nq], [M01, IDENTB], [PT])
                                if g0 == 0:
                                    cp(maskT[0:16, 0, 0:nq], pb(pt)[0:16, 0:nq], [PT], [MT])
                                    if len(grp) > 1:
                                        cp(maskT[:, 1:len(grp), 0:nq], pb(pt)[:, 128:128 * len(grp)].rearrange("p (q t) -> p q t", t=128)[:, :, 0:nq], [PT], [MT])
                                else:
                                    cp(maskT[:, g0:g0 + len(grp), 0:nq], pb(pt)[:, 0:128 * len(grp)].rearrange("p (q t) -> p q t", t=128)[:, :, 0:nq], [PT], [MT])
                            steps = [(kt, kc0, nk, g) for kt, (kc0, nk) in enumerate(ktiles) for g in range(2)]

                            def emit_qk(step):
                                kt, kc0, nk, g = step
                                pt, PT = next_ps()
                                mm(pt[0:nk, 0:4 * nq], ktb[g * 64:g * 64 + 64, kc0:kc0 + nk], qtc[g * 64:g * 64 + 64, :, c0:c0 + nq], True, True, [KTB, QTC], [PT])
                                return pt, PT
                            pend = emit_qk(steps[0])
                            for si, (kt, kc0, nk, g) in enumerate(steps):
                                pt, PT = pend
                                if si + 1 < len(steps):
                                    pend = emit_qk(steps[si + 1])
                                i = pr[0] % 3
                                pr[0] += 1
                                act(pP[i][0:nk, :, 0:nq], pt[0:nk, 0:4 * nq].rearrange("p (j t) -> p j t", j=4), AF.Exp, [PT], [PP[i]], scale=0.125)
                                tt(pP[i][0:nk, :, 0:nq], pP[i][0:nk, :, 0:nq], maskT[0:nk, kt, 0:nq].unsqueeze(1).to_broadcast([nk, 4, nq]), ALU.mult, [PP[i], MT], [PP[i]])
                                for jj in range(4):
                                    mm(psum[6 + g][0:nq, jj * 65:(jj + 1) * 65], pP[i][0:nk, jj, 0:nq], vaug[0:nk, kt, g, :],
                                       (kt == 0 and jj == 0), (kt == len(ktiles) - 1), [PP[i], VAUG], [PS[6 + g]], skip_group_check=True)
                            for g in range(2):
                                act(osb[0:nq, 4 * g:4 * g + 4, :], psum[6 + g][0:nq, 0:260].rearrange("p (j d) -> p j d", d=65), AF.Copy, [PS[6 + g]], [OSB])
                            kb.op("dve", lambda e: e.reciprocal(out=orc[0:nq, :], in_=osb[0:nq, :, 64]), [OSB], [ORC])
                            tt(onb[0:nq, :].rearrange("p (h d) -> p h d", d=64), osb[0:nq, :, 0:64], orc[0:nq, :].unsqueeze(2).to_broadcast([nq, 8, 64]),
                               ALU.mult, [OSB, ORC], [ONB])
                            pt, PT = next_ps()
                            for j in range(4):
                                tr(pb(pt)[:, j * 128:j * 128 + nq], onb[0:nq, j * 128:(j + 1) * 128], identb[0:nq, 0:nq], [ONB, IDENTB], [PT])
                            cp(oct_[:, :, c0:c0 + nq], pb(pt)[:, 0:512].rearrange("p (j t) -> p j t", t=128)[:, :, 0:nq], [PT], [OCT])
                    kb.barrier()

                with kb.phase() as p3:
                    maskT = sb("smT", [128, 17, 128], BF16, stack=p3); MT = Buf("smT")
                    rr = [0]
                    pidxl = sb("pidxl", [128, 256], I32, stack=p3); PIDXL = Buf("pidxl")
                    ts(pidxl[:, :], pidx[:, :], l * N_PHYS * 128, None, ALU.add, None, [PIDX], [PIDXL])
                    cache_ki_f = cache_ki.rearrange("l r c -> (l r) c")
                    cache_k_f = cache_k.rearrange("l r c -> (l r) c")
                    cache_v_f = cache_v.rearrange("l r c -> (l r) c")
                    with kb.phase() as p3a:
                        sc = sb("ssc", [128, 2176], stack=p3a); SC = Buf("ssc")
                        m01 = sb("sm01", [128, 2176], BF16, stack=p3a); M01 = Buf("sm01")
                        tkb = tk_state(p3a, "s")
                        TKB = tkb[-1]
                        rl = [sb(f"srl{i}", [128, 512], stack=p3a) for i in range(2)]
                        RL = [Buf(f"srl{i}") for i in range(2)]
                        kikb = sb("skikb", [64, 16, 256], BF16, stack=p3a); KIKB = Buf("skikb")
                        qiz = [sb(f"sqiz{i}", [64, 16, 128], BF16, stack=p3a) for i in range(1)]
                        QIZ = [Buf(f"sqiz{i}") for i in range(1)]
                        kst = [sb(f"skst{i}", [128, 2, 64], stack=p3a) for i in range(4)]
                        KST = [Buf(f"skst{i}") for i in range(4)]
                        sr = 0
                        for kbk in range(8):
                            for s2 in range(0, 16, 2):
                                pt, PT = next_ps()
                                for sq_ in range(2):
                                    s = s2 + sq_
                                    i = sr % 4
                                    sr += 1
                                    for pg in range(2):
                                        col = s * 16 + kbk * 2 + pg
                                        kb.dma("pool", kst[i][:, pg, :], cache_ki_f, [PIDXL], [KST[i]], KST[i], group=(pg > 0),
                                               indirect=bass.IndirectOffsetOnAxis(ap=pidxl[:, col:col + 1], axis=0))
                                    for pg in range(2):
                                        tr(pt[0:64, (sq_ * 2 + pg) * 128:(sq_ * 2 + pg + 1) * 128], kst[i][:, pg, :], ident[:, :], [KST[i], IDENT], [PT])
                                act(kikb[:, s2:s2 + 2, :], pt[0:64, 0:512].rearrange("p (s k) -> p s k", k=256), AF.Copy, [PT], [KIKB])
                            for h in range(4):
                                i = 0
                                tt(qiz[i][:, :, :], qis[:, h, :].unsqueeze(1).to_broadcast([64, 16, 128]), seqsel[0:64, :, :], ALU.mult, [QIS, SEQSEL], [QIZ[i]])
                                pt, PT = next_ps()
                                for s in range(16):
                                    mm(pt[:, 0:256], qiz[i][:, s, :], kikb[:, s, :], s == 0, s == 15, [QIZ[i], KIKB], [PT])
                                j = rr[0] % 2
                                rr[0] += 1
                                act(rl[j][:, 0:256], pt[:, 0:256], AF.Relu, [PT], [RL[j]])
                                if h == 0:
                                    ts(sc[:, kbk * 256:(kbk + 1) * 256], rl[j][:, 0:256], cw[:, 17, 0:1], None, ALU.mult, None, [RL[j], CW], [SC])
                                else:
                                    stt(sc[:, kbk * 256:(kbk + 1) * 256], rl[j][:, 0:256], cw[:, 17, h:h + 1], sc[:, kbk * 256:(kbk + 1) * 256],
                                        ALU.mult, ALU.add, [RL[j], CW, SC], [SC])
                        for h in range(4):
                            pt, PT = next_ps()
                            mm(pt[:, 0:128], qis[:, h, :], kib[0:64, TP:NT], True, True, [QIS, KIB], [PT])
                            j = rr[0] % 2
                            rr[0] += 1
                            act(rl[j][:, 0:128], pt[:, 0:128], AF.Relu, [PT], [RL[j]])
                            if h == 0:
                                ts(sc[:, 2048:2176], rl[j][:, 0:128], cw[:, 17, 0:1], None, ALU.mult, None, [RL[j], CW], [SC])
                            else:
                                stt(sc[:, 2048:2176], rl[j][:, 0:128], cw[:, 17, h:h + 1], sc[:, 2048:2176], ALU.mult, ALU.add, [RL[j], CW, SC], [SC])
                        kb.op("dve", lambda e: e.reduce_max(out=tkb[0][:, :], in_=sc[:, :], axis=AX.X, apply_absolute_value=True), [SC], [TKB])
                        tt(sc[:, 2048:2176], sc[:, 2048:2176], blkbias, ALU.add, [SC, MASKS], [SC])
                        topk_group([(128, 2176, sc, SC, m01, M01, tkb)])
                        for g0 in range(0, 17, 4):
                            pt, PT = next_ps()
                            ng = min(4, 17 - g0)
                            for q in range(ng):
                                tr(pb(pt)[:, q * 128:(q + 1) * 128], m01[:, (g0 + q) * 128:(g0 + q + 1) * 128], identb[:, :], [M01, IDENTB], [PT])
                            cp(maskT[:, g0:g0 + ng, :], pb(pt)[:, 0:128 * ng].rearrange("p (q t) -> p q t", t=128), [PT], [MT])
                        kb.barrier()
                    with kb.phase() as p3b:
                        kst = [sb(f"sk4{i}", [128, 4, 128], stack=p3b) for i in range(3)]
                        KST = [Buf(f"sk4{i}") for i in range(3)]
                        vst = [sb(f"sv4{i}", [128, 4, 128], stack=p3b) for i in range(3)]
                        VST = [Buf(f"sv4{i}") for i in range(3)]
                        kts = [sb(f"skts{i}", [128, 2048], BF16, stack=p3b) for i in range(2)]
                        KTS = [Buf(f"skts{i}") for i in range(2)]
                        vas = [sb(f"svas{i}", [128, 16, 2, 65], BF16, stack=p3b) for i in range(2)]
                        VAS = [Buf(f"svas{i}") for i in range(2)]
                        pP = [sb(f"sP{i}", [128, 4, 8], BF16, stack=p3b) for i in range(3)]
                        PP = [Buf(f"sP{i}") for i in range(3)]
                        osb = sb("sosb", [8, 8, 65], stack=p3b); OSB = Buf("sosb")
                        orc = sb("sorc", [8, 8], stack=p3b); ORC = Buf("sorc")
                        onb = sb("sonb", [8, 512], BF16, stack=p3b); ONB = Buf("sonb")
                        for i in range(2):
                            ms(vas[i][:, :, :, 64:65], 1.0, [VAS[i]])
                        sr = 0
                        pr = 0
                        for s in range(16):
                            b = s % 2
                            for g4 in range(4):
                                i = sr % 3
                                sr += 1
                                for pg in range(4):
                                    col = s * 16 + g4 * 4 + pg
                                    kb.dma("pool", kst[i][:, pg, :], cache_k_f, [PIDXL], [KST[i]], KST[i], group=(pg > 0),
                                           indirect=bass.IndirectOffsetOnAxis(ap=pidxl[:, col:col + 1], axis=0))
                                    kb.dma("pool", vst[i][:, pg, :], cache_v_f, [PIDXL], [VST[i]], VST[i], group=(pg > 0),
                                           indirect=bass.IndirectOffsetOnAxis(ap=pidxl[:, col:col + 1], axis=0))
                                pt, PT = next_ps()
                                for pg in range(4):
                                    tr(pt[:, pg * 128:(pg + 1) * 128], kst[i][:, pg, :], ident[:, :], [KST[i], IDENT], [PT])
                                act(kts[b][:, g4 * 512:(g4 + 1) * 512], pt[:, :], AF.Copy, [PT], [KTS[b]])
                                cp(vas[b][:, g4 * 4:g4 * 4 + 4, :, 0:64], vst[i][:, :, :].rearrange("p q (g d) -> p q g d", d=64), [VST[i]], [VAS[b]])
                            q0 = TP + 8 * s
                            steps = [(kt, g) for kt in range(17) for g in range(2)]

                            def emit_qk(step):
                                kt, g = step
                                pt, PT = next_ps()
                                if kt < 16:
                                    mm(pt[:, 0:32], kts[b][g * 64:g * 64 + 64, kt * 128:(kt + 1) * 128], qtc[g * 64:g * 64 + 64, :, q0:q0 + 8], True, True, [KTS[b], QTC], [PT])
                                else:
                                    mm(pt[:, 0:32], ktb[g * 64:g * 64 + 64, TP:NT], qtc[g * 64:g * 64 + 64, :, q0:q0 + 8], True, True, [KTB, QTC], [PT])
                                return pt, PT
                            pend = emit_qk(steps[0])
                            for si, (kt, g) in enumerate(steps):
                                pt, PT = pend
                                if si + 1 < len(steps):
                                    pend = emit_qk(steps[si + 1])
                                i = pr % 3
                                pr += 1
                                act(pP[i][:, :, :], pt[:, 0:32].rearrange("p (j t) -> p j t", j=4), AF.Exp, [PT], [PP[i]], scale=0.125)
                                tt(pP[i][:, :, :], pP[i][:, :, :], maskT[:, kt, 8 * s:8 * s + 8].unsqueeze(1).to_broadcast([128, 4, 8]), ALU.mult, [PP[i], MT], [PP[i]])
                                for jj in range(4):
                                    rhs = vas[b][:, kt, g, :] if kt < 16 else vaug[:, 17, g, :]
                                    mm(psum[6 + g][0:8, jj * 65:(jj + 1) * 65], pP[i][:, jj, :], rhs, (kt == 0 and jj == 0), (kt == 16),
                                       [PP[i], VAS[b], VAUG], [PS[6 + g]], skip_group_check=True)
                            for g in range(2):
                                act(osb[:, 4 * g:4 * g + 4, :], psum[6 + g][0:8, 0:260].rearrange("p (j d) -> p j d", d=65), AF.Copy, [PS[6 + g]], [OSB])
                            kb.op("dve", lambda e: e.reciprocal(out=orc[:, :], in_=osb[:, :, 64]), [OSB], [ORC])
                            tt(onb[:, :].rearrange("p (h d) -> p h d", d=64), osb[:, :, 0:64], orc[:, :].unsqueeze(2).to_broadcast([8, 8, 64]),
                               ALU.mult, [OSB, ORC], [ONB])
                            pt, PT = next_ps()
                            for j in range(4):
                                tr(pb(pt)[:, j * 128:j * 128 + 8], onb[:, j * 128:(j + 1) * 128], identb[0:8, 0:8], [ONB, IDENTB], [PT])
                            cp(oct_[:, :, q0:q0 + 8], pb(pt)[:, 0:512].rearrange("p (j t) -> p j t", t=128)[:, :, 0:8], [PT], [OCT])
                        kb.barrier()

                with kb.phase() as p4:
                    zs = [sb(f"czs{i}", [128, 448], stack=p4) for i in range(2)]
                    ZS = [Buf(f"czs{i}") for i in range(2)]
                    for j in range(4):
                        def ev_cz(pt, PT, bi, t0, n, j=j):
                            i = bi % 2
                            act(zs[i][:, 0:n], pt[:, 0:n], AF.Silu, [PT, BIASFM], [ZS[i]], bias=bias_col(l, f"cz{j}"), scale=1.0)
                            tt(oct_[:, j, t0:t0 + n], oct_[:, j, t0:t0 + n], zs[i][:, 0:n], ALU.mult, [OCT, ZS[i]], [OCT])
                        fm_unit(l, f"cz{j}", ev_cz)
                    branch_out(l, 2, oct_, OCT, w_pc[l])
                    kb.barrier()

        for l in range(n_layers):
            with kb.phase() as ph:
                rmsnorm_block(ph, l, normg, "n")
                kb.barrier()

            if debug in (None, "a"):
                phase_A(l)
            if debug in (None, "b"):
                phase_B(l)
            if debug in (None, "c"):
                phase_C(l)

            if debug:
                with kb.phase() as ph:
                    dsb = sb("dsb", [128, NT], stack=ph); DSB = Buf("dsb")
                    for dc in range(8):
                        cp(dsb[:, :], merged[:, dc, :], MG, [DSB])
                        kb.dma("sp", dbg[:, dc * NT:(dc + 1) * NT], dsb[:, :], [DSB], [], DSB)
                    kb.barrier()
                continue

            with kb.phase() as ph:
                hr = [sb(f"hr{i}", [128, 448], stack=ph) for i in range(8)]
                HR = [Buf(f"hr{i}") for i in range(8)]
                rr = 0
                for dc2 in range(8):
                    wt, WT = load_w(w_out[l], 8, [(dc2 * 128, 128, 0)])
                    for bi, (t0, n) in enumerate(TB):
                        i = rr % 8
                        rr += 1
                        kb.dma("sp", hr[i][:, 0:n], hT_d[dc2, :, t0:t0 + n], [HTD[dc2][bi]], [HR[i]], HR[i])
                        pt, PT = next_ps()
                        for c in range(8):
                            mm(pt[:, 0:n], wt[:, c, :], merged[:, c, t0:t0 + n], c == 0, c == 7, [WT, MG[bi]], [PT])
                        tt(hr[i][:, 0:n], hr[i][:, 0:n], pt[:, 0:n], ALU.add, [HR[i], PT], [HR[i]])
                        kb.dma("sp", hT_d[dc2, :, t0:t0 + n], hr[i][:, 0:n], [HR[i]], [HTD[dc2][bi]], HR[i])
                kb.barrier()

        if not debug:
            with kb.phase() as ph:
                hb = [sb(f"fhb{i}", [128, 8, 128], stack=ph) for i in range(2)]
                HB = [Buf(f"fhb{i}") for i in range(2)]
                sq = sb("fsq", [128, 8, 128], stack=ph); SQ = Buf("fsq")
                rs = sb("frs", [128, 128], stack=ph); RS = Buf("frs")
                rs2 = sb("frs2", [128, 128], stack=ph); RS2 = Buf("frs2")
                yo = [sb(f"yo{i}", [128, D], stack=ph) for i in range(2)]
                YO = [Buf(f"yo{i}") for i in range(2)]
                for ti, (c0, n, r0) in enumerate(all_tiles[1:]):
                    i = ti % 2
                    rd = [HTD[c][b] for c in range(8) for b in blocks_of(c0, n)]
                    kb.dma("sp", hb[i][:, :, :], hT_d[:, :, c0:c0 + n].rearrange("c p t -> p c t"), rd, [HB[i]], HB[i])
                    act(sq[:, :, :], hb[i][:, :, :], AF.Square, [HB[i]], [SQ])
                    pt, PT = next_ps()
                    for dc in range(8):
                        mm(pt[:, 0:n], ones_f[:, :], sq[:, dc, :], dc == 0, dc == 7, [ONESF, SQ], [PT])
                    act(rs[:, :], pt[:, 0:n], AF.Sqrt, [PT], [RS], bias=EPS, scale=1.0 / D)
                    kb.op("dve", lambda e: e.reciprocal(out=rs2[:, :], in_=rs[:, :]), [RS], [RS2])
                    for dc in range(8):
                        stt(hb[i][:, dc, :], hb[i][:, dc, :], fng[:, dc:dc + 1], rs2[:, :], ALU.mult, ALU.mult,
                            [HB[i], FNG, RS2], [HB[i]])
                    for hf in range(2):
                        pt, PT = next_ps()
                        for q in range(4):
                            tr(pt[:, q * 128:(q + 1) * 128], hb[i][:, hf * 4 + q, :], ident[:, :], [HB[i], IDENT], [PT])
                        act(yo[i][:, hf * 512:(hf + 1) * 512], pt[:, :], AF.Copy, [PT], [YO[i]])
                    dst = y_p[r0 - 16:r0 - 16 + n, :] if r0 is not None else y_s[:, :]
                    kb.dma("sp", dst, yo[i][:, :], [YO[i]], [], YO[i])
                kb.barrier()

        kb.finish()
        print(f"[kernel] emitted ~{kb.n_ins} instructions, {kb.nsem} semaphores")
    return nc


def _constants():
    ident = np.eye(128, dtype=np.float32)
    rmat = np.zeros((128, 128), np.float32)
    for base in (0, 64):
        for d in range(8):
            rmat[base + d + 8, base + d] = -1.0
            rmat[base + d, base + d + 8] = 1.0
    pos = np.concatenate([np.arange(TP), 2048 + (np.arange(NSM) % 8)]).astype(np.float32)
    inv = (np.float32(ROPE_THETA) ** (-np.arange(8, dtype=np.float32) * np.float32(2.0) / np.float32(16))).astype(np.float32)
    ang = pos[None, :] * inv[:, None]
    cos = np.ones((128, NT), np.float32)
    sin = np.zeros((128, NT), np.float32)
    for base in (0, 64):
        cos[base:base + 8] = np.cos(ang); cos[base + 8:base + 16] = np.cos(ang)
        sin[base:base + 8] = np.sin(ang); sin[base + 8:base + 16] = np.sin(ang)
    a = np.arange(128)
    tri_st = (a[:, None] <= a[None, :]).astype(np.float32)
    same = (a[:, None] // 8 == a[None, :] // 8)
    blk_st = (same & (a[:, None] <= a[None, :])).astype(np.float32)
    tribias = np.where(a[None, :] <= a[:, None], 0.0, NEG).astype(np.float32)
    blkbias = np.where(same & (a[None, :] <= a[:, None]), 0.0, NEG).astype(np.float32)
    masks = np.concatenate([tri_st, blk_st, tribias, blkbias], axis=1)
    seqsel = np.zeros((128, 16, 128), np.float32)
    for s in range(16):
        seqsel[:, s, 8 * s:8 * s + 8] = 1.0
    rowsel = (a[:, None] // 8 == np.arange(16)[None, :]).astype(np.float32)
    return ident, rmat, cos, sin, masks, seqsel.reshape(128, -1), rowsel


_NC_CACHE = {}
DEBUG = None


def kernel(x_prompt, x_sample, cache_k, cache_v, cache_kidx, state_hgrn, state_conv, page_table, meta_tokens, norm_g,
           w_in, b_in, lb_logits, hgrn_norm_g, conv_w, conv_b, conv_ln_g, conv_ln_b, conv_pw, w_pa, w_pb, w_pc, w_out,
           final_norm_g):
    f = lambda a: np.ascontiguousarray(np.asarray(a))
    (x_prompt, x_sample, cache_k, cache_v, cache_kidx, state_hgrn, state_conv, page_table, meta_tokens, norm_g, w_in, b_in,
     lb_logits, hgrn_norm_g, conv_w, conv_b, conv_ln_g, conv_ln_b, conv_pw, w_pa, w_pb, w_pc, w_out, final_norm_g) = map(f, (
        x_prompt, x_sample, cache_k, cache_v, cache_kidx, state_hgrn, state_conv, page_table, meta_tokens, norm_g, w_in, b_in,
        lb_logits, hgrn_norm_g, conv_w, conv_b, conv_ln_g, conv_ln_b, conv_pw, w_pa, w_pb, w_pc, w_out, final_norm_g))
    key = ("nc", DEBUG)
    if key not in _NC_CACHE:
        _NC_CACHE[key] = build_program(n_layers=1 if DEBUG else DEPTH, debug=DEBUG)
    nc = _NC_CACHE[key]
    ident, rmat, cos, sin, masks, seqsel, rowsel = _constants()
    normg_fm = np.ascontiguousarray(norm_g.reshape(DEPTH, 8, 128).transpose(2, 0, 1).reshape(128, DEPTH * 8))
    fng_fm = np.ascontiguousarray(final_norm_g.reshape(8, 128).T)
    bias_fm = np.zeros((128, DEPTH, len(FM_NAMES)), np.float32)
    for l in range(DEPTH):
        for j, n in enumerate(FM_NAMES):
            for (col, width, base) in FM_UNITS[n]:
                bias_fm[base:base + width, l, j] = b_in[l, col:col + width]
    bias_fm = bias_fm.reshape(128, -1)
    btok = np.ascontiguousarray(b_in.reshape(1, -1))
    convw_fm = np.ascontiguousarray(conv_w.reshape(DEPTH, 31, 4, 128).transpose(3, 0, 2, 1).reshape(128, DEPTH * 4 * 31))
    cvec = np.stack([conv_b, conv_ln_g, conv_ln_b], axis=1)
    cvec_fm = np.ascontiguousarray(cvec.reshape(DEPTH, 3, 4, 128).transpose(3, 0, 1, 2).reshape(128, DEPTH * 12))
    lbl_fm = np.ascontiguousarray(lb_logits.reshape(DEPTH, 4, 128).transpose(2, 1, 0).reshape(128, 16))
    ck = cache_k.reshape(DEPTH, N_PHYS * 128, 128)
    cv = cache_v.reshape(DEPTH, N_PHYS * 128, 128)
    cki = cache_kidx.reshape(DEPTH, N_PHYS * 128, 64)
    in_maps = []
    for c in range(8):
        in_maps.append({
            "xp": x_prompt[c], "xs": x_sample[16 * c:16 * c + 16].reshape(NSM, D), "meta": meta_tokens,
            "w_in": w_in, "conv_pw": conv_pw, "w_pa": w_pa, "w_pb": w_pb, "w_pc": w_pc, "w_out": w_out,
            "cache_k": ck, "cache_v": cv, "cache_ki": cki,
            "st_h": np.ascontiguousarray(state_hgrn[:, 16 * c:16 * c + 16]),
            "st_c": np.ascontiguousarray(state_conv[:, 16 * c:16 * c + 16]),
            "ptab": np.ascontiguousarray(page_table[16 * c:16 * c + 16].reshape(1, 256)).astype(np.int32),
            "gng": hgrn_norm_g,
            "c_ident": ident, "c_rmat": rmat, "c_cos": cos, "c_sin": sin, "c_normg": normg_fm, "c_fng": fng_fm,
            "c_bias": bias_fm, "c_btok": btok, "c_masks": masks, "c_seqsel": seqsel, "c_rowsel": rowsel,
            "c_convw": convw_fm, "c_cvec": cvec_fm, "c_lbl": lbl_fm,
        })
    if DEBUG:
        in_maps = in_maps[:1]
    res = run_bass_kernel_spmd(nc, in_maps, core_ids=list(range(len(in_maps))))
    R = res.results
    if DEBUG:
        return [np.asarray(r["dbg"]) for r in R]
    g = lambda k: np.stack([np.asarray(r[k]) for r in R])
    y_prompt = g("y_p")
    y_sample = g("y_s").reshape(128, 8, D)
    pk = g("p_k").transpose(1, 0, 2, 3).reshape(DEPTH, 8, TP, 2, 64)
    pv = g("p_v").transpose(1, 0, 2, 3).reshape(DEPTH, 8, TP, 2, 64)
    pki = g("p_ki").transpose(1, 0, 2, 3).reshape(DEPTH, 8, TP, 64)
    ph_ = g("p_h").transpose(1, 0, 2, 3, 4)
    pc_ = g("p_c").transpose(1, 0, 2, 3)
    sk = g("s_k").transpose(1, 0, 2, 3).reshape(DEPTH, 128, 8, 2, 64)
    sv = g("s_v").transpose(1, 0, 2, 3).reshape(DEPTH, 128, 8, 2, 64)
    ski = g("s_ki").transpose(1, 0, 2, 3).reshape(DEPTH, 128, 8, 64)
    sh = g("s_h").transpose(1, 0, 2, 3, 4, 5).reshape(DEPTH, 128, 4, 128, 128)
    sc_ = g("s_c").transpose(1, 0, 2, 3, 4).reshape(DEPTH, 128, 30, 512)
    c = np.ascontiguousarray
    return (c(y_prompt), c(y_sample), c(pk), c(pv), c(pki), c(ph_), c(pc_), c(sk), c(sv), c(ski), c(sh), c(sc_))
```

```python
import numpy as np
from contextlib import ExitStack, contextmanager
import concourse.bass as bass
import concourse.mybir as mybir
from concourse.bass_utils import run_bass_kernel_spmd

F32 = mybir.dt.float32
BF16 = mybir.dt.bfloat16
I32 = mybir.dt.int32
ALU = mybir.AluOpType
AF = mybir.ActivationFunctionType
AX = mybir.AxisListType

D = 1024
DEPTH = 4
TP = 2064
NSM = 128
NT = TP + NSM
TB = [(0, 448), (448, 448), (896, 448), (1344, 448), (1792, 400)]
N_IN = 8260
EPS = 1e-6
N_PHYS = 2560
ROPE_THETA = 500000.0
SEM_LIMIT = 30000

O_AQ, O_AF, O_AI, O_AZ = 0, 512, 1024, 1536
O_BA, O_BG, O_BZ = 2048, 2560, 3072
O_CQ, O_CK, O_CV, O_CQI, O_CKI, O_CW, O_CZ, O_GATE = 3584, 4096, 4224, 4352, 4608, 4672, 4676, 5188

FM_UNITS = {}
for h in range(4):
    FM_UNITS[f"aq{h}"] = [(O_AQ + 128 * h, 128, 0)]
    FM_UNITS[f"af{h}"] = [(O_AF + 128 * h, 128, 0)]
    FM_UNITS[f"az{h}"] = [(O_AZ + 128 * h, 128, 0)]
    FM_UNITS[f"ba{h}"] = [(O_BA + 128 * h, 128, 0)]
    FM_UNITS[f"bg{h}"] = [(O_BG + 128 * h, 128, 0)]
    FM_UNITS[f"bz{h}"] = [(O_BZ + 128 * h, 128, 0)]
    FM_UNITS[f"cq{h}"] = [(O_CQ + 64 * h, 64, 0), (O_CQ + 64 * (4 + h), 64, 64)]
    FM_UNITS[f"cz{h}"] = [(O_CZ + 128 * h, 128, 0)]
FM_UNITS["ck"] = [(O_CK, 128, 0)]
FM_UNITS["cqi0"] = [(O_CQI, 128, 0)]
FM_UNITS["cqi1"] = [(O_CQI + 128, 128, 0)]
FM_UNITS["cki"] = [(O_CKI, 64, 0), (O_CKI, 64, 64)]
for h in range(4):
    FM_UNITS[f"cqis{h}"] = [(O_CQI + 64 * h, 64, 0)]
for b in range(3):
    for j in range(8):
        FM_UNITS[f"g{b}_{j}"] = [(O_GATE + 1024 * b + 128 * j, 128, 0)]
FM_NAMES = list(FM_UNITS.keys())
FM_IDX = {n: i for i, n in enumerate(FM_NAMES)}


class Buf:
    __slots__ = ("name", "wr", "rd", "dsem", "dcnt")

    def __init__(self, name):
        self.name = name
        self.wr = []
        self.rd = {}
        self.dsem = None
        self.dcnt = 0


class KB:
    def __init__(self, nc, es):
        self.nc = nc
        self.es = es
        self.engs = {"pe": nc.tensor, "act": nc.scalar, "dve": nc.vector, "pool": nc.gpsimd, "sp": nc.sync}
        self.sem = {}
        self.cnt = {}
        self.seen = {k: {} for k in self.engs}
        self.nsem = 0
        self.dma_bufs = []
        self.n_ins = 0
        self.old_ev = {}
        self.free_sems = []
        for k in ("pe", "act", "dve", "pool"):
            self._rot(k)

    def new_sem(self, name):
        self.nsem += 1
        return self.es.enter_context(self.nc.semaphore(f"{name}_{self.nsem}"))

    def _rot(self, k):
        if k in self.sem:
            self.old_ev[k] = (self.sem[k], self.cnt[k])
        self.sem[k] = self.new_sem("e" + k)
        self.cnt[k] = 0

    def _wait(self, ek, ev):
        sem, val = ev
        key = id(sem)
        if ek == "pe" and sem is self.sem["pe"]:
            return
        if self.seen[ek].get(key, 0) >= val:
            return
        self.engs[ek].wait_ge(sem, val)
        self.seen[ek][key] = val
        self.n_ins += 1

    def _deps(self, ek, reads, writes):
        for b in reads:
            for ev in b.wr:
                self._wait(ek, ev)
        for b in writes:
            for ev in b.wr:
                self._wait(ek, ev)
            for ev in b.rd.values():
                self._wait(ek, ev)

    def _record(self, ev, reads, writes, accum=False):
        for b in writes:
            if accum:
                b.wr = [e for e in b.wr if e[0] is not ev[0]] + [ev]
            else:
                b.wr = [ev]
                b.rd = {}
        for b in reads:
            if b in writes:
                continue
            key = id(ev[0])
            old = b.rd.get(key)
            if old is None or old[1] < ev[1]:
                b.rd[key] = ev

    def op(self, ek, fn, reads=(), writes=(), accum=False):
        self._deps(ek, reads, writes)
        ins = fn(self.engs[ek])
        self.cnt[ek] += 1
        ins.then_inc(self.sem[ek], 1)
        ev = (self.sem[ek], self.cnt[ek])
        self._record(ev, reads, writes, accum=accum)
        self.n_ins += 1
        if self.cnt[ek] >= SEM_LIMIT:
            self._rot(ek)

    def dma(self, qk, out_ap, in_ap, reads, writes, sb, group=False, indirect=None, accum=False):
        if sb.dsem is None:
            self.pin(sb)
        self._deps(qk, reads, writes)
        if not group and sb.dcnt > 0:
            self._wait(qk, (sb.dsem, sb.dcnt))
        if indirect is not None:
            ins = self.engs[qk].indirect_dma_start(out=out_ap, out_offset=None, in_=in_ap, in_offset=indirect)
        else:
            ins = self.engs[qk].dma_start(out=out_ap, in_=in_ap)
        ins.then_inc(sb.dsem, 16)
        sb.dcnt += 16
        ev = (sb.dsem, sb.dcnt)
        self._record(ev, reads, writes, accum=accum)
        self.n_ins += 1

    def pin(self, b):
        if b.dsem is None:
            if self.free_sems:
                b.dsem, b.dcnt = self.free_sems.pop()
            else:
                b.dsem, b.dcnt = self.new_sem("d"), 0
            self.dma_bufs.append(b)

    @contextmanager
    def phase(self):
        start = len(self.dma_bufs)
        with ExitStack() as ph:
            yield ph
            self.barrier()
            for b in self.dma_bufs[start:]:
                self.free_sems.append((b.dsem, b.dcnt))
                b.dsem, b.dcnt = None, 0
            del self.dma_bufs[start:]

    def barrier(self):
        evs = [(self.sem[k], self.cnt[k]) if self.cnt[k] > 0 else self.old_ev.get(k) for k in ("pe", "act", "dve", "pool")]
        evs = [e for e in evs if e is not None]
        evs += [(b.dsem, b.dcnt) for b in self.dma_bufs if b.dcnt > 0]
        for ek in self.engs:
            for ev in evs:
                if ek in self.sem and ev[0] is self.sem[ek]:
                    continue
                self._wait(ek, ev)

    def finish(self):
        for b in self.dma_bufs:
            if b.dcnt > 0:
                self._wait("sp", (b.dsem, b.dcnt))


NIT = 15
NEG = -1.0e30
IDX_SCALE = (4 * 64) ** -0.5


def build_program(n_layers=DEPTH, debug=None):
    nc = bass.Bass("TRN2", target_bir_lowering=False)

    def din(name, shape, dt=F32):
        return nc.dram_tensor(name, list(shape), dt, kind="ExternalInput").ap()

    def dout(name, shape, dt=F32):
        return nc.dram_tensor(name, list(shape), dt, kind="ExternalOutput").ap()

    xp = din("xp", [2048, D]); xs = din("xs", [NSM, D]); meta = din("meta", [16, D])
    w_in = din("w_in", [DEPTH, D, N_IN])
    conv_pw = din("conv_pw", [DEPTH, 512, 512])
    w_pa = din("w_pa", [DEPTH, 512, D]); w_pb = din("w_pb", [DEPTH, 512, D]); w_pc = din("w_pc", [DEPTH, 512, D])
    w_out = din("w_out", [DEPTH, D, D])
    cache_k = din("cache_k", [DEPTH, N_PHYS * 128, 128]); cache_v = din("cache_v", [DEPTH, N_PHYS * 128, 128])
    cache_ki = din("cache_ki", [DEPTH, N_PHYS * 128, 64])
    st_h = din("st_h", [DEPTH, 16, 4, 128, 128]); st_c = din("st_c", [DEPTH, 16, 30, 512])
    ptab = din("ptab", [1, 256], I32)
    gng_d = din("gng", [DEPTH, 128])
    c_ident = din("c_ident", [128, 128]); c_rmat = din("c_rmat", [128, 128])
    c_cos = din("c_cos", [128, NT]); c_sin = din("c_sin", [128, NT])
    c_normg = din("c_normg", [128, DEPTH * 8]); c_fng = din("c_fng", [128, 8])
    c_bias = din("c_bias", [128, DEPTH * len(FM_NAMES)])
    c_btok = din("c_btok", [1, DEPTH * N_IN])
    c_masks = din("c_masks", [128, 4 * 128])
    c_seqsel = din("c_seqsel", [128, 16 * 128]); c_rowsel = din("c_rowsel", [128, 16])
    c_convw = din("c_convw", [128, DEPTH * 4 * 31]); c_cvec = din("c_cvec", [128, DEPTH * 12])
    c_lbl = din("c_lbl", [128, 16])

    y_p = dout("y_p", [2048, D]); y_s = dout("y_s", [NSM, D])
    p_k = dout("p_k", [DEPTH, TP, 128]); p_v = dout("p_v", [DEPTH, TP, 128]); p_ki = dout("p_ki", [DEPTH, TP, 64])
    p_h = dout("p_h", [DEPTH, 4, 128, 128]); p_c = dout("p_c", [DEPTH, 30, 512])
    s_k = dout("s_k", [DEPTH, NSM, 128]); s_v = dout("s_v", [DEPTH, NSM, 128]); s_ki = dout("s_ki", [DEPTH, NSM, 64])
    s_h = dout("s_h", [DEPTH, 16, 4, 128, 128]); s_c = dout("s_c", [DEPTH, 16, 30, 512])
    dbg = dout("dbg", [128, 8 * NT]) if debug else None

    hT_d = nc.dram_tensor("hT_d", [8, 128, NT], F32, kind="Internal").ap()
    HTD = [[Buf(f"htd{c}_{b}") for b in range(len(TB))] for c in range(8)]

    def blocks_of(c0, n):
        return [bi for bi, (t0, nn) in enumerate(TB) if t0 < c0 + n and c0 < t0 + nn]

    with ExitStack() as es:
        kb = KB(nc, es)

        sb_n = [0]

        def sb(name, shape, dt=F32, stack=None):
            sb_n[0] += 1
            return (stack or es).enter_context(nc.sbuf_tensor(f"{name}_{sb_n[0]}", list(shape), dt))

        def mm(out, lhsT, rhs, start, stop, reads, writes, **kw):
            kb.op("pe", lambda e: e.matmul(out, lhsT=lhsT, rhs=rhs, start=start, stop=stop, **kw), reads, writes)

        def tr(out, in_, idn, reads, writes):
            kb.op("pe", lambda e: e.transpose(out, in_, idn), reads, writes)

        def act(out, in_, func, reads, writes, **kw):
            kb.op("act", lambda e: e.activation(out=out, in_=in_, func=func, **kw), reads, writes)

        def tt(out, a, b, op, reads, writes, eng="dve"):
            kb.op(eng, lambda e: e.tensor_tensor(out=out, in0=a, in1=b, op=op), reads, writes)

        def ts(out, a, s1, s2, op0, op1, reads, writes, eng="dve", **kw):
            if s2 is None:
                kb.op(eng, lambda e: e.tensor_scalar(out=out, in0=a, scalar1=s1, scalar2=None, op0=op0, **kw), reads, writes)
            else:
                kb.op(eng, lambda e: e.tensor_scalar(out=out, in0=a, scalar1=s1, scalar2=s2, op0=op0, op1=op1, **kw), reads, writes)

        def stt(out, a, scalar, b, op0, op1, reads, writes):
            kb.op("dve", lambda e: e.scalar_tensor_tensor(out=out, in0=a, scalar=scalar, in1=b, op0=op0, op1=op1), reads, writes)

        def cp(out, in_, reads, writes, eng="dve"):
            kb.op(eng, lambda e: e.tensor_copy(out=out, in_=in_), reads, writes)

        def ms(ap, val, writes, eng="dve"):
            kb.op(eng, lambda e: e.memset(ap, val), [], writes)

        def const(name, shape, src, dt=F32):
            t = sb(name, shape, dt); B_ = Buf(name)
            kb.dma("sp", t[:], src, [], [B_], B_)
            return t, B_

        ident, IDENT = const("ident", [128, 128], c_ident)
        rmat, RMAT = const("rmat", [128, 128], c_rmat)
        normg, NORMG = const("normg", [128, DEPTH * 8], c_normg)
        fng, FNG = const("fng", [128, 8], c_fng)
        biasfm, BIASFM = const("biasfm", [128, DEPTH * len(FM_NAMES)], c_bias)
        masks, MASKS = const("masks", [128, 512], c_masks)
        rowsel, ROWSEL = const("rowsel", [128, 16], c_rowsel)
        convw, CONVW = const("convw", [128, DEPTH * 124], c_convw)
        cvec, CVEC = const("cvec", [128, DEPTH * 12], c_cvec)
        lbl, LBL = const("lbl", [128, 16], c_lbl)
        tri_st = masks[:, 0:128]; blk_st = masks[:, 128:256]; tribias = masks[:, 256:384]; blkbias = masks[:, 384:512]
        ones_f = sb("ones_f", [128, 128]); ONESF = Buf("ones_f")
        ones_b = sb("ones_b", [1, 128], BF16); ONESB = Buf("ones_b")
        identb = sb("identb", [128, 128], BF16); IDENTB = Buf("identb")
        half = sb("half", [128, 1]); HALF = Buf("half")
        neg29 = sb("neg29", [128, 1]); NEG29 = Buf("neg29")
        seqsel = sb("seqsel", [128, 16, 128], BF16); SEQSEL = Buf("seqsel")
        ms(ones_f[:], 1.0, [ONESF]); ms(ones_b[:], 1.0, [ONESB]); ms(half[:], 0.5, [HALF]); ms(neg29[:], -1.0e29, [NEG29])
        cp(identb[:], ident[:], [IDENT], [IDENTB])
        pow2 = sb("pow2", [128, NIT]); POW2 = Buf("pow2")
        for k_ in range(NIT):
            ms(pow2[:, k_:k_ + 1], 2.0 ** -(k_ + 1), [POW2])

        lbe = sb("lbe", [128, 4, 4]); LBE = Buf("lbe")
        lbm = sb("lbm", [128, 4]); LBM = Buf("lbm")
        lbv = sb("lbv", [128, 4, 4]); LBV = Buf("lbv")
        omlv = sb("omlv", [128, 4, 4]); OMLV = Buf("omlv")
        nomlv = sb("nomlv", [128, 4, 4]); NOMLV = Buf("nomlv")
        lb3 = lbl[:, :].rearrange("p (h l) -> p h l", l=4)
        kb.op("dve", lambda e: e.tensor_reduce(out=lbm[:, :], in_=lb3, axis=AX.X, op=ALU.max), [LBL], [LBM])
        tt(lbe[:], lb3, lbm[:, :].unsqueeze(2).to_broadcast([128, 4, 4]), ALU.subtract, [LBL, LBM], [LBE])
        act(lbe[:], lbe[:], AF.Exp, [LBE], [LBE])
        kb.op("dve", lambda e: e.tensor_reduce(out=lbm[:, :], in_=lbe[:], axis=AX.X, op=ALU.add), [LBE], [LBM])
        kb.op("dve", lambda e: e.reciprocal(out=lbm[:, :], in_=lbm[:, :]), [LBM], [LBM])
        tt(lbe[:], lbe[:], lbm[:, :].unsqueeze(2).to_broadcast([128, 4, 4]), ALU.mult, [LBE, LBM], [LBE])
        ms(lbv[:, :, 0:1], 0.0, [LBV])
        cp(lbv[:, :, 1:2], lbe[:, :, 1:2], [LBE], [LBV])
        tt(lbv[:, :, 2:3], lbv[:, :, 1:2], lbe[:, :, 2:3], ALU.add, [LBV, LBE], [LBV])
        tt(lbv[:, :, 3:4], lbv[:, :, 2:3], lbe[:, :, 3:4], ALU.add, [LBV, LBE], [LBV])
        ts(omlv[:], lbv[:], -1.0, 1.0, ALU.mult, ALU.add, [LBV], [OMLV])
        ts(nomlv[:], omlv[:], -1.0, None, ALU.mult, None, [OMLV], [NOMLV])
        with ExitStack() as ph0:
            ssf = sb("ssf", [128, 16, 128], stack=ph0); SSF = Buf("ssf")
            kb.dma("sp", ssf[:], c_seqsel.rearrange("p (s t) -> p s t", t=128), [], [SSF], SSF)
            cp(seqsel[:], ssf[:], [SSF], [SEQSEL])
            kb.barrier()

        ptb = sb("ptb", [128, 256], I32); PTB = Buf("ptb")
        iop = sb("iop", [128, 1], I32); IOP = Buf("iop")
        pidx = sb("pidx", [128, 256], I32); PIDX = Buf("pidx")
        kb.dma("sp", ptb[:], ptab.partition_broadcast(128), [], [PTB], PTB)
        kb.op("pool", lambda e: e.iota(iop[:], pattern=[[0, 1]], base=0, channel_multiplier=1), [], [IOP])
        ts(pidx[:], ptb[:], 128, iop[:, 0:1], ALU.mult, ALU.add, [PTB, IOP], [PIDX])

        hnT = sb("hnT", [128, 8, NT], BF16)
        HN = [Buf(f"hn{b}") for b in range(len(TB))]
        merged = sb("merged", [128, 8, NT], BF16)
        MG = [Buf(f"mg{b}") for b in range(len(TB))]

        psum = [es.enter_context(nc.psum_tensor(f"ps{i}", [128, 512], F32)) for i in range(8)]
        PS = [Buf(f"ps{i}") for i in range(8)]
        ps_rr = [0]

        def next_ps():
            i = ps_rr[0] % 6
            ps_rr[0] += 1
            return psum[i], PS[i]

        wst = [sb(f"wst{i}", [128, 8, 128]) for i in range(3)]
        WST = [Buf(f"wst{i}") for i in range(3)]
        for b_ in WST:
            kb.pin(b_)
        wbf = [sb(f"wbf{i}", [128, 8, 128], BF16) for i in range(3)]
        WBF = [Buf(f"wbf{i}") for i in range(3)]
        w_rr = [0]

        def load_w(w2d, kc, pieces):
            i = w_rr[0] % 3
            w_rr[0] += 1
            wv = w2d.rearrange("(c p) n -> p c n", p=128)
            first = True
            for (col, width, dst) in pieces:
                kb.dma("sp", wst[i][:, 0:kc, dst:dst + width], wv[:, :, col:col + width], [], [WST[i]], WST[i],
                       group=not first)
                first = False
            tot = max(d + w for (_, w, d) in pieces)
            cp(wbf[i][:, 0:kc, 0:tot], wst[i][:, 0:kc, 0:tot], [WST[i]], [WBF[i]], eng="pool")
            return wbf[i], WBF[i]

        def fm_unit(l, name, evac, blocks=None, pieces=None, rows=128):
            wt, WT = load_w(w_in[l], 8, pieces or FM_UNITS[name])
            for bi, (t0, n) in enumerate(TB):
                if blocks is not None and bi not in blocks:
                    continue
                pt, PT = next_ps()
                for dc in range(8):
                    mm(pt[0:rows, 0:n], wt[:, dc, 0:rows], hnT[:, dc, t0:t0 + n], dc == 0, dc == 7, [WT, HN[bi]], [PT])
                evac(pt, PT, bi, t0, n)

        def bias_col(l, name):
            j = l * len(FM_NAMES) + FM_IDX[name]
            return biasfm[:, j:j + 1]

        btok = sb("btok", [1, 128]); BTOK = Buf("btok")
        kb.pin(BTOK)
        btok_b = sb("btok_b", [1, 128], BF16); BTOKB = Buf("btok_b")

        def tok_unit(l, col, width, tiles, evac):
            wt, WT = load_w(w_in[l], 8, [(col, width, 0)])
            kb.dma("sp", btok[:, 0:width], c_btok[:, l * N_IN + col:l * N_IN + col + width], [], [BTOK], BTOK)
            cp(btok_b[:, 0:width], btok[:, 0:width], [BTOK], [BTOKB])
            for ti, (c0, n) in enumerate(tiles):
                pt, PT = next_ps()
                hb_ = [HN[b] for b in blocks_of(c0, n)]
                for dc in range(8):
                    mm(pt[0:n, 0:width], hnT[:, dc, c0:c0 + n], wt[:, dc, 0:width], dc == 0, False, [WT] + hb_, [PT])
                mm(pt[0:n, 0:width], ones_b[0:1, 0:n], btok_b[0:1, 0:width], False, True, [ONESB, BTOKB], [PT])
                evac(pt, PT, ti, c0, n)

        gsb = [sb(f"gsb{i}", [128, 448]) for i in range(2)]
        GSB = [Buf(f"gsb{i}") for i in range(2)]
        gtmp = sb("gtmp", [128, 448]); GTMP = Buf("gtmp")
        g_rr = [0]

        def branch_out(l, br, XT, XB, w2d):
            for dc in range(8):
                gname = f"g{br}_{dc}"
                wtg, WTG = load_w(w_in[l], 8, FM_UNITS[gname])
                wty, WTY = load_w(w2d, 4, [(dc * 128, 128, 0)])
                for bi, (t0, n) in enumerate(TB):
                    i = g_rr[0] % 2
                    g_rr[0] += 1
                    pg, PG = next_ps()
                    for c in range(8):
                        mm(pg[:, 0:n], wtg[:, c, :], hnT[:, c, t0:t0 + n], c == 0, c == 7, [WTG, HN[bi]], [PG])
                    if debug:
                        ms(gsb[i][:, 0:n], 1.0, [GSB[i]])
                    else:
                        act(gsb[i][:, 0:n], pg[:, 0:n], AF.Sigmoid, [PG, BIASFM], [GSB[i]], bias=bias_col(l, gname), scale=1.0)
                    py, PY = next_ps()
                    for c in range(4):
                        mm(py[:, 0:n], wty[:, c, :], XT[:, c, t0:t0 + n], c == 0, c == 3, [WTY, XB], [PY])
                    if br == 0 or debug:
                        tt(merged[:, dc, t0:t0 + n], py[:, 0:n], gsb[i][:, 0:n], ALU.mult, [PY, GSB[i]], [MG[bi]])
                    else:
                        tt(gtmp[:, 0:n], py[:, 0:n], gsb[i][:, 0:n], ALU.mult, [PY, GSB[i]], [GTMP])
                        tt(merged[:, dc, t0:t0 + n], merged[:, dc, t0:t0 + n], gtmp[:, 0:n], ALU.add, [MG[bi], GTMP], [MG[bi]])

        with kb.phase() as ph:
            xt = [sb(f"xt{i}", [128, D], stack=ph) for i in range(2)]
            XT_ = [Buf(f"xt{i}") for i in range(2)]
            xo = [sb(f"xo{i}", [128, 8, 128], stack=ph) for i in range(2)]
            XO = [Buf(f"xo{i}") for i in range(2)]
            tiles = [(meta, 0, 16, 0)] + [(xp, 128 * i, 128, 16 + 128 * i) for i in range(16)] + [(xs, 0, 128, TP)]
            for ti, (src, r0, nr, c0) in enumerate(tiles):
                i = ti % 2
                kb.dma("sp", xt[i][0:nr, :], src[r0:r0 + nr, :], [], [XT_[i]], XT_[i])
                for hf in range(2):
                    pt, PT = next_ps()
                    for q in range(4):
                        dc = hf * 4 + q
                        tr(pt[:, q * 128:q * 128 + nr], xt[i][0:nr, dc * 128:(dc + 1) * 128], ident[0:nr, 0:nr], [XT_[i], IDENT], [PT])
                    act(xo[i][:, hf * 4:hf * 4 + 4, 0:nr], pt[:, :].rearrange("p (q t) -> p q t", q=4)[:, :, 0:nr], AF.Copy, [PT], [XO[i]])
                wr = [HTD[c][b] for c in range(8) for b in blocks_of(c0, nr)]
                kb.dma("sp", hT_d[:, :, c0:c0 + nr].rearrange("c p t -> p c t"), xo[i][:, :, 0:nr], [XO[i]], wr, XO[i], accum=True)
            kb.barrier()

        tok_tiles = [(0, 16, 0)] + [(16 + 128 * i, 128, 16 + 128 * i) for i in range(16)]
        all_tiles = tok_tiles + [(TP, 128, None)]

        def rmsnorm_block(ph, l, gains, tag):
            hb = [sb(f"hb{i}{tag}", [128, 8, 448], stack=ph) for i in range(2)]
            HB = [Buf(f"hb{i}") for i in range(2)]
            sq = sb("sq" + tag, [128, 8, 448], stack=ph); SQ = Buf("sq")
            rs = sb("rs" + tag, [128, 448], stack=ph); RS = Buf("rs")
            rs2 = sb("rs2" + tag, [128, 448], stack=ph); RS2 = Buf("rs2")
            for bi, (t0, n) in enumerate(TB):
                i = bi % 2
                kb.dma("sp", hb[i][:, :, 0:n], hT_d[:, :, t0:t0 + n].rearrange("c p t -> p c t"),
                       [HTD[c][bi] for c in range(8)], [HB[i]], HB[i])
                act(sq[:, :, 0:n], hb[i][:, :, 0:n], AF.Square, [HB[i]], [SQ])
                pt, PT = next_ps()
                for dc in range(8):
                    mm(pt[:, 0:n], ones_f[:, :], sq[:, dc, 0:n], dc == 0, dc == 7, [ONESF, SQ], [PT])
                act(rs[:, 0:n], pt[:, 0:n], AF.Sqrt, [PT], [RS], bias=EPS, scale=1.0 / D)
                kb.op("dve", lambda e: e.reciprocal(out=rs2[:, 0:n], in_=rs[:, 0:n]), [RS], [RS2])
                for dc in range(8):
                    stt(hnT[:, dc, t0:t0 + n], hb[i][:, dc, 0:n], gains[:, l * 8 + dc:l * 8 + dc + 1], rs2[:, 0:n],
                        ALU.mult, ALU.mult, [HB[i], NORMG, RS2], [HN[bi]])

        ot_rr = [0]

        def rows_out(ot, OT, srcf, SRC, width, dst_of, tiles_, idn=None):
            for (c0, n, r0) in tiles_:
                pt, PT = next_ps()
                tr(pt[0:n, 0:128], srcf[:, c0:c0 + n], ident[:, :], [SRC, IDENT], [PT])
                j = ot_rr[0] % 3
                ot_rr[0] += 1
                act(ot[j][0:n, 0:width], pt[0:n, 0:width], AF.Copy, [PT], [OT[j]])
                kb.dma("sp", dst_of(r0, n), ot[j][0:n, 0:width], [OT[j]], [], OT[j])

        def phase_B(l):
            cv0 = l * 12
            rrb = [0]
            with kb.phase() as ph:
                cy = sb("cy", [128, 4, NT], stack=ph); CY = Buf("cy")
                sg = [sb(f"bsg{i}", [128, 448], stack=ph) for i in range(2)]
                SG = [Buf(f"bsg{i}") for i in range(2)]
                with kb.phase() as ph1:
                    up = sb("up", [128, 4, 30 + TP], stack=ph1); UP = Buf("up")
                    xps = sb("xps", [128, 4, 16, 38], stack=ph1); XPS = Buf("xps")
                    stc = [sb(f"stc{i}", [120, 512], stack=ph1) for i in range(2)]
                    STC = [Buf(f"stc{i}") for i in range(2)]
                    usm = sb("usm", [128, 4, 128], stack=ph1); USM = Buf("usm")
                    bot = [sb(f"bot{i}", [128, 512], stack=ph1) for i in range(2)]
                    BOT = [Buf(f"bot{i}") for i in range(2)]
                    DD = Buf("dd")
                    ms(up[:, :, 0:30], 0.0, [UP])
                    for g4 in range(4):
                        i = g4 % 2
                        kb.dma("sp", stc[i][:, :], st_c[l, 4 * g4:4 * g4 + 4, :, :].rearrange("s r c -> (s r) c"), [], [STC[i]], STC[i])
                        pt, PT = next_ps()
                        for j in range(4):
                            tr(pt[:, j * 128:j * 128 + 120], stc[i][:, j * 128:(j + 1) * 128], ident[0:120, 0:120], [STC[i], IDENT], [PT])
                        for j in range(4):
                            kb.op("act", lambda e: e.activation(out=xps[:, j, 4 * g4:4 * g4 + 4, 0:30],
                                                                in_=pt[:, j * 128:j * 128 + 120].rearrange("p (s r) -> p s r", r=30),
                                                                func=AF.Copy), [PT], [XPS], accum=True)
                    for j in range(4):
                        wta, WTA = load_w(w_in[l], 8, FM_UNITS[f"ba{j}"])
                        wtg, WTG = load_w(w_in[l], 8, FM_UNITS[f"bg{j}"])
                        for bi, (t0, n) in enumerate(TB):
                            i = rrb[0] % 2
                            rrb[0] += 1
                            pg, PG = next_ps()
                            for c in range(8):
                                mm(pg[:, 0:n], wtg[:, c, :], hnT[:, c, t0:t0 + n], c == 0, c == 7, [WTG, HN[bi]], [PG])
                            act(sg[i][:, 0:n], pg[:, 0:n], AF.Sigmoid, [PG, BIASFM], [SG[i]], bias=bias_col(l, f"bg{j}"), scale=1.0)
                            pa, PA = next_ps()
                            for c in range(8):
                                mm(pa[:, 0:n], wta[:, c, :], hnT[:, c, t0:t0 + n], c == 0, c == 7, [WTA, HN[bi]], [PA])
                            np_ = max(0, min(t0 + n, TP) - t0)
                            if np_ > 0:
                                stt(up[:, j, 30 + t0:30 + t0 + np_], pa[:, 0:np_], bias_col(l, f"ba{j}"), sg[i][:, 0:np_],
                                    ALU.add, ALU.mult, [PA, BIASFM, SG[i]], [UP])
                            if t0 + n > TP:
                                so = TP - t0
                                kb.op("dve", lambda e: e.scalar_tensor_tensor(
                                    out=xps[:, j, :, 30:38], in0=pa[:, so:so + 128].rearrange("p (s t) -> p s t", t=8),
                                    scalar=bias_col(l, f"ba{j}"), in1=sg[i][:, so:so + 128].rearrange("p (s t) -> p s t", t=8),
                                    op0=ALU.add, op1=ALU.mult), [PA, BIASFM, SG[i]], [XPS], accum=True)
                    for j in range(4):
                        wcol = lambda k: convw[:, l * 124 + j * 31 + k:l * 124 + j * 31 + k + 1]
                        bcol = cvec[:, cv0 + j:cv0 + j + 1]
                        ysv = cy[:, j, TP:NT].rearrange("p (s t) -> p s t", t=8)
                        ts(cy[:, j, 0:TP], up[:, j, 0:TP], wcol(0), bcol, ALU.mult, ALU.add, [UP, CONVW, CVEC], [CY])
                        ts(ysv, xps[:, j, :, 0:8], wcol(0), bcol, ALU.mult, ALU.add, [XPS, CONVW, CVEC], [CY])
                        for k in range(1, 31):
                            stt(cy[:, j, 0:TP], up[:, j, k:k + TP], wcol(k), cy[:, j, 0:TP], ALU.mult, ALU.add, [UP, CONVW, CY], [CY])
                            stt(ysv, xps[:, j, :, k:k + 8], wcol(k), ysv, ALU.mult, ALU.add, [XPS, CONVW, CY], [CY])
                    pt, PT = next_ps()
                    for j in range(4):
                        tr(pt[0:30, j * 128:(j + 1) * 128], up[:, j, TP:TP + 30], ident[:, :], [UP, IDENT], [PT])
                    act(bot[0][0:30, :], pt[0:30, :], AF.Copy, [PT], [BOT[0]])
                    kb.dma("sp", p_c[l, :, :], bot[0][0:30, :], [BOT[0]], [], BOT[0])
                    for j in range(4):
                        cp(usm[:, j, :].rearrange("p (s t) -> p s t", t=8), xps[:, j, :, 30:38], [XPS], [USM])
                    pt, PT = next_ps()
                    for j in range(4):
                        tr(pt[:, j * 128:(j + 1) * 128], usm[:, j, :], ident[:, :], [USM, IDENT], [PT])
                    act(bot[1][:, :], pt[:, :], AF.Copy, [PT], [BOT[1]])
                    for s in range(16):
                        kb.dma("sp", s_c[l, s, 22:30, :], bot[1][8 * s:8 * s + 8, :], [BOT[1]], [], BOT[1], group=(s > 0))
                    kb.dma("sp", s_c[l, :, 0:22, :], st_c[l, :, 8:30, :], [], [], DD)
                    kb.barrier()
                with kb.phase() as ph2:
                    yn = sb("yn", [128, 4, NT], BF16, stack=ph2); YN = Buf("yn")
                    zb = sb("zb", [128, 4, NT], BF16, stack=ph2); ZB = Buf("zb")
                    ysq = sb("ysq", [128, 4, 448], stack=ph2); YSQ = Buf("ysq")
                    mean = sb("mean", [128, 448], stack=ph2); MEAN = Buf("mean")
                    msq = sb("msq", [128, 448], stack=ph2); MSQ = Buf("msq")
                    rstd = sb("rstd", [128, 448], stack=ph2); RSTD = Buf("rstd")
                    dtm = [sb(f"dtm{i}", [128, 448], stack=ph2) for i in range(2)]
                    DTM = [Buf(f"dtm{i}") for i in range(2)]
                    for bi, (t0, n) in enumerate(TB):
                        act(ysq[:, :, 0:n], cy[:, :, t0:t0 + n], AF.Square, [CY], [YSQ])
                        p1, P1 = next_ps()
                        for j in range(4):
                            mm(p1[:, 0:n], ones_f[:, :], cy[:, j, t0:t0 + n], j == 0, j == 3, [ONESF, CY], [P1])
                        p2, P2 = next_ps()
                        for j in range(4):
                            mm(p2[:, 0:n], ones_f[:, :], ysq[:, j, 0:n], j == 0, j == 3, [ONESF, YSQ], [P2])
                        act(mean[:, 0:n], p1[:, 0:n], AF.Identity, [P1], [MEAN], scale=1.0 / 512)
                        tt(msq[:, 0:n], mean[:, 0:n], mean[:, 0:n], ALU.mult, [MEAN], [MSQ])
                        stt(msq[:, 0:n], p2[:, 0:n], 1.0 / 512, msq[:, 0:n], ALU.mult, ALU.subtract, [P2, MSQ], [MSQ])
                        act(rstd[:, 0:n], msq[:, 0:n], AF.Sqrt, [MSQ], [RSTD], bias=EPS, scale=1.0)
                        kb.op("dve", lambda e: e.reciprocal(out=rstd[:, 0:n], in_=rstd[:, 0:n]), [RSTD], [RSTD])
                        for j in range(4):
                            i = j % 2
                            tt(dtm[i][:, 0:n], cy[:, j, t0:t0 + n], mean[:, 0:n], ALU.subtract, [CY, MEAN], [DTM[i]])
                            tt(dtm[i][:, 0:n], dtm[i][:, 0:n], rstd[:, 0:n], ALU.mult, [DTM[i], RSTD], [DTM[i]])
                            act(yn[:, j, t0:t0 + n], dtm[i][:, 0:n], AF.Silu, [DTM[i], CVEC], [YN],
                                scale=cvec[:, cv0 + 4 + j:cv0 + 5 + j], bias=cvec[:, cv0 + 8 + j:cv0 + 9 + j])
                    for j in range(4):
                        wtz, WTZ = load_w(w_in[l], 8, FM_UNITS[f"bz{j}"])
                        wtp, WTP = load_w(conv_pw[l], 4, [(j * 128, 128, 0)])
                        for bi, (t0, n) in enumerate(TB):
                            i = rrb[0] % 2
                            rrb[0] += 1
                            pz, PZ = next_ps()
                            for c in range(8):
                                mm(pz[:, 0:n], wtz[:, c, :], hnT[:, c, t0:t0 + n], c == 0, c == 7, [WTZ, HN[bi]], [PZ])
                            act(sg[i][:, 0:n], pz[:, 0:n], AF.Silu, [PZ, BIASFM], [SG[i]], bias=bias_col(l, f"bz{j}"), scale=1.0)
                            pp, PP = next_ps()
                            for c in range(4):
                                mm(pp[:, 0:n], wtp[:, c, :], yn[:, c, t0:t0 + n], c == 0, c == 3, [WTP, YN], [PP])
                            tt(zb[:, j, t0:t0 + n], pp[:, 0:n], sg[i][:, 0:n], ALU.mult, [PP, SG[i]], [ZB])
                    branch_out(l, 1, zb, ZB, w_pb[l])
                    kb.barrier()

        def phase_A(l):
            pb = lambda p: p[:, :].bitcast(BF16)
            chunks = [(0, 16)] + [(16 + 64 * c, 64) for c in range(32)]
            with kb.phase() as ph:
                xa = sb("xa", [128, 4, NT], BF16, stack=ph); XA = Buf("xa")
                gng = sb("gngt", [128, 128], stack=ph); GNG = Buf("gng")
                kb.dma("sp", gng[:], gng_d[l:l + 1, :].partition_broadcast(128), [], [GNG], GNG)
                T = [sb(f"aT{i}", [128, NT], stack=ph) for i in range(4)]
                TT = [Buf(f"aT{i}") for i in range(4)]
                qt_ = sb("aqt", [128, NT], BF16, stack=ph); QT = Buf("aqt")
                kt_ = sb("akt", [128, NT], BF16, stack=ph); KT = Buf("akt")
                kd_ = sb("akd", [128, NT], BF16, stack=ph); KD = Buf("akd")
                az_ = sb("aaz", [128, NT], BF16, stack=ph); AZ = Buf("aaz")
                qs = [sb(f"aqs{i}", [128, 448], stack=ph) for i in range(2)]
                QS = [Buf(f"aqs{i}") for i in range(2)]
                r1 = sb("aR1", [128, 4352], BF16, stack=ph); R1 = Buf("aR1")
                r2 = sb("aR2", [128, 4224], BF16, stack=ph); R2 = Buf("aR2")
                r3 = sb("aR3", [128, 4224], BF16, stack=ph); R3 = Buf("aR3")
                V = r1[0:64, :].rearrange("p (c v) -> p c v", v=128)
                SO = r1[:, 0:4096].bitcast(F32).rearrange("p (s v) -> p s v", v=128)
                KDT = r2[0:64, :].rearrange("p (c v) -> p c v", v=128)
                SOb = r2[:, 0:2048].rearrange("p (s v) -> p s v", v=128)
                QZ = r2[:, 2048:4096].rearrange("p (s v) -> p s v", v=128)
                OR = r3[0:64, :].rearrange("p (c v) -> p c v", v=128)
                VZ = r3[:, 0:2048].rearrange("p (s v) -> p s v", v=128)
                vs = sb("avs", [128, 128], BF16, stack=ph); VS = Buf("avs")
                kdts = sb("akdts", [128, 128], BF16, stack=ph); KDTS = Buf("akdts")
                ors = sb("aors", [128, 128], BF16, stack=ph); ORS = Buf("aors")
                attm = [sb(f"attm{i}", [128, 128], BF16, stack=ph) for i in range(2)]
                ATT = [Buf(f"attm{i}") for i in range(2)]
                junk = sb("ajunk", [128, 128], BF16, stack=ph); JUNK = Buf("ajunk")
                ss = sb("ass", [128, 34], stack=ph); SS = Buf("ass")
                rsa = sb("arsa", [128, 34], stack=ph); RSA = Buf("arsa")
                S = sb("aS", [128, 128], stack=ph); S_ = Buf("aS")
                Sb2 = [sb(f"aSb{i}", [128, 128], BF16, stack=ph) for i in range(2)]
                SB2 = [Buf(f"aSb{i}") for i in range(2)]
                ebl = sb("aebl", [128, 49], stack=ph); EBL = Buf("aebl")
                bs = sb("abs", [128, 49], stack=ph); BS = Buf("abs")
                bl = sb("abl", [128, 49], stack=ph); BL = Buf("abl")
                rr = [0]

                def views(t):
                    return (t[:, 16:TP].rearrange("p (c t) -> p c t", t=64), t[:, TP:NT].rearrange("p (s t) -> p s t", t=8))

                for hd in range(4):
                    lbc = lbv[:, hd, l:l + 1]; omlc = omlv[:, hd, l:l + 1]; nomlc = nomlv[:, hd, l:l + 1]
                    B = T[0]

                    def ev_f(pt, PT, bi, t0, n):
                        act(T[0][:, t0:t0 + n], pt[:, 0:n], AF.Sigmoid, [PT, BIASFM], [TT[0]], bias=bias_col(l, f"af{hd}"), scale=1.0)
                    fm_unit(l, f"af{hd}", ev_f)
                    act(T[1][:, :], T[0][:, :], AF.Ln, [TT[0], OMLV, LBV], [TT[1]], scale=omlc, bias=lbc)
                    ts(T[2][:, :], T[0][:, :], nomlc, omlc, ALU.mult, ALU.add, [TT[0], NOMLV, OMLV], [TT[2]])
                    onesbc = ones_f[:, 0:1]
                    kb.op("dve", lambda e: e.tensor_tensor_scan(out=T[0][:, 0:TP], data0=onesbc.to_broadcast([128, TP]),
                                                                 data1=T[1][:, 0:TP], initial=0.0, op0=ALU.mult, op1=ALU.add),
                          [TT[1], ONESF, TT[0]], [TT[0]])
                    kb.op("dve", lambda e: e.tensor_tensor_scan(out=T[0][:, TP:NT], data0=onesbc.to_broadcast([128, NSM]),
                                                                 data1=T[1][:, TP:NT], initial=0.0, op0=ALU.mult, op1=ALU.add),
                          [TT[1], ONESF, TT[0]], [TT[0]])
                    ms(bs[:, 0:1], 0.0, [BS]); ms(bs[:, 33:34], 0.0, [BS])
                    cp(bs[:, 1:33], B[:, 15:15 + 64 * 32:64], [TT[0]], [BS])
                    cp(bs[:, 34:49], B[:, TP + 7:TP + 7 + 8 * 15:8], [TT[0]], [BS])
                    cp(bl[:, 0:1], B[:, 15:16], [TT[0]], [BL])
                    cp(bl[:, 1:33], B[:, 79:79 + 64 * 32:64], [TT[0]], [BL])
                    cp(bl[:, 33:49], B[:, TP + 7:NT:8], [TT[0]], [BL])
                    bx, bsm = views(B)
                    t1x, t1s = views(T[1])
                    cp(T[1][:, 0:16], B[:, 0:16], [TT[0]], [TT[1]])
                    tt(t1x, bx, bs[:, 1:33].unsqueeze(2).to_broadcast([128, 32, 64]), ALU.subtract, [TT[0], BS], [TT[1]])
                    tt(t1s, bsm, bs[:, 33:49].unsqueeze(2).to_broadcast([128, 16, 8]), ALU.subtract, [TT[0], BS], [TT[1]])
                    act(T[3][:, :], T[1][:, :], AF.Exp, [TT[1]], [TT[3]])
                    act(T[1][:, :], T[1][:, :], AF.Exp, [TT[1]], [TT[1]], scale=-1.0)
                    tt(kt_[:, :], T[2][:, :], T[1][:, :], ALU.mult, [TT[2], TT[1]], [KT])
                    tt(T[1][:, 0:16], B[:, 0:16], bl[:, 0:1].to_broadcast([128, 16]), ALU.subtract, [TT[0], BL, KT], [TT[1]])
                    tt(t1x, bx, bl[:, 1:33].unsqueeze(2).to_broadcast([128, 32, 64]), ALU.subtract, [TT[0], BL], [TT[1]])
                    tt(t1s, bsm, bl[:, 33:49].unsqueeze(2).to_broadcast([128, 16, 8]), ALU.subtract, [TT[0], BL], [TT[1]])
                    act(T[1][:, :], T[1][:, :], AF.Exp, [TT[1]], [TT[1]], scale=-1.0)
                    tt(kd_[:, :], T[2][:, :], T[1][:, :], ALU.mult, [TT[2], TT[1]], [KD])
                    cp(ebl[:, 0:1], T[3][:, 15:16], [TT[3]], [EBL])
                    cp(ebl[:, 1:33], T[3][:, 79:79 + 64 * 32:64], [TT[3]], [EBL])
                    cp(ebl[:, 33:49], T[3][:, TP + 7:NT:8], [TT[3]], [EBL])

                    def ev_q(pt, PT, bi, t0, n):
                        i = rr[0] % 2
                        rr[0] += 1
                        act(qs[i][:, 0:n], pt[:, 0:n], AF.Silu, [PT, BIASFM], [QS[i]], bias=bias_col(l, f"aq{hd}"), scale=1.0)
                        tt(qt_[:, t0:t0 + n], qs[i][:, 0:n], T[3][:, t0:t0 + n], ALU.mult, [QS[i], TT[3]], [QT])
                    fm_unit(l, f"aq{hd}", ev_q)

                    def ev_z(pt, PT, bi, t0, n):
                        act(az_[:, t0:t0 + n], pt[:, 0:n], AF.Silu, [PT, BIASFM], [AZ], bias=bias_col(l, f"az{hd}"), scale=1.0)
                    fm_unit(l, f"az{hd}", ev_z)

                    def ev_v(pt, PT, ti, c0, n):
                        if ti < 33:
                            act(V[0:n, ti, :], pt[0:n, 0:128], AF.Copy, [PT], [R1])
                        else:
                            act(vs[:, :], pt[:, 0:128], AF.Copy, [PT], [VS])
                    tok_unit(l, O_AI + 128 * hd, 128, chunks + [(TP, 128)], ev_v)

                    pt, PT = next_ps()
                    tr(pb(pt)[0:16, 0:128], kd_[:, 0:16], identb[:, :], [KD, IDENTB], [PT])
                    act(KDT[0:16, 0, :], pb(pt)[0:16, 0:128], AF.Copy, [PT], [R2])
                    for g in range(8):
                        pt, PT = next_ps()
                        for q in range(4):
                            c0 = 16 + 64 * (4 * g + q)
                            tr(pb(pt)[0:64, q * 128:(q + 1) * 128], kd_[:, c0:c0 + 64], identb[:, :], [KD, IDENTB], [PT])
                        act(KDT[0:64, 1 + 4 * g:5 + 4 * g, :], pb(pt)[0:64, 0:512].rearrange("p (q v) -> p q v", v=128), AF.Copy, [PT], [R2])
                    pt, PT = next_ps()
                    tr(pb(pt)[:, 0:128], kd_[:, TP:NT], identb[:, :], [KD, IDENTB], [PT])
                    act(kdts[:, :], pb(pt)[:, 0:128], AF.Copy, [PT], [KDTS])

                    ms(S[:, :], 0.0, [S_]); ms(Sb2[0][:, :], 0.0, [SB2[0]]); ms(ss[:, :], 1.0, [SS])
                    for ci, (c0, n) in enumerate(chunks):
                        i = ci % 2
                        cur, CUR = Sb2[ci % 2], SB2[ci % 2]
                        nxt, NXT = Sb2[(ci + 1) % 2], SB2[(ci + 1) % 2]
                        pS, PSB = next_ps()
                        mm(pS[:, 0:128], KDT[0:n, ci, :], V[0:n, ci, :], True, True, [R2, R1], [PSB])
                        pa, PA = next_ps()
                        mm(pa[0:n, 0:n], kt_[:, c0:c0 + n], qt_[:, c0:c0 + n], True, True, [KT, QT], [PA])
                        stt(nxt[:, :], S[:, :], ebl[:, ci:ci + 1], pS[:, 0:128], ALU.mult, ALU.add, [S_, EBL, PSB], [NXT])
                        stt(S[:, :], S[:, :], ebl[:, ci:ci + 1], pS[:, 0:128], ALU.mult, ALU.add, [S_, EBL, PSB], [S_])
                        tt(attm[i][0:n, 0:n], pa[0:n, 0:n], tri_st[0:n, 0:n], ALU.mult, [PA, MASKS], [ATT[i]])
                        po, PO = next_ps()
                        mm(po[0:n, 0:128], qt_[:, c0:c0 + n], cur[:, :], True, False, [QT, CUR], [PO])
                        mm(po[0:n, 0:128], attm[i][0:n, 0:n], V[0:n, ci, :], False, True, [ATT[i], R1], [PO])
                        act(OR[0:n, ci, :], po[0:n, 0:128], AF.Copy, [PO], [R3])
                        act(junk[0:n, :], po[0:n, 0:128], AF.Square, [PO], [JUNK, SS], accum_out=ss[0:n, ci:ci + 1])
                    kb.dma("sp", p_h[l, hd, :, :], S[:, :], [S_], [], S_)
                    act(rsa[0:64, 0:33], ss[0:64, 0:33], AF.Sqrt, [SS], [RSA], bias=EPS, scale=1.0 / 128)
                    kb.op("dve", lambda e: e.reciprocal(out=rsa[0:64, 0:33], in_=rsa[0:64, 0:33]), [RSA], [RSA])
                    tt(OR[:, :, :], OR[:, :, :], rsa[0:64, 0:33].unsqueeze(2).to_broadcast([64, 33, 128]), ALU.mult, [R3, RSA], [R3])
                    tt(OR[:, :, :], OR[:, :, :], gng[0:64, :].unsqueeze(1).to_broadcast([64, 33, 128]), ALU.mult, [R3, GNG], [R3])
                    pt, PT = next_ps()
                    tr(pb(pt)[:, 0:16], OR[0:16, 0, :], identb[0:16, 0:16], [R3, IDENTB], [PT])
                    tt(xa[:, hd, 0:16], pb(pt)[:, 0:16], az_[:, 0:16], ALU.mult, [PT, AZ], [XA])
                    for g in range(4):
                        pt, PT = next_ps()
                        for q in range(8):
                            tr(pb(pt)[:, q * 64:(q + 1) * 64], OR[0:64, 1 + 8 * g + q, :], identb[0:64, 0:64], [R3, IDENTB], [PT])
                        c0 = 16 + 512 * g
                        tt(xa[:, hd, c0:c0 + 512], pb(pt)[:, 0:512], az_[:, c0:c0 + 512], ALU.mult, [PT, AZ], [XA])

                    kb.dma("sp", SO, st_h[l, :, hd, :, :].rearrange("s k v -> k s v"), [], [R1], R1)
                    cp(SOb, SO, [R1], [R2], eng="pool")
                    kb.op("dve", lambda e: e.tensor_tensor(out=QZ, in0=qt_[:, TP:NT].unsqueeze(1).to_broadcast([128, 16, 128]),
                                                           in1=seqsel[:, :, :], op=ALU.mult), [QT, SEQSEL], [R2], accum=True)
                    tt(VZ, vs[:, :].unsqueeze(1).to_broadcast([128, 16, 128]), rowsel[:, :].unsqueeze(2).to_broadcast([128, 16, 128]),
                       ALU.mult, [VS, ROWSEL], [R3])
                    pa, PA = next_ps()
                    mm(pa[:, 0:128], kt_[:, TP:NT], qt_[:, TP:NT], True, True, [KT, QT], [PA])
                    tt(attm[0][:, :], pa[:, 0:128], blk_st, ALU.mult, [PA, MASKS], [ATT[0]])
                    po, PO = next_ps()
                    for s in range(16):
                        mm(po[:, 0:128], QZ[:, s, :], SOb[:, s, :], s == 0, False, [R2], [PO])
                    mm(po[:, 0:128], attm[0][:, :], vs[:, :], False, True, [ATT[0], VS], [PO])
                    act(ors[:, :], po[:, 0:128], AF.Copy, [PO], [ORS])
                    act(junk[:, :], po[:, 0:128], AF.Square, [PO], [JUNK, SS], accum_out=ss[:, 33:34])
                    for s in range(16):
                        pS, PSB = next_ps()
                        mm(pS[:, 0:128], kdts[:, :], VZ[:, s, :], True, True, [KDTS, R3], [PSB])
                        stt(SO[:, s, :], SO[:, s, :], ebl[:, 33 + s:34 + s], pS[:, 0:128], ALU.mult, ALU.add, [R1, EBL, PSB], [R1])
                    kb.dma("sp", s_h[l, :, hd, :, :].rearrange("s k v -> k s v"), SO, [R1], [], R1)
                    act(rsa[:, 33:34], ss[:, 33:34], AF.Sqrt, [SS], [RSA], bias=EPS, scale=1.0 / 128)
                    kb.op("dve", lambda e: e.reciprocal(out=rsa[:, 33:34], in_=rsa[:, 33:34]), [RSA], [RSA])
                    ts(ors[:, :], ors[:, :], rsa[:, 33:34], None, ALU.mult, None, [ORS, RSA], [ORS])
                    tt(ors[:, :], ors[:, :], gng[:, :], ALU.mult, [ORS, GNG], [ORS])
                    pt, PT = next_ps()
                    tr(pb(pt)[:, 0:128], ors[:, :], identb[:, :], [ORS, IDENTB], [PT])
                    tt(xa[:, hd, TP:NT], pb(pt)[:, 0:128], az_[:, TP:NT], ALU.mult, [PT, AZ], [XA])
                branch_out(l, 0, xa, XA, w_pa[l])
                kb.barrier()

        def topk_group(items):
            act_items = []
            for (nq, S, sc, SC, m01, M01, st) in items:
                (amax, mid, htab, cnt, gh, lo, TKB) = st
                if S <= 256:
                    continue
                ts(gh[0:nq, :], amax[0:nq, :], 2.0, 1.0, ALU.mult, ALU.add, [TKB], [TKB])
                ts(htab[0:nq, :], pow2[0:nq, :], gh[0:nq, 0:1], None, ALU.mult, None, [POW2, TKB], [TKB])
                ms(mid[0:nq, :], -0.5, [TKB])
                act_items.append((nq, S, sc, SC, m01, M01, st))
            for it in range(NIT):
                for (nq, S, sc, SC, m01, M01, st) in act_items:
                    (amax, mid, htab, cnt, gh, lo, TKB) = st
                    kb.op("dve", lambda e: e.tensor_scalar(out=m01[0:nq, 0:S], in0=sc[0:nq, 0:S], scalar1=mid[0:nq, 0:1], scalar2=None,
                                                           op0=ALU.is_gt, op1=ALU.add, accum_out=cnt[0:nq, 0:1]), [SC, TKB], [TKB, M01])
                for (nq, S, sc, SC, m01, M01, st) in act_items:
                    (amax, mid, htab, cnt, gh, lo, TKB) = st
                    ts(gh[0:nq, :], cnt[0:nq, :], 255.5, htab[0:nq, it:it + 1], ALU.is_gt, ALU.mult, [TKB], [TKB])
                for (nq, S, sc, SC, m01, M01, st) in act_items:
                    (amax, mid, htab, cnt, gh, lo, TKB) = st
                    if it < NIT - 1:
                        stt(mid[0:nq, :], gh[0:nq, :], htab[0:nq, it + 1:it + 2], mid[0:nq, :], ALU.subtract, ALU.add, [TKB], [TKB])
                    else:
                        stt(lo[0:nq, :], gh[0:nq, :], htab[0:nq, it:it + 1], mid[0:nq, :], ALU.subtract, ALU.add, [TKB], [TKB])
            for (nq, S, sc, SC, m01, M01, st) in items:
                (amax, mid, htab, cnt, gh, lo, TKB) = st
                thr = neg29[0:nq, 0:1] if S <= 256 else lo[0:nq, 0:1]
                ts(m01[0:nq, 0:S], sc[0:nq, 0:S], thr, None, ALU.is_gt, None, [SC, TKB, NEG29], [M01])

        def tk_state(stack, tag):
            return (sb(f"tka{tag}", [128, 1], stack=stack), sb(f"tkm{tag}", [128, 1], stack=stack), sb(f"tkh{tag}", [128, NIT], stack=stack),
                    sb(f"tkc{tag}", [128, 1], stack=stack), sb(f"tkg{tag}", [128, 1], stack=stack), sb(f"tkl{tag}", [128, 1], stack=stack),
                    Buf(f"tk{tag}"))

        def phase_C(l):
            pb = lambda p: p[:, :].bitcast(BF16)
            with kb.phase() as ph:
                qtc = sb("cqt", [128, 4, NT], BF16, stack=ph); QTC = Buf("cqt")
                qi = sb("cqi", [128, 2, NT], BF16, stack=ph); QI = Buf("cqi")
                qis = sb("cqis", [64, 4, 128], BF16, stack=ph); QIS = Buf("cqis")
                ktb = sb("cktb", [128, NT], BF16, stack=ph); KTB = Buf("cktb")
                kib = sb("ckib", [128, NT], BF16, stack=ph); KIB = Buf("ckib")
                vaug = sb("cvaug", [128, 18, 2, 65], BF16, stack=ph); VAUG = Buf("cvaug")
                cw = sb("ccw", [128, 18, 4], stack=ph); CW = Buf("ccw")
                oct_ = sb("coct", [128, 4, NT], BF16, stack=ph); OCT = Buf("coct")
                with kb.phase() as p1:
                    cosT = sb("cosT", [128, NT], stack=p1); COS = Buf("cos")
                    sinT = sb("sinT", [128, NT], stack=p1); SIN = Buf("sin")
                    kb.dma("sp", cosT[:], c_cos, [], [COS], COS)
                    kb.dma("sp", sinT[:], c_sin, [], [SIN], SIN)
                    xf = [sb(f"xf{i}", [128, 448], stack=p1) for i in range(2)]
                    XF = [Buf(f"xf{i}") for i in range(2)]
                    t1 = sb("t1", [128, 448], stack=p1); T1 = Buf("t1")
                    t2 = sb("t2", [128, 448], stack=p1); T2 = Buf("t2")
                    kTf = sb("kTf", [128, NT], stack=p1); KTF = Buf("kTf")
                    kiTf = sb("kiTf", [128, NT], stack=p1); KITF = Buf("kiTf")
                    ot = [sb(f"ot{i}", [128, 128], stack=p1) for i in range(3)]
                    OT = [Buf(f"ot{i}") for i in range(3)]
                    vtok = [sb(f"vtok{i}", [128, 128], stack=p1) for i in range(2)]
                    VTOK = [Buf(f"vtok{i}") for i in range(2)]

                    def rope_evac(dst_of, DST, name, rows=128):
                        def evac(pt, PT, bi, t0, n):
                            i = bi % 2
                            act(xf[i][0:rows, 0:n], pt[0:rows, 0:n], AF.Identity, [PT, BIASFM], [XF[i]], bias=bias_col(l, name)[0:rows, :], scale=1.0)
                            p2, P2 = next_ps()
                            mm(p2[0:rows, 0:n], rmat[0:rows, 0:rows], xf[i][0:rows, 0:n], True, True, [RMAT, XF[i]], [P2])
                            tt(t1[0:rows, 0:n], xf[i][0:rows, 0:n], cosT[0:rows, t0:t0 + n], ALU.mult, [XF[i], COS], [T1])
                            tt(t2[0:rows, 0:n], p2[0:rows, 0:n], sinT[0:rows, t0:t0 + n], ALU.mult, [P2, SIN], [T2])
                            tt(dst_of(t0, n), t1[0:rows, 0:n], t2[0:rows, 0:n], ALU.add, [T1, T2], [DST])
                        return evac

                    fm_unit(l, "ck", rope_evac(lambda t0, n: kTf[:, t0:t0 + n], KTF, "ck"))
                    fm_unit(l, "cki", rope_evac(lambda t0, n: kiTf[:, t0:t0 + n], KITF, "cki"))
                    cp(ktb[:, :], kTf[:, :], [KTF], [KTB], eng="pool")
                    cp(kib[:, :], kiTf[:, :], [KITF], [KIB], eng="pool")
                    rows_out(ot, OT, kTf, KTF, 128, lambda r0, n: (p_k[l, r0:r0 + n, :] if r0 is not None else s_k[l, :, :]), all_tiles)
                    rows_out(ot, OT, kiTf, KITF, 64, lambda r0, n: (p_ki[l, r0:r0 + n, :] if r0 is not None else s_ki[l, :, :]), all_tiles)
                    for j in range(4):
                        fm_unit(l, f"cq{j}", rope_evac(lambda t0, n, j=j: qtc[:, j, t0:t0 + n], QTC, f"cq{j}"))
                    for j in range(2):
                        fm_unit(l, f"cqi{j}", rope_evac(lambda t0, n, j=j: qi[:, j, t0:t0 + n], QI, f"cqi{j}"))
                    for h in range(4):
                        def evq(pt, PT, bi, t0, n, h=h):
                            so = TP - t0
                            i = bi % 2
                            act(xf[i][0:64, 0:128], pt[0:64, so:so + 128], AF.Identity, [PT, BIASFM], [XF[i]],
                                bias=bias_col(l, f"cqis{h}")[0:64, :], scale=1.0)
                            p2, P2 = next_ps()
                            mm(p2[0:64, 0:128], rmat[0:64, 0:64], xf[i][0:64, 0:128], True, True, [RMAT, XF[i]], [P2])
                            tt(t1[0:64, 0:128], xf[i][0:64, 0:128], cosT[0:64, TP:NT], ALU.mult, [XF[i], COS], [T1])
                            tt(t2[0:64, 0:128], p2[0:64, 0:128], sinT[0:64, TP:NT], ALU.mult, [P2, SIN], [T2])
                            tt(qis[:, h, :], t1[0:64, 0:128], t2[0:64, 0:128], ALU.add, [T1, T2], [QIS])
                        fm_unit(l, f"cqis{h}", evq, blocks={4}, rows=64)

                    ms(vaug[:, :, :, 64:65], 1.0, [VAUG])

                    def ev_v(pt, PT, ti, c0, n):
                        i = ti % 2
                        act(vtok[i][0:n, :], pt[0:n, 0:128], AF.Copy, [PT], [VTOK[i]])
                        cp(vaug[0:n, ti, :, 0:64], vtok[i][0:n, :].rearrange("p (g d) -> p g d", d=64), [VTOK[i]], [VAUG])
                        r0 = all_tiles[ti][2]
                        dst = p_v[l, r0:r0 + n, :] if r0 is not None else s_v[l, :, :]
                        kb.dma("sp", dst, vtok[i][0:n, :], [VTOK[i]], [], VTOK[i])
                    tok_unit(l, O_CV, 128, [(c0, n) for (c0, n, _) in all_tiles], ev_v)

                    def ev_w(pt, PT, ti, c0, n):
                        act(cw[0:n, ti, :], pt[0:n, 0:4], AF.Identity, [PT], [CW], scale=IDX_SCALE)
                    tok_unit(l, O_CW, 4, [(c0, n) for (c0, n, _) in all_tiles], ev_w)
                    kb.barrier()

                with kb.phase() as p2s:
                    NSL = 2
                    scs = [sb(f"csc{i}", [128, TP], stack=p2s) for i in range(NSL)]; SCS = [Buf(f"csc{i}") for i in range(NSL)]
                    m01s = [sb(f"cm01{i}", [128, TP], BF16, stack=p2s) for i in range(NSL)]; M01S = [Buf(f"cm01{i}") for i in range(NSL)]
                    mTs = [sb(f"cmT{i}", [128, 17, 128], BF16, stack=p2s) for i in range(NSL)]; MTS = [Buf(f"cmT{i}") for i in range(NSL)]
                    sts = [tk_state(p2s, f"c{i}") for i in range(NSL)]
                    rl = [sb(f"crl{i}", [128, 512], stack=p2s) for i in range(2)]
                    RL = [Buf(f"crl{i}") for i in range(2)]
                    pP = [sb(f"cP{i}", [128, 4, 128], BF16, stack=p2s) for i in range(3)]
                    PP = [Buf(f"cP{i}") for i in range(3)]
                    osb = sb("cosb", [128, 8, 65], stack=p2s); OSB = Buf("cosb")
                    orc = sb("corc", [128, 8], stack=p2s); ORC = Buf("corc")
                    onb = sb("conb", [128, 512], BF16, stack=p2s); ONB = Buf("conb")
                    rr = [0]
                    pr = [0]
                    for grp0 in range(0, len(tok_tiles), NSL):
                        grp_tiles = list(enumerate(tok_tiles))[grp0:grp0 + NSL]
                        items = []
                        for slot, (qt_i, (c0, nq, _)) in enumerate(grp_tiles):
                            sc, SC, m01, M01, st = scs[slot], SCS[slot], m01s[slot], M01S[slot], sts[slot]
                            S = c0 + nq
                            for h in range(4):
                                b0 = (h % 2) * 64
                                for k0 in range(0, S, 512):
                                    kn = min(512, S - k0)
                                    i = rr[0] % 2
                                    rr[0] += 1
                                    pt, PT = next_ps()
                                    mm(pt[0:nq, 0:kn], qi[b0:b0 + 64, h // 2, c0:c0 + nq], kib[b0:b0 + 64, k0:k0 + kn], True, True, [QI, KIB], [PT])
                                    act(rl[i][0:nq, 0:kn], pt[0:nq, 0:kn], AF.Relu, [PT], [RL[i]])
                                    if h == 0:
                                        ts(sc[0:nq, k0:k0 + kn], rl[i][0:nq, 0:kn], cw[0:nq, qt_i, 0:1], None, ALU.mult, None, [RL[i], CW], [SC])
                                    else:
                                        stt(sc[0:nq, k0:k0 + kn], rl[i][0:nq, 0:kn], cw[0:nq, qt_i, h:h + 1], sc[0:nq, k0:k0 + kn],
                                            ALU.mult, ALU.add, [RL[i], CW, SC], [SC])
                            if S > 256:
                                kb.op("dve", lambda e: e.reduce_max(out=st[0][0:nq, :], in_=sc[0:nq, 0:S], axis=AX.X, apply_absolute_value=True),
                                      [SC], [st[-1]])
                            tt(sc[0:nq, c0:c0 + nq], sc[0:nq, c0:c0 + nq], tribias[0:nq, 0:nq], ALU.add, [SC, MASKS], [SC])
                            items.append((nq, S, sc, SC, m01, M01, st))
                        topk_group(items)
                        for slot, (qt_i, (c0, nq, _)) in enumerate(grp_tiles):
                            m01, M01, maskT, MT = m01s[slot], M01S[slot], mTs[slot], MTS[slot]
                            ktiles = [(0, 16)] + [(16 + 128 * i, 128) for i in range(qt_i)]
                            for g0 in range(0, len(ktiles), 4):
                                pt, PT = next_ps()
                                grp = ktiles[g0:g0 + 4]
                                for q, (kc0, nk) in enumerate(grp):
                                    tr(pb(pt)[0:nk, q * 128:q * 128 + nq], m01[0:nq, kc0:kc0 + nk], identb[0:nq, 0:nq], [M01, IDENTB], [PT])
                                if g0 == 0:
                                    cp(maskT[0:16, 0, 0:nq], pb(pt)[0:16, 0:nq], [PT], [MT])
                                    if len(grp) > 1:
                                        cp(maskT[:, 1:len(grp), 0:nq], pb(pt)[:, 128:128 * len(grp)].rearrange("p (q t) -> p q t", t=128)[:, :, 0:nq], [PT], [MT])
                                else:
                                    cp(maskT[:, g0:g0 + len(grp), 0:nq], pb(pt)[:, 0:128 * len(grp)].rearrange("p (q t) -> p q t", t=128)[:, :, 0:nq], [PT], [MT])
                            steps = [(kt, kc0, nk, g) for kt, (kc0, nk) in enumerate(ktiles) for g in range(2)]

                            def emit_qk(step):
                                kt, kc0, nk, g = step
                                pt, PT = next_ps()
                                mm(pt[0:nk, 0:4 * nq], ktb[g * 64:g * 64 + 64, kc0:kc0 + nk], qtc[g * 64:g * 64 + 64, :, c0:c0 + nq], True, True, [KTB, QTC], [PT])
                                return pt, PT
                            pend = emit_qk(steps[0])
                            for si, (kt, kc0, nk, g) in enumerate(steps):
                                pt, PT = pend
                                if si + 1 < len(steps):
                                    pend = emit_qk(steps[si + 1])
                                i = pr[0] % 3
                                pr[0] += 1
                                act(pP[i][0:nk, :, 0:nq], pt[0:nk, 0:4 * nq].rearrange("p (j t) -> p j t", j=4), AF.Exp, [PT], [PP[i]], scale=0.125)
                                tt(pP[i][0:nk, :, 0:nq], pP[i][0:nk, :, 0:nq], maskT[0:nk, kt, 0:nq].unsqueeze(1).to_broadcast([nk, 4, nq]), ALU.mult, [PP[i], MT], [PP[i]])
                                for jj in range(4):
                                    mm(psum[6 + g][0:nq, jj * 65:(jj + 1) * 65], pP[i][0:nk, jj, 0:nq], vaug[0:nk, kt, g, :],
                                       (kt == 0 and jj == 0), (kt == len(ktiles) - 1), [PP[i], VAUG], [PS[6 + g]], skip_group_check=True)
                            for g in range(2):
                                act(osb[0:nq, 4 * g:4 * g + 4, :], psum[6 + g][0:nq, 0:260].rearrange("p (j d) -> p j d", d=65), AF.Copy, [PS[6 + g]], [OSB])
                            kb.op("dve", lambda e: e.reciprocal(out=orc[0:nq, :], in_=osb[0:nq, :, 64]), [OSB], [ORC])
                            tt(onb[0:nq, :].rearrange("p (h d) -> p h d", d=64), osb[0:nq, :, 0:64], orc[0:nq, :].unsqueeze(2).to_broadcast([nq, 8, 64]),
                               ALU.mult, [OSB, ORC], [ONB])
                            pt, PT = next_ps()
                            for j in range(4):
                                tr(pb(pt)[:, j * 128:j * 128 + nq], onb[0:nq, j * 128:(j + 1) * 128], identb[0:nq, 0:nq], [ONB, IDENTB], [PT])
                            cp(oct_[:, :, c0:c0 + nq], pb(pt)[:, 0:512].rearrange("p (j t) -> p j t", t=128)[:, :, 0:nq], [PT], [OCT])
                    kb.barrier()

                with kb.phase() as p3:
                    maskT = sb("smT", [128, 17, 128], BF16, stack=p3); MT = Buf("smT")
                    rr = [0]
                    pidxl = sb("pidxl", [128, 256], I32, stack=p3); PIDXL = Buf("pidxl")
                    ts(pidxl[:, :], pidx[:, :], l * N_PHYS * 128, None, ALU.add, None, [PIDX], [PIDXL])
                    cache_ki_f = cache_ki.rearrange("l r c -> (l r) c")
                    cache_k_f = cache_k.rearrange("l r c -> (l r) c")
                    cache_v_f = cache_v.rearrange("l r c -> (l r) c")
                    with kb.phase() as p3a:
                        sc = sb("ssc", [128, 2176], stack=p3a); SC = Buf("ssc")
                        m01 = sb("sm01", [128, 2176], BF16, stack=p3a); M01 = Buf("sm01")
                        tkb = tk_state(p3a, "s")
                        TKB = tkb[-1]
                        rl = [sb(f"srl{i}", [128, 512], stack=p3a) for i in range(2)]
                        RL = [Buf(f"srl{i}") for i in range(2)]
                        kikb = sb("skikb", [64, 16, 256], BF16, stack=p3a); KIKB = Buf("skikb")
                        qiz = [sb(f"sqiz{i}", [64, 16, 128], BF16, stack=p3a) for i in range(1)]
                        QIZ = [Buf(f"sqiz{i}") for i in range(1)]
                        kst = [sb(f"skst{i}", [128, 2, 64], stack=p3a) for i in range(12)]
                        KST = [Buf(f"skst{i}") for i in range(12)]
                        sr = 0
                        for kbk in range(8):
                            for s2 in range(0, 16, 2):
                                pt, PT = next_ps()
                                for sq_ in range(2):
                                    s = s2 + sq_
                                    i = sr % 12
                                    sr += 1
                                    for pg in range(2):
                                        col = s * 16 + kbk * 2 + pg
                                        kb.dma("pool", kst[i][:, pg, :], cache_ki_f, [PIDXL], [KST[i]], KST[i], group=(pg > 0),
                                               indirect=bass.IndirectOffsetOnAxis(ap=pidxl[:, col:col + 1], axis=0))
                                    for pg in range(2):
                                        tr(pt[0:64, (sq_ * 2 + pg) * 128:(sq_ * 2 + pg + 1) * 128], kst[i][:, pg, :], ident[:, :], [KST[i], IDENT], [PT])
                                act(kikb[:, s2:s2 + 2, :], pt[0:64, 0:512].rearrange("p (s k) -> p s k", k=256), AF.Copy, [PT], [KIKB])
                            for h in range(4):
                                i = 0
                                tt(qiz[i][:, :, :], qis[:, h, :].unsqueeze(1).to_broadcast([64, 16, 128]), seqsel[0:64, :, :], ALU.mult, [QIS, SEQSEL], [QIZ[i]])
                                pt, PT = next_ps()
                                for s in range(16):
                                    mm(pt[:, 0:256], qiz[i][:, s, :], kikb[:, s, :], s == 0, s == 15, [QIZ[i], KIKB], [PT])
                                j = rr[0] % 2
                                rr[0] += 1
                                act(rl[j][:, 0:256], pt[:, 0:256], AF.Relu, [PT], [RL[j]])
                                if h == 0:
                                    ts(sc[:, kbk * 256:(kbk + 1) * 256], rl[j][:, 0:256], cw[:, 17, 0:1], None, ALU.mult, None, [RL[j], CW], [SC])
                                else:
                                    stt(sc[:, kbk * 256:(kbk + 1) * 256], rl[j][:, 0:256], cw[:, 17, h:h + 1], sc[:, kbk * 256:(kbk + 1) * 256],
                                        ALU.mult, ALU.add, [RL[j], CW, SC], [SC])
                        for h in range(4):
                            pt, PT = next_ps()
                            mm(pt[:, 0:128], qis[:, h, :], kib[0:64, TP:NT], True, True, [QIS, KIB], [PT])
                            j = rr[0] % 2
                            rr[0] += 1
                            act(rl[j][:, 0:128], pt[:, 0:128], AF.Relu, [PT], [RL[j]])
                            if h == 0:
                                ts(sc[:, 2048:2176], rl[j][:, 0:128], cw[:, 17, 0:1], None, ALU.mult, None, [RL[j], CW], [SC])
                            else:
                                stt(sc[:, 2048:2176], rl[j][:, 0:128], cw[:, 17, h:h + 1], sc[:, 2048:2176], ALU.mult, ALU.add, [RL[j], CW, SC], [SC])
                        kb.op("dve", lambda e: e.reduce_max(out=tkb[0][:, :], in_=sc[:, :], axis=AX.X, apply_absolute_value=True), [SC], [TKB])
                        tt(sc[:, 2048:2176], sc[:, 2048:2176], blkbias, ALU.add, [SC, MASKS], [SC])
                        topk_group([(128, 2176, sc, SC, m01, M01, tkb)])
                        for g0 in range(0, 17, 4):
                            pt, PT = next_ps()
                            ng = min(4, 17 - g0)
                            for q in range(ng):
                                tr(pb(pt)[:, q * 128:(q + 1) * 128], m01[:, (g0 + q) * 128:(g0 + q + 1) * 128], identb[:, :], [M01, IDENTB], [PT])
                            cp(maskT[:, g0:g0 + ng, :], pb(pt)[:, 0:128 * ng].rearrange("p (q t) -> p q t", t=128), [PT], [MT])
                        kb.barrier()
                    with kb.phase() as p3b:
                        kst = [sb(f"sk4{i}", [128, 4, 128], stack=p3b) for i in range(4)]
                        KST = [Buf(f"sk4{i}") for i in range(4)]
                        vst = [sb(f"sv4{i}", [128, 4, 128], stack=p3b) for i in range(4)]
                        VST = [Buf(f"sv4{i}") for i in range(4)]
                        kts = [sb(f"skts{i}", [128, 2048], BF16, stack=p3b) for i in range(2)]
                        KTS = [Buf(f"skts{i}") for i in range(2)]
                        vas = [sb(f"svas{i}", [128, 16, 2, 65], BF16, stack=p3b) for i in range(2)]
                        VAS = [Buf(f"svas{i}") for i in range(2)]
                        pP = [sb(f"sP{i}", [128, 4, 8], BF16, stack=p3b) for i in range(3)]
                        PP = [Buf(f"sP{i}") for i in range(3)]
                        osb = sb("sosb", [8, 8, 65], stack=p3b); OSB = Buf("sosb")
                        orc = sb("sorc", [8, 8], stack=p3b); ORC = Buf("sorc")
                        onb = sb("sonb", [8, 512], BF16, stack=p3b); ONB = Buf("sonb")
                        for i in range(2):
                            ms(vas[i][:, :, :, 64:65], 1.0, [VAS[i]])
                        sr = 0
                        pr = 0
                        for s in range(16):
                            b = s % 2
                            for g4 in range(4):
                                i = sr % 4
                                sr += 1
                                for pg in range(4):
                                    col = s * 16 + g4 * 4 + pg
                                    kb.dma("pool", kst[i][:, pg, :], cache_k_f, [PIDXL], [KST[i]], KST[i], group=(pg > 0),
                                           indirect=bass.IndirectOffsetOnAxis(ap=pidxl[:, col:col + 1], axis=0))
                                    kb.dma("pool", vst[i][:, pg, :], cache_v_f, [PIDXL], [VST[i]], VST[i], group=(pg > 0),
                                           indirect=bass.IndirectOffsetOnAxis(ap=pidxl[:, col:col + 1], axis=0))
                                pt, PT = next_ps()
                                for pg in range(4):
                                    tr(pt[:, pg * 128:(pg + 1) * 128], kst[i][:, pg, :], ident[:, :], [KST[i], IDENT], [PT])
                                act(kts[b][:, g4 * 512:(g4 + 1) * 512], pt[:, :], AF.Copy, [PT], [KTS[b]])
                                cp(vas[b][:, g4 * 4:g4 * 4 + 4, :, 0:64], vst[i][:, :, :].rearrange("p q (g d) -> p q g d", d=64), [VST[i]], [VAS[b]])
                            q0 = TP + 8 * s
                            steps = [(kt, g) for kt in range(17) for g in range(2)]

                            def emit_qk(step):
                                kt, g = step
                                pt, PT = next_ps()
                                if kt < 16:
                                    mm(pt[:, 0:32], kts[b][g * 64:g * 64 + 64, kt * 128:(kt + 1) * 128], qtc[g * 64:g * 64 + 64, :, q0:q0 + 8], True, True, [KTS[b], QTC], [PT])
                                else:
                                    mm(pt[:, 0:32], ktb[g * 64:g * 64 + 64, TP:NT], qtc[g * 64:g * 64 + 64, :, q0:q0 + 8], True, True, [KTB, QTC], [PT])
                                return pt, PT
                            pend = emit_qk(steps[0])
                            for si, (kt, g) in enumerate(steps):
                                pt, PT = pend
                                if si + 1 < len(steps):
                                    pend = emit_qk(steps[si + 1])
                                i = pr % 3
                                pr += 1
                                act(pP[i][:, :, :], pt[:, 0:32].rearrange("p (j t) -> p j t", j=4), AF.Exp, [PT], [PP[i]], scale=0.125)
                                tt(pP[i][:, :, :], pP[i][:, :, :], maskT[:, kt, 8 * s:8 * s + 8].unsqueeze(1).to_broadcast([128, 4, 8]), ALU.mult, [PP[i], MT], [PP[i]])
                                for jj in range(4):
                                    rhs = vas[b][:, kt, g, :] if kt < 16 else vaug[:, 17, g, :]
                                    mm(psum[6 + g][0:8, jj * 65:(jj + 1) * 65], pP[i][:, jj, :], rhs, (kt == 0 and jj == 0), (kt == 16),
                                       [PP[i], VAS[b], VAUG], [PS[6 + g]], skip_group_check=True)
                            for g in range(2):
                                act(osb[:, 4 * g:4 * g + 4, :], psum[6 + g][0:8, 0:260].rearrange("p (j d) -> p j d", d=65), AF.Copy, [PS[6 + g]], [OSB])
                            kb.op("dve", lambda e: e.reciprocal(out=orc[:, :], in_=osb[:, :, 64]), [OSB], [ORC])
                            tt(onb[:, :].rearrange("p (h d) -> p h d", d=64), osb[:, :, 0:64], orc[:, :].unsqueeze(2).to_broadcast([8, 8, 64]),
                               ALU.mult, [OSB, ORC], [ONB])
                            pt, PT = next_ps()
                            for j in range(4):
                                tr(pb(pt)[:, j * 128:j * 128 + 8], onb[:, j * 128:(j + 1) * 128], identb[0:8, 0:8], [ONB, IDENTB], [PT])
                            cp(oct_[:, :, q0:q0 + 8], pb(pt)[:, 0:512].rearrange("p (j t) -> p j t", t=128)[:, :, 0:8], [PT], [OCT])
                        kb.barrier()

                with kb.phase() as p4:
                    zs = [sb(f"czs{i}", [128, 448], stack=p4) for i in range(2)]
                    ZS = [Buf(f"czs{i}") for i in range(2)]
                    for j in range(4):
                        def ev_cz(pt, PT, bi, t0, n, j=j):
                            i = bi % 2
                            act(zs[i][:, 0:n], pt[:, 0:n], AF.Silu, [PT, BIASFM], [ZS[i]], bias=bias_col(l, f"cz{j}"), scale=1.0)
                            tt(oct_[:, j, t0:t0 + n], oct_[:, j, t0:t0 + n], zs[i][:, 0:n], ALU.mult, [OCT, ZS[i]], [OCT])
                        fm_unit(l, f"cz{j}", ev_cz)
                    branch_out(l, 2, oct_, OCT, w_pc[l])
                    kb.barrier()

        for l in range(n_layers):
            with kb.phase() as ph:
                rmsnorm_block(ph, l, normg, "n")
                kb.barrier()

            if debug in (None, "a"):
                phase_A(l)
            if debug in (None, "b"):
                phase_B(l)
            if debug in (None, "c"):
                phase_C(l)

            if debug:
                with kb.phase() as ph:
                    dsb = sb("dsb", [128, NT], stack=ph); DSB = Buf("dsb")
                    for dc in range(8):
                        cp(dsb[:, :], merged[:, dc, :], MG, [DSB])
                        kb.dma("sp", dbg[:, dc * NT:(dc + 1) * NT], dsb[:, :], [DSB], [], DSB)
                    kb.barrier()
                continue

            with kb.phase() as ph:
                hr = [sb(f"hr{i}", [128, 448], stack=ph) for i in range(8)]
                HR = [Buf(f"hr{i}") for i in range(8)]
                rr = 0
                for dc2 in range(8):
                    wt, WT = load_w(w_out[l], 8, [(dc2 * 128, 128, 0)])
                    for bi, (t0, n) in enumerate(TB):
                        i = rr % 8
                        rr += 1
                        kb.dma("sp", hr[i][:, 0:n], hT_d[dc2, :, t0:t0 + n], [HTD[dc2][bi]], [HR[i]], HR[i])
                        pt, PT = next_ps()
                        for c in range(8):
                            mm(pt[:, 0:n], wt[:, c, :], merged[:, c, t0:t0 + n], c == 0, c == 7, [WT, MG[bi]], [PT])
                        tt(hr[i][:, 0:n], hr[i][:, 0:n], pt[:, 0:n], ALU.add, [HR[i], PT], [HR[i]])
                        kb.dma("sp", hT_d[dc2, :, t0:t0 + n], hr[i][:, 0:n], [HR[i]], [HTD[dc2][bi]], HR[i])
                kb.barrier()

        if not debug:
            with kb.phase() as ph:
                hb = [sb(f"fhb{i}", [128, 8, 128], stack=ph) for i in range(2)]
                HB = [Buf(f"fhb{i}") for i in range(2)]
                sq = sb("fsq", [128, 8, 128], stack=ph); SQ = Buf("fsq")
                rs = sb("frs", [128, 128], stack=ph); RS = Buf("frs")
                rs2 = sb("frs2", [128, 128], stack=ph); RS2 = Buf("frs2")
                yo = [sb(f"yo{i}", [128, D], stack=ph) for i in range(2)]
                YO = [Buf(f"yo{i}") for i in range(2)]
                for ti, (c0, n, r0) in enumerate(all_tiles[1:]):
                    i = ti % 2
                    rd = [HTD[c][b] for c in range(8) for b in blocks_of(c0, n)]
                    kb.dma("sp", hb[i][:, :, :], hT_d[:, :, c0:c0 + n].rearrange("c p t -> p c t"), rd, [HB[i]], HB[i])
                    act(sq[:, :, :], hb[i][:, :, :], AF.Square, [HB[i]], [SQ])
                    pt, PT = next_ps()
                    for dc in range(8):
                        mm(pt[:, 0:n], ones_f[:, :], sq[:, dc, :], dc == 0, dc == 7, [ONESF, SQ], [PT])
                    act(rs[:, :], pt[:, 0:n], AF.Sqrt, [PT], [RS], bias=EPS, scale=1.0 / D)
                    kb.op("dve", lambda e: e.reciprocal(out=rs2[:, :], in_=rs[:, :]), [RS], [RS2])
                    for dc in range(8):
                        stt(hb[i][:, dc, :], hb[i][:, dc, :], fng[:, dc:dc + 1], rs2[:, :], ALU.mult, ALU.mult,
                            [HB[i], FNG, RS2], [HB[i]])
                    for hf in range(2):
                        pt, PT = next_ps()
                        for q in range(4):
                            tr(pt[:, q * 128:(q + 1) * 128], hb[i][:, hf * 4 + q, :], ident[:, :], [HB[i], IDENT], [PT])
                        act(yo[i][:, hf * 512:(hf + 1) * 512], pt[:, :], AF.Copy, [PT], [YO[i]])
                    dst = y_p[r0 - 16:r0 - 16 + n, :] if r0 is not None else y_s[:, :]
                    kb.dma("sp", dst, yo[i][:, :], [YO[i]], [], YO[i])
                kb.barrier()

        kb.finish()
        print(f"[kernel] emitted ~{kb.n_ins} instructions, {kb.nsem} semaphores")
    return nc


def _constants():
    ident = np.eye(128, dtype=np.float32)
    rmat = np.zeros((128, 128), np.float32)
    for base in (0, 64):
        for d in range(8):
            rmat[base + d + 8, base + d] = -1.0
            rmat[base + d, base + d + 8] = 1.0
    pos = np.concatenate([np.arange(TP), 2048 + (np.arange(NSM) % 8)]).astype(np.float32)
    inv = (np.float32(ROPE_THETA) ** (-np.arange(8, dtype=np.float32) * np.float32(2.0) / np.float32(16))).astype(np.float32)
    ang = pos[None, :] * inv[:, None]
    cos = np.ones((128, NT), np.float32)
    sin = np.zeros((128, NT), np.float32)
    for base in (0, 64):
        cos[base:base + 8] = np.cos(ang); cos[base + 8:base + 16] = np.cos(ang)
        sin[base:base + 8] = np.sin(ang); sin[base + 8:base + 16] = np.sin(ang)
    a = np.arange(128)
    tri_st = (a[:, None] <= a[None, :]).astype(np.float32)
    same = (a[:, None] // 8 == a[None, :] // 8)
    blk_st = (same & (a[:, None] <= a[None, :])).astype(np.float32)
    tribias = np.where(a[None, :] <= a[:, None], 0.0, NEG).astype(np.float32)
    blkbias = np.where(same & (a[None, :] <= a[:, None]), 0.0, NEG).astype(np.float32)
    masks = np.concatenate([tri_st, blk_st, tribias, blkbias], axis=1)
    seqsel = np.zeros((128, 16, 128), np.float32)
    for s in range(16):
        seqsel[:, s, 8 * s:8 * s + 8] = 1.0
    rowsel = (a[:, None] // 8 == np.arange(16)[None, :]).astype(np.float32)
    return ident, rmat, cos, sin, masks, seqsel.reshape(128, -1), rowsel


_NC_CACHE = {}
DEBUG = None


def kernel(x_prompt, x_sample, cache_k, cache_v, cache_kidx, state_hgrn, state_conv, page_table, meta_tokens, norm_g,
           w_in, b_in, lb_logits, hgrn_norm_g, conv_w, conv_b, conv_ln_g, conv_ln_b, conv_pw, w_pa, w_pb, w_pc, w_out,
           final_norm_g):
    f = lambda a: np.ascontiguousarray(np.asarray(a))
    (x_prompt, x_sample, cache_k, cache_v, cache_kidx, state_hgrn, state_conv, page_table, meta_tokens, norm_g, w_in, b_in,
     lb_logits, hgrn_norm_g, conv_w, conv_b, conv_ln_g, conv_ln_b, conv_pw, w_pa, w_pb, w_pc, w_out, final_norm_g) = map(f, (
        x_prompt, x_sample, cache_k, cache_v, cache_kidx, state_hgrn, state_conv, page_table, meta_tokens, norm_g, w_in, b_in,
        lb_logits, hgrn_norm_g, conv_w, conv_b, conv_ln_g, conv_ln_b, conv_pw, w_pa, w_pb, w_pc, w_out, final_norm_g))
    key = ("nc", DEBUG)
    if key not in _NC_CACHE:
        _NC_CACHE[key] = build_program(n_layers=1 if DEBUG else DEPTH, debug=DEBUG)
    nc = _NC_CACHE[key]
    ident, rmat, cos, sin, masks, seqsel, rowsel = _constants()
    normg_fm = np.ascontiguousarray(norm_g.reshape(DEPTH, 8, 128).transpose(2, 0, 1).reshape(128, DEPTH * 8))
    fng_fm = np.ascontiguousarray(final_norm_g.reshape(8, 128).T)
    bias_fm = np.zeros((128, DEPTH, len(FM_NAMES)), np.float32)
    for l in range(DEPTH):
        for j, n in enumerate(FM_NAMES):
            for (col, width, base) in FM_UNITS[n]:
                bias_fm[base:base + width, l, j] = b_in[l, col:col + width]
    bias_fm = bias_fm.reshape(128, -1)
    btok = np.ascontiguousarray(b_in.reshape(1, -1))
    convw_fm = np.ascontiguousarray(conv_w.reshape(DEPTH, 31, 4, 128).transpose(3, 0, 2, 1).reshape(128, DEPTH * 4 * 31))
    cvec = np.stack([conv_b, conv_ln_g, conv_ln_b], axis=1)
    cvec_fm = np.ascontiguousarray(cvec.reshape(DEPTH, 3, 4, 128).transpose(3, 0, 1, 2).reshape(128, DEPTH * 12))
    lbl_fm = np.ascontiguousarray(lb_logits.reshape(DEPTH, 4, 128).transpose(2, 1, 0).reshape(128, 16))
    ck = cache_k.reshape(DEPTH, N_PHYS * 128, 128)
    cv = cache_v.reshape(DEPTH, N_PHYS * 128, 128)
    cki = cache_kidx.reshape(DEPTH, N_PHYS * 128, 64)
    in_maps = []
    for c in range(8):
        in_maps.append({
            "xp": x_prompt[c], "xs": x_sample[16 * c:16 * c + 16].reshape(NSM, D), "meta": meta_tokens,
            "w_in": w_in, "conv_pw": conv_pw, "w_pa": w_pa, "w_pb": w_pb, "w_pc": w_pc, "w_out": w_out,
            "cache_k": ck, "cache_v": cv, "cache_ki": cki,
            "st_h": np.ascontiguousarray(state_hgrn[:, 16 * c:16 * c + 16]),
            "st_c": np.ascontiguousarray(state_conv[:, 16 * c:16 * c + 16]),
            "ptab": np.ascontiguousarray(page_table[16 * c:16 * c + 16].reshape(1, 256)).astype(np.int32),
            "gng": hgrn_norm_g,
            "c_ident": ident, "c_rmat": rmat, "c_cos": cos, "c_sin": sin, "c_normg": normg_fm, "c_fng": fng_fm,
            "c_bias": bias_fm, "c_btok": btok, "c_masks": masks, "c_seqsel": seqsel, "c_rowsel": rowsel,
            "c_convw": convw_fm, "c_cvec": cvec_fm, "c_lbl": lbl_fm,
        })
    if DEBUG:
        in_maps = in_maps[:1]
    res = run_bass_kernel_spmd(nc, in_maps, core_ids=list(range(len(in_maps))))
    R = res.results
    if DEBUG:
        return [np.asarray(r["dbg"]) for r in R]
    g = lambda k: np.stack([np.asarray(r[k]) for r in R])
    y_prompt = g("y_p")
    y_sample = g("y_s").reshape(128, 8, D)
    pk = g("p_k").transpose(1, 0, 2, 3).reshape(DEPTH, 8, TP, 2, 64)
    pv = g("p_v").transpose(1, 0, 2, 3).reshape(DEPTH, 8, TP, 2, 64)
    pki = g("p_ki").transpose(1, 0, 2, 3).reshape(DEPTH, 8, TP, 64)
    ph_ = g("p_h").transpose(1, 0, 2, 3, 4)
    pc_ = g("p_c").transpose(1, 0, 2, 3)
    sk = g("s_k").transpose(1, 0, 2, 3).reshape(DEPTH, 128, 8, 2, 64)
    sv = g("s_v").transpose(1, 0, 2, 3).reshape(DEPTH, 128, 8, 2, 64)
    ski = g("s_ki").transpose(1, 0, 2, 3).reshape(DEPTH, 128, 8, 64)
    sh = g("s_h").transpose(1, 0, 2, 3, 4, 5).reshape(DEPTH, 128, 4, 128, 128)
    sc_ = g("s_c").transpose(1, 0, 2, 3, 4).reshape(DEPTH, 128, 30, 512)
    c = np.ascontiguousarray
    return (c(y_prompt), c(y_sample), c(pk), c(pv), c(pki), c(ph_), c(pc_), c(sk), c(sv), c(ski), c(sh), c(sc_))
```
